# Optimizing a Trainium2 kernel written in Bass

```python
import math
import jax, jax.numpy as jnp
from jax import lax
import numpy as np

D_MODEL = 1024
BATCH = 4
SEQ = 4096
DEPTH = 2

D_MIX = D_MODEL
D_SSM = D_MIX // 2
D_ATTN = D_MIX - D_SSM
SSM_P = 16
SSM_G = D_SSM // SSM_P
SSM_N = 64
HEAD_DIM = 64
N_HEADS = D_ATTN // HEAD_DIM
D_IN = D_SSM + 3 * D_ATTN + N_HEADS
D_FF = 4 * D_MODEL
Q_BLOCK = 128
EPS = 1e-6
DT_MIN = 1e-3
DT_MAX = 1e-1

kernel_name = "hybrid_s5_fox_parallel_heads"


def rms_norm(x, g):
    xf = x.astype(jnp.float32)
    out = xf * lax.rsqrt(jnp.mean(xf * xf, axis=-1, keepdims=True) + EPS)
    return (out * g.astype(jnp.float32)).astype(x.dtype)


def _ssm_combine(left, right):
    ar1, ai1, br1, bi1 = left
    ar2, ai2, br2, bi2 = right
    ar = ar2 * ar1 - ai2 * ai1
    ai = ar2 * ai1 + ai2 * ar1
    br = ar2 * br1 - ai2 * bi1 + br2
    bi = ar2 * bi1 + ai2 * br1 + bi2
    return (ar, ai, br, bi)


def s5_mixer(u, log_dt, lam_re, lam_im, b_re, b_im, c_re, c_im, d_skip, glu_w, glu_b):
    f32 = jnp.float32
    bsz, seq, _ = u.shape
    uf = u.astype(f32).reshape(bsz, seq, SSM_G, SSM_P).transpose(1, 0, 2, 3)
    lam_re = lam_re.astype(f32)
    lam_im = lam_im.astype(f32)
    dt = jnp.exp(log_dt.astype(f32))[:, None]
    mag = jnp.exp(lam_re * dt)
    a_re = mag * jnp.cos(lam_im * dt)
    a_im = mag * jnp.sin(lam_im * dt)
    den = lam_re * lam_re + lam_im * lam_im
    nr = a_re - 1.0
    s_re = (nr * lam_re + a_im * lam_im) / den
    s_im = (a_im * lam_re - nr * lam_im) / den
    b_re = b_re.astype(f32)
    b_im = b_im.astype(f32)
    bb_re = s_re[..., None] * b_re - s_im[..., None] * b_im
    bb_im = s_re[..., None] * b_im + s_im[..., None] * b_re
    bu_re = jnp.einsum('lbgp,gnp->lbgn', uf, bb_re)
    bu_im = jnp.einsum('lbgp,gnp->lbgn', uf, bb_im)
    a_re_l = jnp.broadcast_to(a_re[None, None], (seq, 1, SSM_G, SSM_N))
    a_im_l = jnp.broadcast_to(a_im[None, None], (seq, 1, SSM_G, SSM_N))
    _, _, x_re, x_im = lax.associative_scan(_ssm_combine, (a_re_l, a_im_l, bu_re, bu_im), axis=0)
    y = (jnp.einsum('lbgn,gpn->lbgp', x_re, c_re.astype(f32))
         - jnp.einsum('lbgn,gpn->lbgp', x_im, c_im.astype(f32)))
    y = y + d_skip.astype(f32).reshape(SSM_G, SSM_P) * uf
    y = y.transpose(1, 0, 2, 3).reshape(bsz, seq, D_SSM)
    g = jax.nn.gelu(y)
    out = g * jax.nn.sigmoid(g @ glu_w.astype(f32) + glu_b.astype(f32))
    return out.astype(u.dtype)


def forgetting_attention(q, k, v, f_logit, f_bias):
    bsz, seq, _ = q.shape
    def heads(t):
        return t.reshape(bsz, seq, N_HEADS, HEAD_DIM).transpose(0, 2, 1, 3)
    qh, kh, vh = heads(q), heads(k), heads(v)
    log_f = jax.nn.log_sigmoid(f_logit.astype(jnp.float32) + f_bias.astype(jnp.float32))
    csum = jnp.cumsum(log_f, axis=1).transpose(0, 2, 1)
    scale = 1.0 / math.sqrt(HEAD_DIM)
    outs = []
    for i in range(seq // Q_BLOCK):
        q0 = i * Q_BLOCK
        kend = q0 + Q_BLOCK
        qb = qh[:, :, q0:kend]
        kb = kh[:, :, :kend]
        vb = vh[:, :, :kend]
        s = jnp.einsum('bhqd,bhkd->bhqk', qb, kb).astype(jnp.float32) * scale
        s = s + csum[:, :, q0:kend, None] - csum[:, :, None, :kend]
        mask = (q0 + jnp.arange(Q_BLOCK))[:, None] >= jnp.arange(kend)[None, :]
        s = jnp.where(mask[None, None], s, -jnp.inf)
        p = jax.nn.softmax(s, axis=-1)
        outs.append(jnp.einsum('bhqk,bhkd->bhqd', p.astype(vb.dtype), vb))
    o = jnp.concatenate(outs, axis=2)
    return o.transpose(0, 2, 1, 3).reshape(bsz, seq, D_ATTN)


def sq_relu_mlp(x, w_up, w_down):
    h = jax.nn.relu(x @ w_up)
    return (h * h) @ w_down


def setup_inputs(seed: int = 0) -> dict:
    key = jax.random.key(seed)
    ks = jax.random.split(key, 22)
    f32 = jnp.float32
    nrm = lambda k, s: jax.random.normal(k, s, f32)
    x = nrm(ks[0], (BATCH, SEQ, D_MODEL))
    ln1_g = 1.0 + 0.02 * nrm(ks[1], (DEPTH, D_MODEL))
    w_in = nrm(ks[2], (DEPTH, D_MODEL, D_IN)) * D_MODEL ** -0.5
    gate_scale = jnp.concatenate([jnp.ones((D_IN - N_HEADS,), f32), jnp.full((N_HEADS,), 0.1, f32)])
    w_in = w_in * gate_scale
    ssm_log_dt = jax.random.uniform(ks[3], (DEPTH, SSM_G), f32, math.log(DT_MIN), math.log(DT_MAX))
    ssm_lambda_re = -0.5 + 0.01 * nrm(ks[4], (DEPTH, SSM_G, SSM_N))
    ssm_lambda_im = jnp.broadcast_to(jnp.pi * jnp.arange(SSM_N, dtype=f32), (DEPTH, SSM_G, SSM_N))
    b_scale = (2.0 * SSM_P) ** -0.5
    ssm_b_re = nrm(ks[5], (DEPTH, SSM_G, SSM_N, SSM_P)) * b_scale
    ssm_b_im = nrm(ks[6], (DEPTH, SSM_G, SSM_N, SSM_P)) * b_scale
    c_scale = (2.0 * SSM_N) ** -0.5
    ssm_c_re = nrm(ks[7], (DEPTH, SSM_G, SSM_P, SSM_N)) * c_scale
    ssm_c_im = nrm(ks[8], (DEPTH, SSM_G, SSM_P, SSM_N)) * c_scale
    ssm_d = nrm(ks[9], (DEPTH, D_SSM))
    glu_w = nrm(ks[10], (DEPTH, D_SSM, D_SSM)) * D_SSM ** -0.5
    glu_b = 0.01 * nrm(ks[11], (DEPTH, D_SSM))
    fgate_b = 4.0 + 0.5 * nrm(ks[12], (DEPTH, N_HEADS))
    gn_ssm_g = 1.0 + 0.02 * nrm(ks[13], (DEPTH, D_SSM))
    gn_attn_g = 1.0 + 0.02 * nrm(ks[14], (DEPTH, D_ATTN))
    w_out = nrm(ks[15], (DEPTH, D_MIX, D_MODEL)) * D_MIX ** -0.5
    ln2_g = 1.0 + 0.02 * nrm(ks[16], (DEPTH, D_MODEL))
    w_up = nrm(ks[17], (DEPTH, D_MODEL, D_FF)) * D_MODEL ** -0.5
    w_down = nrm(ks[18], (DEPTH, D_FF, D_MODEL)) * D_FF ** -0.5
    final_g = 1.0 + 0.02 * nrm(ks[19], (D_MODEL,))
    return {"x": x, "ln1_g": ln1_g, "w_in": w_in, "ssm_log_dt": ssm_log_dt,
            "ssm_lambda_re": ssm_lambda_re, "ssm_lambda_im": ssm_lambda_im,
            "ssm_b_re": ssm_b_re, "ssm_b_im": ssm_b_im, "ssm_c_re": ssm_c_re, "ssm_c_im": ssm_c_im,
            "ssm_d": ssm_d, "glu_w": glu_w, "glu_b": glu_b, "fgate_b": fgate_b,
            "gn_ssm_g": gn_ssm_g, "gn_attn_g": gn_attn_g, "w_out": w_out, "ln2_g": ln2_g,
            "w_up": w_up, "w_down": w_down, "final_g": final_g}


def reference(x, ln1_g, w_in, ssm_log_dt, ssm_lambda_re, ssm_lambda_im, ssm_b_re, ssm_b_im,
              ssm_c_re, ssm_c_im, ssm_d, glu_w, glu_b, fgate_b, gn_ssm_g, gn_attn_g, w_out,
              ln2_g, w_up, w_down, final_g):
    h = x
    splits = [D_SSM, D_SSM + D_ATTN, D_SSM + 2 * D_ATTN, D_SSM + 3 * D_ATTN]
    for l in range(DEPTH):
        xn = rms_norm(h, ln1_g[l])
        proj = xn @ w_in[l]
        u, q, k, v, f_logit = jnp.split(proj, splits, axis=-1)
        y_ssm = s5_mixer(u, ssm_log_dt[l], ssm_lambda_re[l], ssm_lambda_im[l], ssm_b_re[l],
                         ssm_b_im[l], ssm_c_re[l], ssm_c_im[l], ssm_d[l], glu_w[l], glu_b[l])
        y_att = forgetting_attention(q, k, v, f_logit, fgate_b[l])
        mixed = jnp.concatenate([rms_norm(y_ssm, gn_ssm_g[l]), rms_norm(y_att, gn_attn_g[l])], axis=-1)
        h = h + mixed @ w_out[l]
        h = h + sq_relu_mlp(rms_norm(h, ln2_g[l]), w_up[l], w_down[l])
    return rms_norm(h, final_g)
```

```python
import contextlib
import os
import numpy as np
import concourse.bass as bass
import concourse.mybir as mybir
from concourse.bass_utils import run_bass_kernel_spmd

F32 = mybir.dt.float32
BF16 = mybir.dt.bfloat16
AF = mybir.ActivationFunctionType
ALU = mybir.AluOpType

ENGS = ("pe", "act", "dve", "pool", "sp")
L = 4096
D = 1024
NG = 8
DEPTH = 2
EPS = 1e-6
NHL, NBL, NPL = 4, 2, 8
LH = L // 2
NCAT = 256 + 512 + 512 + 256 + 104
OU, OQ, OK_, OV, OF = 0, 256, 768, 1280, 1536
U32 = mybir.dt.uint32
VL1, VL2, VGS, VGB, VFB, VDS, VGA = 0, 8, 16, 20, 24, 25, 27
NV = 31
C_ID, C_PM, C_TRI, C_SE, C_SO, C_SELK, C_SELQ, C_MISC = 0, 128, 256, 384, 512, 640, 640 + 1024, 640 + 2048
NCONST = C_MISC + 4


class Sched:
    def __init__(self, nc, st, n_dma_sems=8):
        self.nc = nc
        self.nd = n_dma_sems
        self.csem = {e: st.enter_context(nc.semaphore("c_" + e)) for e in ENGS}
        self.dsem = {(q, k): st.enter_context(nc.semaphore("d_%s%d" % (q, k)))
                     for q in ("sp", "act", "pool") for k in range(n_dma_sems)}
        self.base = {e: 0 for e in ENGS}
        self.ops = {e: [] for e in ENGS}
        self.last_w = {}
        self.readers = {}
        self.seen = {e: {} for e in ENGS}
        self.dma_uses = {}
        self.dma_rr = {e: 0 for e in ENGS}
        self.ninst = 0
        self.ccsem = st.enter_context(nc.semaphore("ccsem"))
        self.ncc = 0

    def collective(self, in_ap, out_ap):
        self.flush()
        self.ncc += 1
        n = self.ncc
        self.ops["pool"].append(dict(kind="cc", waits=[], inc=False, fn=lambda e: e.collective_compute(
            "AllGather", ALU.bypass, replica_groups=[[0, 1], [2, 3], [4, 5], [6, 7]], ins=[in_ap], outs=[out_ap])))
        for e in ENGS:
            self.ops[e].append(dict(kind="wcc", waits=[], inc=False, n=n))
        self.flush()

    def _need(self, eng, ref, waits):
        if ref is None:
            return
        if ref[0] == "c":
            _, src, idx = ref
            if src == "pe" and eng == "pe":
                return
            cur = self.seen[eng].get(("c", src), -1)
            if idx > cur:
                self.seen[eng][("c", src)] = idx
                self.ops[src][idx]["inc"] = True
                waits.append(ref)
        else:
            _, q, k, val = ref
            cur = self.seen[eng].get(("d", q, k), 0)
            if val > cur:
                self.seen[eng][("d", q, k)] = val
                waits.append(ref)

    def _deps(self, eng, reads, writes):
        waits = []
        for k in reads:
            self._need(eng, self.last_w.get(k), waits)
        for k in writes:
            self._need(eng, self.last_w.get(k), waits)
            for r in self.readers.get(k, ()):
                self._need(eng, r, waits)
        return waits

    def _commit(self, ref, reads, writes):
        for k in reads:
            self.readers.setdefault(k, []).append(ref)
        for k in writes:
            self.last_w[k] = ref
            self.readers[k] = []

    def op(self, eng, fn, reads=(), writes=()):
        waits = self._deps(eng, reads, writes)
        idx = len(self.ops[eng])
        self.ops[eng].append(dict(kind="c", fn=fn, waits=waits, inc=False))
        self._commit(("c", eng, idx), reads, writes)

    def I(self, eng, meth, reads, writes, *a, **kw):
        self.op(eng, lambda e: getattr(e, meth)(*a, **kw), reads, writes)

    def mm(self, reads, writes, items):
        items = list(items)

        def fn(e):
            ins = None
            for (o, l, r, st_, sp_) in items:
                ins = e.matmul(o, l, r, start=st_, stop=sp_)
            return ins
        self.op("pe", fn, reads, writes)

    def dma(self, q, out, in_, reads=(), writes=()):
        k = self.dma_rr[q] % self.nd
        self.dma_rr[q] += 1
        uses = self.dma_uses.get((q, k), 0)
        waits = []
        if uses > 0:
            self._need(q, ("d", q, k, 16 * uses), waits)
        waits += self._deps(q, reads, writes)
        self.dma_uses[(q, k)] = uses + 1
        ref = ("d", q, k, 16 * (uses + 1))
        self.ops[q].append(dict(kind="d", fn=lambda e: e.dma_start(out=out, in_=in_), waits=waits,
                                inc=False, dsem=(q, k)))
        self._commit(ref, reads, writes)

    def gather(self, out, in_, idx, reads=(), writes=()):
        q = "pool"
        k = self.dma_rr[q] % self.nd
        self.dma_rr[q] += 1
        uses = self.dma_uses.get((q, k), 0)
        waits = []
        if uses > 0:
            self._need(q, ("d", q, k, 16 * uses), waits)
        waits += self._deps(q, reads, writes)
        self.dma_uses[(q, k)] = uses + 1
        ref = ("d", q, k, 16 * (uses + 1))
        self.ops[q].append(dict(kind="d", waits=waits, inc=False, dsem=(q, k), fn=lambda e: e.indirect_dma_start(
            out=out, out_offset=None, in_=in_, in_offset=bass.IndirectOffsetOnAxis(ap=idx, axis=0))))
        self._commit(ref, reads, writes)

    def barrier(self):
        lastc = {}
        for e in ENGS:
            for i in range(len(self.ops[e]) - 1, -1, -1):
                if self.ops[e][i]["kind"] == "c":
                    lastc[e] = i
                    break
        for e in ENGS:
            waits = []
            for src, i in lastc.items():
                self._need(e, ("c", src, i), waits)
            for (q, k), uses in self.dma_uses.items():
                self._need(e, ("d", q, k, 16 * uses), waits)
            if waits:
                self.ops[e].append(dict(kind="w", fn=None, waits=waits, inc=False))
        self.last_w.clear()
        self.readers.clear()

    def flush(self):
        self.barrier()
        nc = self.nc
        val = {}
        for e in ENGS:
            c = self.base[e]
            vals = []
            for rec in self.ops[e]:
                if rec["kind"] == "c" and rec["inc"]:
                    c += 1
                    vals.append(c)
                else:
                    vals.append(None)
            val[e] = vals
            self.base[e] = c
        ops = self.ops

        def run(e, eng):
            for rec in ops[e]:
                for w in rec["waits"]:
                    if w[0] == "c":
                        eng.wait_ge(self.csem[w[1]], val[w[1]][w[2]])
                    else:
                        eng.wait_ge(self.dsem[(w[1], w[2])], w[3])
                if rec["kind"] == "c":
                    ins = rec["fn"](eng)
                    if rec["inc"]:
                        ins.then_inc(self.csem[e], 1)
                elif rec["kind"] == "d":
                    rec["fn"](eng).then_inc(self.dsem[rec["dsem"]], 16)
                elif rec["kind"] == "cc":
                    rec["fn"](eng).then_inc(self.ccsem, 1)
                elif rec["kind"] == "wcc":
                    eng.wait_ge(self.ccsem, rec["n"])
                self.ninst += 1

        with nc.Block() as block:
            @block.tensor
            def _(eng):
                run("pe", eng)

            @block.scalar
            def _(eng):
                run("act", eng)

            @block.vector
            def _(eng):
                run("dve", eng)

            @block.gpsimd
            def _(eng):
                run("pool", eng)

            @block.sync
            def _(eng):
                run("sp", eng)
        self.ops = {e: [] for e in ENGS}
        for e in ENGS:
            for k in list(self.seen[e].keys()):
                if k[0] == "c":
                    del self.seen[e][k]


class Ring:
    def __init__(self, tiles, name):
        self.tiles = tiles
        self.name = name
        self.i = 0

    def next(self):
        k = self.i % len(self.tiles)
        self.i += 1
        return self.tiles[k], (self.name, k)


def build(debug=False, nlayers=DEPTH):
    nc = bass.Bass("TRN2", target_bir_lowering=False)
    din = lambda n, shp, dt=F32: nc.dram_tensor(n, shp, dt, kind="ExternalInput")
    xT = din("xT", [D, L])
    xTo = din("xTo", [D, LH])
    gidx_d = din("gidx", [128, 32], U32)
    wcat = din("wcat", [DEPTH, D, NCAT])
    w_out = din("w_out", [DEPTH, D, D])
    glu_w = din("glu_w", [DEPTH, 512, 512])
    w_up = din("w_up", [DEPTH, D, 4096])
    w_down = din("w_down", [DEPTH, 4096, D])
    vecs = din("vecs", [DEPTH, 128, NV])
    fing = din("fing", [128, 8])
    ssm_small = din("ssm_small", [DEPTH, 128, 24])
    ssm_big = din("ssm_big", [DEPTH, 128, 512])
    consts = din("consts", [128, NCONST])
    outT = nc.dram_tensor("outT", [D, LH], F32, kind="ExternalOutput")
    h1own = nc.dram_tensor("h1own", [D, LH], F32, kind="Internal")
    h1ownb = nc.dram_tensor("h1ownb", [D, LH], BF16, kind="Internal")
    H1 = nc.dram_tensor("H1", [2 * D, LH], BF16, kind="Internal")
    UT = nc.dram_tensor("UT", [NBL, 128, 8, 512], BF16, kind="Internal")
    SND = nc.dram_tensor("SND", [2, 4, 512, 512], BF16, kind="Internal")
    GAT = nc.dram_tensor("GAT", [2 * 2 * 4 * 512, 512], BF16, kind="Internal")
    wcat_b = nc.dram_tensor("wcat_b", [DEPTH, D, NCAT], BF16, kind="Internal")
    w_out_b = nc.dram_tensor("w_out_b", [DEPTH, D, D], BF16, kind="Internal")
    glu_b = nc.dram_tensor("glu_wb", [DEPTH, 512, 512], BF16, kind="Internal")
    w_up_b = nc.dram_tensor("w_up_b", [DEPTH, D, 4096], BF16, kind="Internal")
    w_dn_b = nc.dram_tensor("w_dn_b", [DEPTH, 4096, D], BF16, kind="Internal")

    top = contextlib.ExitStack()
    with top:
        s = Sched(nc, top)

        uid = [0]

        def mk(st):
            uid[0] += 1
            pre = "u%d_" % uid[0]

            def sb(name, shape, dt=F32):
                return st.enter_context(nc.sbuf_tensor(pre + name, shape, dt))

            def ps(name, shape, dt=F32):
                return st.enter_context(nc.psum_tensor(pre + name, shape, dt))
            return sb, ps

        gsb, _ = mk(top)
        cst = gsb("cst", [128, NCONST])
        s.dma("sp", cst[:], consts.ap(), writes=["cst"])
        identb = gsb("identb", [128, 128], BF16)
        trib = gsb("trib", [128, 128], BF16)
        selk = gsb("selk", [128, 8, 128], BF16)
        selq = gsb("selq", [128, 8, 128], BF16)
        s.I("dve", "tensor_copy", ["cst"], ["identb"], out=identb[:], in_=cst[:, C_ID:C_ID + 128])
        s.I("dve", "tensor_copy", ["cst"], ["trib"], out=trib[:], in_=cst[:, C_TRI:C_TRI + 128])
        s.I("dve", "tensor_copy", ["cst"], ["selk"], out=selk[:].rearrange("p h c -> p (h c)"), in_=cst[:, C_SELK:C_SELK + 1024])
        s.I("dve", "tensor_copy", ["cst"], ["selq"], out=selq[:].rearrange("p h c -> p (h c)"), in_=cst[:, C_SELQ:C_SELQ + 1024])
        identf = cst[:, C_ID:C_ID + 128]
        selEO = [cst[:, C_SE:C_SE + 128], cst[:, C_SO:C_SO + 128]]
        pmf = cst[:, C_PM:C_PM + 128]
        mkc = [cst[:, C_MISC + 0:C_MISC + 1], cst[:, C_MISC + 1:C_MISC + 2]]
        epsc = cst[:, C_MISC + 2:C_MISC + 3]
        onec = cst[:, C_MISC + 3:C_MISC + 4]
        onesf = gsb("onesf", [128, 128])
        s.I("dve", "memset", [], ["onesf"], onesf[:], 1.0)
        onesb = gsb("onesb", [128, 128], BF16)
        s.I("dve", "memset", [], ["onesb"], onesb[:], 1.0)
        vec = [gsb("vec%d" % l, [128, NV]) for l in range(DEPTH)]
        for l in range(DEPTH):
            s.dma("sp", vec[l][:], vecs.ap()[l], writes=["vecs"])
        fg = gsb("fg", [128, 8])
        s.dma("sp", fg[:], fing.ap(), writes=["vecs"])
        gix = gsb("gix", [128, 32], U32)
        s.dma("sp", gix[:], gidx_d.ap(), writes=["gix"])

        with contextlib.ExitStack() as st:
            sb, ps = mk(st)
            stg = Ring([sb("stg%d" % i, [128, 4096]) for i in range(3)], "stg")
            stb = Ring([sb("stb%d" % i, [128, 4096], BF16) for i in range(3)], "stb")
            cnt = [0]

            def cvt(src, dst, shape):
                t32, k32 = stg.next()
                tb, kb = stb.next()
                n = int(np.prod(shape[1:]))
                if len(shape) == 3:
                    v32 = t32[:, 0:n].rearrange("p (a b) -> p a b", a=shape[1])
                    vb = tb[:, 0:n].rearrange("p (a b) -> p a b", a=shape[1])
                else:
                    v32 = t32[:, 0:n]
                    vb = tb[:, 0:n]
                s.dma("sp", v32, src, writes=[k32])
                eng = ("dve", "pool", "act")[cnt[0] % 3]
                cnt[0] += 1
                if eng == "act":
                    s.I("act", "activation", [k32], [kb], out=tb[:, 0:n], in_=t32[:, 0:n], func=AF.Copy)
                else:
                    s.I(eng, "tensor_copy", [k32], [kb], out=tb[:, 0:n], in_=t32[:, 0:n])
                s.dma("pool", dst, vb, reads=[kb])

            def v3(t, l, r0, nr):
                return t.ap()[l, r0 * 128:(r0 + nr) * 128, :].rearrange("(a p) n -> p a n", p=128)

            for rc in range(8):
                cvt(wcat.ap()[0, rc * 128:(rc + 1) * 128, :], wcat_b.ap()[0, rc * 128:(rc + 1) * 128, :], [128, NCAT])
            s.flush()

        def bg_pieces():
            for l in range(nlayers):
                if l > 0:
                    for rc in range(8):
                        yield (wcat.ap()[l, rc * 128:(rc + 1) * 128, :], wcat_b.ap()[l, rc * 128:(rc + 1) * 128, :], [128, NCAT])
                for rc in range(0, 8, 4):
                    yield (v3(w_out, l, rc, 4), v3(w_out_b, l, rc, 4), [128, 4, 1024])
                yield (v3(glu_w, l, 0, 4), v3(glu_b, l, 0, 4), [128, 4, 512])
                for rc in range(8):
                    yield (w_up.ap()[l, rc * 128:(rc + 1) * 128, :], w_up_b.ap()[l, rc * 128:(rc + 1) * 128, :], [128, 4096])
                for rc in range(0, 32, 4):
                    yield (v3(w_down, l, rc, 4), v3(w_dn_b, l, rc, 4), [128, 4, 1024])

        def v3(t, l, r0, nr):
            return t.ap()[l, r0 * 128:(r0 + nr) * 128, :].rearrange("(a p) n -> p a n", p=128)

        evac_mode = ["alt"]

        def rmsnorm(rings, srcs, P, gcols, Dn, outs):
            pmr, sqr, tmpr, rstdr = rings
            pt, pk = pmr.next()
            n = len(srcs)
            for c, (a, k) in enumerate(srcs):
                sq, sk = sqr.next()
                if evac_mode[0] == "dve":
                    s.I("pool", "tensor_tensor", [k], [sk], out=sq[0:P, :], in0=a, in1=a, op=ALU.mult)
                else:
                    s.I("act", "activation", [k], [sk], out=sq[0:P, :], in_=a, func=AF.Square)
                s.mm([sk, "onesb"], [pk], [(pt[0:P, :], onesb[0:P, 0:P], sq[0:P, :], c == 0, c == n - 1)])
            tm, tk = tmpr.next()
            s.I("act", "activation", [pk, "cst"], [tk], out=tm[0:P, :], in_=pt[0:P, :], func=AF.Ln,
                scale=1.0 / Dn, bias=epsc[0:P, :])
            rs, rk = rstdr.next()
            s.I("act", "activation", [tk], [rk], out=rs[0:P, :], in_=tm[0:P, :], func=AF.Exp, scale=-0.5)
            for c, ((a, k), (o, ok)) in enumerate(zip(srcs, outs)):
                s.I("dve", "scalar_tensor_tensor", [k, rk, "vecs"], [ok], out=o, in0=a, scalar=gcols[c],
                    in1=rs[0:P, :], op0=ALU.mult, op1=ALU.mult)

        evac_cnt = [0]

        def evac(out, in_, reads, writes):
            e = ("act", "dve")[evac_cnt[0] % 2]
            if evac_mode[0] == "dve":
                e = "dve"
            evac_cnt[0] += 1
            if e == "act":
                s.I("act", "activation", reads, writes, out=out, in_=in_, func=AF.Copy)
            else:
                s.I("dve", "tensor_copy", reads, writes, out=out, in_=in_)

        for l in range(nlayers):
            def hsrc_ap(c, t):
                if l == 0:
                    return xT.ap()[c * 128:(c + 1) * 128, t * 512:(t + 1) * 512]
                r0 = (c // 4) * 1024 + (t // 4) * 512 + (c % 4) * 128
                return H1.ap()[r0:r0 + 128, (t % 4) * 512:(t % 4 + 1) * 512]
            hown = xTo if l == 0 else h1own
            V = vec[l]
            lst = contextlib.ExitStack()
            SS = {}
            pmr_bg = [None]

            def make_setup():
                lsb, _lps = mk(lst)
                ssm_s = lsb("ssm_s", [128, 24])
                ssm_b = lsb("ssm_b", [128, 4, NPL, 16])
                sc = lsb("sc", [128, 64, NPL])
                big = lsb("big", [128, 12, NPL, 16])
                Ebd = lsb("Ebd", [128, 2, 8, NPL, 2, 16], BF16)
                Fbd = lsb("Fbd", [128, 2, 8, NPL, 2, 16], BF16)
                Cbd = lsb("Cbd", [128, 2, NPL, 2, 16], BF16)
                Klag = lsb("Klag", [128, NBL, 8, 128], BF16)
                A1t = lsb("A1t", [128, 2, NPL])
                A2t = lsb("A2t", [128, 2, NPL])
                ktmp = lsb("ktmp", [128, 128])
                SS.update(ssm_s=ssm_s, ssm_b=ssm_b, sc=sc, big=big, Ebd=Ebd, Fbd=Fbd, Cbd=Cbd, Klag=Klag, A1t=A1t, A2t=A2t, ktmp=ktmp)

                def ssm_setup():
                        s.dma("sp", ssm_s[:], ssm_small.ap()[l], writes=["setup"])
                        s.dma("sp", ssm_b[:].rearrange("p a b c -> p (a b c)"), ssm_big.ap()[l], writes=["setup"])
                        SK = ["setup"]

                        def tt(o, a, b, op):
                            s.I("dve", "tensor_tensor", SK, SK, out=o, in0=a, in1=b, op=op)

                        def ts(o, a, s1, op0, s2=None, op1=None):
                            if op1 is None:
                                s.I("dve", "tensor_scalar", SK + ["cst"], SK, out=o, in0=a, scalar1=s1, scalar2=None, op0=op0)
                            else:
                                s.I("dve", "tensor_scalar", SK + ["cst"], SK, out=o, in0=a, scalar1=s1, scalar2=s2, op0=op0, op1=op1)

                        def af(o, a, func, scale=1.0):
                            s.I("act", "activation", SK, SK, out=o, in_=a, func=func, scale=scale)

                        ldt, lre, lim = ssm_s[:, 0:8], ssm_s[:, 8:16], ssm_s[:, 16:24]
                        S_ = lambda i: sc[:, i, :]
                        dt_, x1, mag, th, s16, sn, cs, t1_, t2_, t3_ = [S_(i) for i in range(10)]
                        are, aim, den, rden, nr, sre, sim = [S_(i) for i in range(10, 17)]
                        af(dt_, ldt, AF.Exp)
                        tt(x1, lre, dt_, ALU.mult)
                        af(mag, x1, AF.Exp)
                        tt(th, lim, dt_, ALU.mult)
                        af(s16, th, AF.Sin, 1.0 / 16)
                        af(sn, th, AF.Sin, 1.0 / 8)
                        yield
                        tt(t1_, s16, s16, ALU.mult)
                        ts(cs, t1_, -2.0, ALU.mult, 1.0, ALU.add)
                        for _ in range(3):
                            tt(t1_, cs, cs, ALU.mult)
                            tt(t2_, sn, sn, ALU.mult)
                            tt(t3_, cs, sn, ALU.mult)
                            tt(cs, t1_, t2_, ALU.subtract)
                            ts(sn, t3_, 2.0, ALU.mult)
                        tt(are, mag, cs, ALU.mult)
                        tt(aim, mag, sn, ALU.mult)
                        tt(t1_, lre, lre, ALU.mult)
                        tt(t2_, lim, lim, ALU.mult)
                        yield
                        tt(den, t1_, t2_, ALU.add)
                        s.I("dve", "reciprocal", SK, SK, out=rden, in_=den)
                        ts(nr, are, -1.0, ALU.add)
                        tt(t1_, nr, lre, ALU.mult)
                        tt(t2_, aim, lim, ALU.mult)
                        tt(t1_, t1_, t2_, ALU.add)
                        yield
                        tt(sre, t1_, rden, ALU.mult)
                        tt(t1_, aim, lre, ALU.mult)
                        tt(t2_, nr, lim, ALU.mult)
                        tt(t1_, t1_, t2_, ALU.subtract)
                        tt(sim, t1_, rden, ALU.mult)
                        bc = lambda a: a.unsqueeze(2).to_broadcast([128, NPL, 16])
                        Bre, Bim, CreT, CimT = [ssm_b[:, i, :, :] for i in range(4)]
                        Bbre, Bbim, m1, m2, Ere, Eim = [big[:, i, :, :] for i in range(6)]

                        def cmul(ore, oim, are_, aim_, pre, pim):
                            tt(m1, are_, bc(pre), ALU.mult)
                            tt(m2, aim_, bc(pim), ALU.mult)
                            tt(ore, m1, m2, ALU.subtract)
                            tt(m1, aim_, bc(pre), ALU.mult)
                            tt(m2, are_, bc(pim), ALU.mult)
                            tt(oim, m1, m2, ALU.add)

                        cmul(Bbre, Bbim, Bre, Bim, sre, sim)
                        yield
                        Pre = [S_(20 + 2 * j) for j in range(9)]
                        Pim = [S_(21 + 2 * j) for j in range(9)]
                        s.I("dve", "memset", SK, SK, Pre[0], 1.0)
                        s.I("dve", "memset", SK, SK, Pim[0], 0.0)
                        for j in range(1, 9):
                            tt(t1_, Pre[j - 1], are, ALU.mult)
                            tt(t2_, Pim[j - 1], aim, ALU.mult)
                            tt(Pre[j], t1_, t2_, ALU.subtract)
                            tt(t1_, Pre[j - 1], aim, ALU.mult)
                            tt(t2_, Pim[j - 1], are, ALU.mult)
                            tt(Pim[j], t1_, t2_, ALU.add)
                        s.I("dve", "tensor_copy", SK, SK, out=A1t[:, 0, :], in_=Pre[8])
                        s.I("dve", "tensor_copy", SK, SK, out=A1t[:, 1, :], in_=Pre[8])
                        ts(A2t[:, 0, :], Pim[8], -1.0, ALU.mult)
                        s.I("dve", "tensor_copy", SK, SK, out=A2t[:, 1, :], in_=Pim[8])
                        yield
                        for j in range(8):
                            cmul(Ere, Eim, Bbre, Bbim, Pre[7 - j], Pim[7 - j])
                            yield
                            for g2 in range(2):
                                ts(Ebd[:, 0, j, :, g2, :], Ere, mkc[g2], ALU.mult)
                                ts(Ebd[:, 1, j, :, g2, :], Eim, mkc[g2], ALU.mult)
                        for j in range(8):
                            cmul(Ere, Eim, CreT, CimT, Pre[j + 1], Pim[j + 1])
                            yield
                            for g2 in range(2):
                                ts(Fbd[:, 0, j, :, g2, :], Ere, mkc[g2], ALU.mult)
                                ts(Fbd[:, 1, j, :, g2, :], Eim, mkc[g2], ALU.mult, -1.0, ALU.mult)
                        for g2 in range(2):
                            ts(Cbd[:, 0, :, g2, :], CreT, mkc[g2], ALU.mult)
                            ts(Cbd[:, 1, :, g2, :], CimT, mkc[g2], ALU.mult, -1.0, ALU.mult)
                        for b in range(NBL):
                            for d in range(8):
                                pt, pk = pmr_bg[0].next()
                                er = Ebd[:, 0, 7 - d, 4 * b:4 * b + 4, :, :].rearrange("p a g q -> p (a g q)")
                                ei = Ebd[:, 1, 7 - d, 4 * b:4 * b + 4, :, :].rearrange("p a g q -> p (a g q)")
                                cr = Cbd[:, 0, 4 * b:4 * b + 4, :, :].rearrange("p a g q -> p (a g q)")
                                ci = Cbd[:, 1, 4 * b:4 * b + 4, :, :].rearrange("p a g q -> p (a g q)")
                                s.mm(SK, [pk], [(pt[:, 0:128], er, cr, True, False), (pt[:, 0:128], ei, ci, False, True)])
                                if d == 0:
                                    s.I("dve", "tensor_tensor", [pk, "cst"], ["ktmp"], out=ktmp[:], in0=pt[:, 0:128], in1=pmf, op=ALU.mult)
                                    s.I("dve", "scalar_tensor_tensor", ["ktmp", "cst", "vecs"], ["Klag"], out=Klag[:, b, d, :], in0=identf,
                                        scalar=V[:, VDS + b:VDS + b + 1], in1=ktmp[:], op0=ALU.mult, op1=ALU.add)
                                else:
                                    s.I("dve", "tensor_tensor", [pk, "cst"], ["Klag"], out=Klag[:, b, d, :], in0=pt[:, 0:128], in1=pmf, op=ALU.mult)
                            yield

                        SS.update(Pre=Pre, Pim=Pim, S_=S_)
                        yield
                return ssm_setup()

            sgen = make_setup() if l > 0 else None
            evac_mode[0] = "dve"
            with contextlib.ExitStack() as st:
                sb, ps = mk(st)
                KT = [sb("KT%d" % h, [128, L], BF16) for h in range(NHL)]
                Vr = sb("Vr", [128, 32, 384], BF16)
                s.I("pool", "memset", [], ["Vr"], Vr[:].rearrange("p k (a c) -> p k a c", c=192)[:, :, :, 64:128], 1.0)
                dsb = [sb("dsb%d" % i, [128, 512]) for i in range(2)]
                for i in range(2):
                    s.I("pool", "memset", [], [("dsb", i)], dsb[i][:], 0.0)
                wr = Ring([sb("wr%d" % i, [128, 8, 128], BF16) for i in range(3)], "wr")
                wv = sb("wv", [128, 8, 256], BF16)
                s.dma("sp", wv[:], wcat_b.ap()[l, :, OV:OV + 256].rearrange("(k p) n -> p k n", p=128), writes=["wv"])
                hr = Ring([sb("hr%d" % i, [128, 512], F32 if l == 0 else BF16) for i in range(8)], "hr")
                sqr = Ring([sb("sq%d" % i, [128, 512], BF16) for i in range(3)], "sq")
                tmpr = Ring([sb("tm%d" % i, [128, 512]) for i in range(1)], "tm")
                rstdr = Ring([sb("rs%d" % i, [128, 512]) for i in range(1)], "rs")
                xn = sb("xn", [128, 8, 512], BF16)
                QT2 = [[sb("QT%d_%d" % (z, h), [128, 512], BF16) for h in range(NHL)] for z in range(2)]
                if l == 0:
                    bstg = Ring([sb("bstg%d" % i, [128, 4096]) for i in range(2)], "bstg")
                    bstb = Ring([sb("bstb%d" % i, [128, 4096], BF16) for i in range(2)], "bstb")
                ust = Ring([sb("ust%d" % i, [128, 8, 64], BF16) for i in range(2)], "ust")
                caug = sb("caug", [128, 512], BF16)
                s.I("dve", "memset", [], ["caug"], caug[:], 0.0)
                s.I("dve", "memset", ["caug"], ["caug"], caug[96:104, :], 1.0)
                onesr = sb("onesr", [128, 512], BF16)
                s.I("dve", "memset", [], ["onesr"], onesr[:], 1.0)
                fz = sb("fz", [128, 512])
                fX = [sb("fX%d" % i, [128, 512]) for i in range(2)]
                fr1 = fz
                fb1 = sb("fb1", [128, 512], BF16)
                fb2 = sb("fb2", [128, 512], BF16)
                ptr = Ring([sb("pt%d" % i, [128, 512], BF16) for i in range(3)], "pt")
                ydr = Ring([sb("yd%d" % i, [128, 512]) for i in range(1)], "yd")
                yor = Ring([sb("yo%d" % i, [128, 512], BF16) for i in range(2)], "yo")
                pmr = Ring([ps("pm%d" % i, [128, 512]) for i in range(3)], "pm")
                psr = Ring([ps("psS%d" % i, [128, 512]) for i in range(3)], "psS")
                pyr = Ring([ps("py%d" % i, [128, 512]) for i in range(2)], "py")
                pdr = pmr
                FP = 104

                def wload(c0, ncol):
                    wt, wk = wr.next()
                    s.dma("sp", wt[:, :, 0:ncol], wcat_b.ap()[l, :, c0:c0 + ncol].rearrange("(k p) n -> p k n", p=128), writes=[wk])
                    return wt, wk

                def prep(t):
                    t0 = t * 512
                    QT = QT2[t % 2]
                    hs = []
                    for c in range(8):
                        ht, hk = hr.next()
                        s.dma("sp", ht[:], hsrc_ap(c, t), writes=[hk])
                        hs.append((ht[:], hk))
                    rmsnorm((pmr, sqr, tmpr, rstdr), hs, 128, [V[:, VL1 + c:VL1 + c + 1] for c in range(8)], float(D),
                            [(xn[:, c, :], "xn") for c in range(8)])
                    yield
                    for m in range(NBL):
                        wt, wk = wload(OU + 128 * m, 128)
                        pt, pk = pmr.next()
                        s.mm([wk, "xn"], [pk], [(pt[:], wt[:, k, :], xn[:, k, :], k == 0, k == 7) for k in range(8)])
                        ut, uk = ust.next()
                        evac(ut[:].rearrange("p j c -> p c j"), pt[:].rearrange("p (c j) -> p c j", j=8), [pk], [uk])
                        s.dma("pool", UT.ap()[m, :, :, 64 * t:64 * (t + 1)], ut[:], reads=[uk])
                    yield
                    wt, wk = wload(OF, FP)
                    pt, pk = pmr.next()
                    s.mm([wk, "xn"], [pk], [(pt[0:FP, :], wt[:, k, 0:FP], xn[:, k, :], k == 0, k == 7) for k in range(8)])
                    s.I("dve", "tensor_scalar", [pk, "vecs"], ["fz"], out=fz[0:FP, :], in0=pt[0:FP, :], scalar1=V[0:FP, VFB:VFB + 1],
                        scalar2=-1.0, op0=ALU.add, op1=ALU.mult)
                    s.I("act", "activation", ["fz"], ["fz"], out=fz[0:FP, :], in_=fz[0:FP, :], func=AF.Exp)
                    s.I("act", "activation", ["fz", "cst"], ["fz"], out=fz[0:FP, :], in_=fz[0:FP, :], func=AF.Ln, bias=onec[0:FP, :])
                    Xc, Xp = fX[t % 2], fX[(t + 1) % 2]
                    init = 0.0 if t == 0 else Xp[0:FP, 511:512]
                    s.I("dve", "tensor_tensor_scan", ["fz", "onesr", ("fX", (t + 1) % 2)], [("fX", t % 2)], out=Xc[0:FP, :],
                        data0=onesr[0:FP, :], data1=fz[0:FP, :], initial=init, op0=ALU.mult, op1=ALU.add)
                    xk = ("fX", t % 2)
                    s.I("dve", "tensor_scalar", [xk], ["fb1"], out=fb1[0:FP, :], in0=Xc[0:FP, :], scalar1=8.0, scalar2=None, op0=ALU.mult)
                    s.I("dve", "scalar_tensor_tensor", [xk, "fb1"], ["fz"], out=fr1[0:FP, :], in0=Xc[0:FP, :], scalar=8.0, in1=fb1[0:FP, :],
                        op0=ALU.mult, op1=ALU.subtract)
                    s.I("dve", "tensor_copy", ["fb1", "caug"], ["caug"], out=caug[0:8, :], in_=fb1[0:8, :])
                    s.I("dve", "tensor_copy", ["fz"], ["fb2"], out=fb2[0:FP, :], in_=fr1[0:FP, :])
                    s.I("dve", "tensor_copy", ["fb2", "caug"], ["caug"], out=caug[32:40, :], in_=fb2[32:40, :])
                    s.I("dve", "tensor_tensor", ["fz", "fb2"], ["fz"], out=fr1[0:FP, :], in0=fr1[0:FP, :], in1=fb2[0:FP, :], op=ALU.subtract)
                    s.I("dve", "tensor_copy", ["fz", "caug"], ["caug"], out=caug[64:72, :], in_=fr1[64:72, :])
                    yield
                    for h in range(NHL):
                        wt, wk = wload(OK_ + 128 * h, 128)
                        pt, pk = pmr.next()
                        s.mm([wk, "xn", "caug", "selk"], [pk],
                             [(pt[:], wt[:, k, :], xn[:, k, :], k == 0, False) for k in range(8)] +
                             [(pt[:], selk[:, h, :], caug[:], False, True)])
                        evac(KT[h][:, t0:t0 + 512], pt[:], [pk], [("KT", h, t)])
                        wt, wk = wload(OQ + 128 * h, 128)
                        pt, pk = pmr.next()
                        s.mm([wk, "xn", "caug", "selq"], [pk],
                             [(pt[:], wt[:, k, :], xn[:, k, :], k == 0, False) for k in range(8)] +
                             [(pt[:], selq[:, h, :], caug[:], False, True)])
                        evac(QT[h][:], pt[:], [pk], [("QT", t % 2, h)])
                        yield
                    for i in range(4):
                        pt, pk = pmr.next()
                        s.mm(["wv", "xn"], [pk], [(pt[:, 0:256], xn[:, k, 128 * i:128 * (i + 1)], wv[:, k, :], k == 0, k == 7) for k in range(8)])
                        vv = Vr[:, 4 * t + i, :].rearrange("p (a c) -> p a c", c=192)
                        pv = pt[:, 0:256].rearrange("p (a e d) -> p a e d", e=2, d=64)
                        evac(vv[:, :, 0:64], pv[:, :, 0, :], [pk], [("Vr", t)])
                        evac(vv[:, :, 128:192], pv[:, :, 1, :], [pk], [("Vr", t)])
                    yield

                def att(t):
                    t0 = t * 512
                    QT = QT2[t % 2]
                    nkb = 4 * (t + 1)
                    blocks = [(h, kb) for h in range(NHL) for kb in range(nkb)]
                    LA = 2
                    Sinfo = {}
                    hstate = {}

                    def emit_S(i):
                        h, kb = blocks[i]
                        j = kb - 4 * t
                        q0 = 128 * j if j > 0 else 0
                        pt, pk = psr.next()
                        s.mm([("KT", h, kb // 4), ("QT", t % 2, h)], [pk], [(pt[:, q0:512], KT[h][:, 128 * kb:128 * (kb + 1)], QT[h][:, q0:512], True, True)])
                        Sinfo[i] = (pt, pk, q0, j)

                    def emit_rest(i):
                        h, kb = blocks[i]
                        pt, pk, q0, j = Sinfo.pop(i)
                        par = h % 2
                        vb0 = 192 * (h // 2) + 64 * par
                        if kb == 0:
                            hstate[h] = pyr.next()
                        py, pyk = hstate[h]
                        pb_, pbk = ptr.next()
                        s.I("act", "activation", [pk], [pbk], out=pb_[:, q0:512], in_=pt[:, q0:512], func=AF.Exp, scale=0.125)
                        if j >= 0:
                            s.I("dve", "tensor_tensor", [pbk, "trib"], [pbk], out=pb_[:, q0:q0 + 128], in0=pb_[:, q0:q0 + 128],
                                in1=trib[:], op=ALU.mult)
                        s.mm([pbk, ("Vr", kb // 4), "Vr"], [pyk], [(py[:, q0:512], Vr[:, kb, vb0:vb0 + 128], pb_[:, q0:512], kb == 0, kb == nkb - 1)])
                        if kb != nkb - 1:
                            return
                        ysl = slice(0, 64) if par == 0 else slice(64, 128)
                        dsl = slice(64, 128) if par == 0 else slice(0, 64)
                        s.I("dve", "tensor_copy", [pyk], [("dsb", par)], out=dsb[par][dsl, :], in_=py[dsl, :])
                        pd, pdk = pdr.next()
                        s.mm([("dsb", par), "cst"], [pdk], [(pd[:], selEO[par], dsb[par][:], True, True)])
                        yd, ydk = ydr.next()
                        s.I("dve", "reciprocal", [pdk], [ydk], out=yd[ysl, :], in_=pd[ysl, :])
                        if par == 0:
                            hstate["yo"] = yor.next()
                        yo, yok = hstate["yo"]
                        s.I("dve", "tensor_tensor", [pyk, ydk], [yok], out=yo[ysl, :], in0=py[ysl, :], in1=yd[ysl, :], op=ALU.mult)
                        if par == 1:
                            s.dma("pool", SND.ap()[t // 4, t % 4, 256 + 64 * (h - 1):256 + 64 * (h + 1), :], yo[:], reads=[yok])

                    for i in range(len(blocks) + LA):
                        if i < len(blocks):
                            emit_S(i)
                        if i - LA >= 0:
                            emit_rest(i - LA)
                        if i % int(os.environ.get("KYIELD", "4")) == 3:
                            yield

                bgc = [0]

                def bg_gen():
                    for (src_, dst_, shape) in bg_pieces():
                        t32, k32 = bstg.next()
                        tb, kb_ = bstb.next()
                        n = int(np.prod(shape[1:]))
                        if len(shape) == 3:
                            v32 = t32[:, 0:n].rearrange("p (a b) -> p a b", a=shape[1])
                            vb = tb[:, 0:n].rearrange("p (a b) -> p a b", a=shape[1])
                        else:
                            v32 = t32[:, 0:n]
                            vb = tb[:, 0:n]
                        s.dma("sp", v32, src_, writes=[k32])
                        bgc[0] += 1
                        if bgc[0] % 2 == 0:
                            s.I("dve", "tensor_copy", [k32], [kb_], out=tb[:, 0:n], in_=t32[:, 0:n])
                        else:
                            s.I("act", "activation", [k32], [kb_], out=tb[:, 0:n], in_=t32[:, 0:n], func=AF.Copy)
                        s.dma("pool", dst_, vb, reads=[kb_])
                        yield

                def alternate(*gs):
                    gens = [g for g in gs if g is not None]
                    while gens:
                        for g in list(gens):
                            try:
                                next(g)
                            except StopIteration:
                                gens.remove(g)

                import itertools
                bg = bg_gen() if (l == 0 and not os.environ.get("KNOBG")) else None
                pmr_bg[0] = pmr
                alternate(prep(0))
                for t in range(NG):
                    nb = 7 if t < NG - 1 else 1000
                    if os.environ.get("KSEQ"):
                        alternate(att(t))
                        alternate(prep(t + 1) if t + 1 < NG else None)
                    else:
                        alternate(att(t), prep(t + 1) if t + 1 < NG else None, itertools.islice(bg, nb) if bg is not None else None,
                                  itertools.islice(sgen, 14 if t < NG - 1 else 100000) if sgen is not None else None)
                s.flush()
            evac_mode[0] = "alt"
            if os.environ.get("KSTOP") == "1":
                return nc

            with contextlib.ExitStack() as st:
                sb, ps = mk(st)
                pmr = Ring([ps("pm%d" % i, [128, 512]) for i in range(4)], "pm")
                ptb = Ring([ps("ptb%d" % i, [128, 4, 128], BF16) for i in range(2)], "ptb")
                if l == 0:
                    pmr_bg[0] = pmr
                    for _ in make_setup():
                        pass
                Pre, Pim, S_ = SS["Pre"], SS["Pim"], SS["S_"]
                Ebd, Fbd, Klag, A1t, A2t = SS["Ebd"], SS["Fbd"], SS["Klag"], SS["A1t"], SS["A2t"]
                SK = ["setup"]
                DS = sb("DS", [128, 2, NPL, 513])
                s.I("dve", "memset", [], ["DS"], DS[:, :, :, 0:1], 0.0)
                ubr = Ring([sb("ub%d" % i, [128, 8, 512], BF16) for i in range(2)], "ub")
                with contextlib.ExitStack() as st2:
                    sb2, _ = mk(st2)
                    Epad = sb2("Epad", [128, 4, 2, 8, 128], BF16)
                    Wbp = sb2("Wbp", [128, 4, 2, 8, 128], BF16)
                    s.I("pool", "memset", [], ["Epad"], Epad[:], 0.0)
                    for b in range(NBL):
                        ub, ubk = ubr.next()
                        s.dma("sp", ub[:], UT.ap()[b], writes=[ubk])
                        for pb in range(4):
                            s.I("pool", "tensor_copy", SK + ["Epad"], ["Epad"], out=Epad[:, pb, :, :, 32 * pb:32 * pb + 32],
                                in_=Ebd[:, :, :, 4 * b + pb, :, :].rearrange("p r j g q -> p r j (g q)"))
                        for pb in range(4):
                            for ri in range(2):
                                for jj in range(0, 8, 4):
                                    tp, tpk = ptb.next()

                                    def fn(e, tp=tp, pb=pb, ri=ri, jj=jj):
                                        ins = None
                                        for x in range(4):
                                            ins = e.transpose(tp[:, x, :], Epad[:, pb, ri, jj + x, :], identb[:])
                                        return ins
                                    s.op("pe", fn, ["Epad", "identb"], [tpk])
                                    evac(Wbp[:, pb, ri, jj:jj + 4, :], tp[:], [tpk], ["Wbp"])
                        for pb in range(4):
                            for ri in range(2):
                                pt, pk = pmr.next()
                                s.mm(["Wbp", ubk], [pk], [(pt[:], Wbp[:, pb, ri, j, :], ub[:, j, :], j == 0, j == 7) for j in range(8)])
                                evac(DS[:, ri, 4 * b + pb, 1:513], pt[:], [pk], ["DS"])
                s.barrier()
                with contextlib.ExitStack() as st2:
                    sb2, _ = mk(st2)
                    ct1 = sb2("ct1", [128, 2, NPL, 32])
                    ct2 = sb2("ct2", [128, 2, NPL, 32])
                    tabr = sb2("tabr", [128, NPL, 32])
                    tabi = sb2("tabi", [128, NPL, 32])
                    C1 = sb2("C1", [128, 2, NPL, 32])
                    C2 = sb2("C2", [128, 2, NPL, 32])
                    B1 = sb2("B1", [128, 2, NPL])
                    B2 = sb2("B2", [128, 2, NPL])
                    DSv = DS[:, :, :, 1:513].rearrange("p r a (s i) -> p r a s i", i=32)
                    CK = ["DS", "setup", "ct1"]

                    def dv(meth, **kw):
                        s.I("dve", meth, CK, CK, **kw)

                    def cstep(cur, prev, c1, c2, n):
                        s.I("dve", "tensor_tensor", ["DS", "setup", "ct1"], ["ct1"], out=ct1[:, :, :, 0:n], in0=c1, in1=prev, op=ALU.mult)
                        s.I("dve", "tensor_tensor", ["DS", "setup", "ct2"], ["ct2"], out=ct2[:, 0, :, 0:n], in0=c2[:, 0], in1=prev[:, 1], op=ALU.mult)
                        s.I("dve", "tensor_tensor", ["DS", "setup", "ct2"], ["ct2"], out=ct2[:, 1, :, 0:n], in0=c2[:, 1], in1=prev[:, 0], op=ALU.mult)
                        s.I("dve", "tensor_tensor", ["DS", "ct1"], ["DS"], out=cur, in0=cur, in1=ct1[:, :, :, 0:n], op=ALU.add)
                        s.I("dve", "tensor_tensor", ["DS", "ct2"], ["DS"], out=cur, in0=cur, in1=ct2[:, :, :, 0:n], op=ALU.add)

                    A1b = A1t[:].unsqueeze(3).to_broadcast([128, 2, NPL, 16])
                    A2b = A2t[:].unsqueeze(3).to_broadcast([128, 2, NPL, 16])
                    for i in range(1, 32):
                        cstep(DSv[:, :, :, :, i], DSv[:, :, :, :, i - 1], A1b, A2b, 16)
                    pwr, pwi, q1, q2, q3 = [S_(40 + i) for i in range(5)]
                    dv("tensor_copy", out=pwr, in_=Pre[8])
                    dv("tensor_copy", out=pwi, in_=Pim[8])
                    dv("tensor_copy", out=tabr[:, :, 0], in_=Pre[8])
                    dv("tensor_copy", out=tabi[:, :, 0], in_=Pim[8])
                    w = 1
                    m1t = ct1[:, 0, :, :]
                    m2t = ct1[:, 1, :, :]
                    while w < 32:
                        pb_r = pwr.unsqueeze(2).to_broadcast([128, NPL, w])
                        pb_i = pwi.unsqueeze(2).to_broadcast([128, NPL, w])
                        dv("tensor_tensor", out=m1t[:, :, 0:w], in0=tabr[:, :, 0:w], in1=pb_r, op=ALU.mult)
                        dv("tensor_tensor", out=m2t[:, :, 0:w], in0=tabi[:, :, 0:w], in1=pb_i, op=ALU.mult)
                        dv("tensor_tensor", out=tabr[:, :, w:2 * w], in0=m1t[:, :, 0:w], in1=m2t[:, :, 0:w], op=ALU.subtract)
                        dv("tensor_tensor", out=m1t[:, :, 0:w], in0=tabr[:, :, 0:w], in1=pb_i, op=ALU.mult)
                        dv("tensor_tensor", out=m2t[:, :, 0:w], in0=tabi[:, :, 0:w], in1=pb_r, op=ALU.mult)
                        dv("tensor_tensor", out=tabi[:, :, w:2 * w], in0=m1t[:, :, 0:w], in1=m2t[:, :, 0:w], op=ALU.add)
                        dv("tensor_tensor", out=q1, in0=pwr, in1=pwr, op=ALU.mult)
                        dv("tensor_tensor", out=q2, in0=pwi, in1=pwi, op=ALU.mult)
                        dv("tensor_tensor", out=q3, in0=pwr, in1=pwi, op=ALU.mult)
                        dv("tensor_tensor", out=pwr, in0=q1, in1=q2, op=ALU.subtract)
                        dv("tensor_scalar", out=pwi, in0=q3, scalar1=2.0, scalar2=None, op0=ALU.mult)
                        w *= 2
                    dv("tensor_copy", out=B1[:, 0, :], in_=pwr)
                    dv("tensor_copy", out=B1[:, 1, :], in_=pwr)
                    dv("tensor_scalar", out=B2[:, 0, :], in0=pwi, scalar1=-1.0, scalar2=None, op0=ALU.mult)
                    dv("tensor_copy", out=B2[:, 1, :], in_=pwi)
                    dv("tensor_copy", out=C1[:, 0], in_=tabr[:])
                    dv("tensor_copy", out=C1[:, 1], in_=tabr[:])
                    dv("tensor_scalar", out=C2[:, 0], in0=tabi[:], scalar1=-1.0, scalar2=None, op0=ALU.mult)
                    dv("tensor_copy", out=C2[:, 1], in_=tabi[:])
                    B1b = B1[:].unsqueeze(3).to_broadcast([128, 2, NPL, 1])
                    B2b = B2[:].unsqueeze(3).to_broadcast([128, 2, NPL, 1])
                    for sg in range(1, 16):
                        cstep(DSv[:, :, :, sg, 31:32], DSv[:, :, :, sg - 1, 31:32], B1b, B2b, 1)
                    for sg in range(1, 16):
                        tp_ = DSv[:, :, :, sg - 1, 31:32].to_broadcast([128, 2, NPL, 31])
                        cstep(DSv[:, :, :, sg, 0:31], tp_, C1[:, :, :, 0:31], C2[:, :, :, 0:31], 31)
                s.barrier()
                with contextlib.ExitStack() as st2:
                    sb2, _ = mk(st2)
                    Fpad = sb2("Fpad", [128, 4, 2, 8, 128], BF16)
                    Sbf = sb2("Sbf", [128, 2, 4, 512], BF16)
                    gnat = Ring([sb2("gnat%d" % i, [128, 4096], BF16) for i in range(2)], "gnat")
                    s.I("dve", "memset", [], ["Fpad"], Fpad[:], 0.0)
                    for b in range(NBL):
                        ub, ubk = ubr.next()
                        s.dma("sp", ub[:], UT.ap()[b], writes=[ubk])
                        for pb in range(4):
                            s.I("dve", "tensor_copy", SK + ["Fpad"], ["Fpad"], out=Fpad[:, pb, :, :, 32 * pb:32 * pb + 32],
                                in_=Fbd[:, :, :, 4 * b + pb, :, :].rearrange("p r j g q -> p r j (g q)"))
                        for ri in range(2):
                            s.I("dve", "tensor_copy", ["DS", "Sbf"], ["Sbf"], out=Sbf[:, ri, :, :], in_=DS[:, ri, 4 * b:4 * b + 4, 0:512])
                        gn, gnk = gnat.next()
                        gv = gn[:].rearrange("p (c j) -> p j c", j=8)
                        for j in range(8):
                            pt, pk = pmr.next()
                            items = [(pt[:], Klag[:, b, d, :], ub[:, j - d, :], d == 0, False) for d in range(j + 1)]
                            items += [(pt[:], Fpad[:, pb, ri, j, :], Sbf[:, ri, pb, :], False, (pb == 3 and ri == 1))
                                      for pb in range(4) for ri in range(2)]
                            s.mm(["Klag", ubk, "Fpad", "Sbf"], [pk], items)
                            s.I("act", "activation", [pk, gnk], [gnk], out=gv[:, j, :], in_=pt[:], func=AF.Gelu_apprx_tanh)
                        for g8 in range(8):
                            s.dma("pool", SND.ap()[g8 // 4, g8 % 4, 128 * b:128 * (b + 1), :], gn[:, 512 * g8:512 * (g8 + 1)], reads=[gnk])
                if os.environ.get("KSTOP") == "2":
                    s.flush()
                    return nc
                for hf in range(2):
                    s.collective(SND.ap()[hf].rearrange("b r c -> (b r) c"), GAT.ap()[4096 * hf:4096 * (hf + 1), :])
                if os.environ.get("KSTOP") == "3":
                    return nc

            lst.close()
            with contextlib.ExitStack() as st:
                sb, ps = mk(st)
                pmr = Ring([ps("pm%d" % i, [128, 512]) for i in range(8)], "pm")
                gtl = sb("gtl", [128, 4, 512], BF16)
                gb = gtl
                sgr = Ring([sb("sg%d" % i, [128, 512]) for i in range(2)], "sg")
                osm = sb("osm", [128, 4, 512])
                sqr = Ring([sb("sq%d" % i, [128, 512], BF16) for i in range(3)], "sq")
                tmpr = Ring([sb("tm%d" % i, [128, 512]) for i in range(1)], "tm")
                rstdr = Ring([sb("rs%d" % i, [128, 512]) for i in range(2)], "rs")
                mixs = sb("mixs", [128, 4, 512], BF16)
                mixa = sb("mixa", [128, 4, 512], BF16)
                yar = Ring([sb("ya%d" % i, [128, 512], BF16) for i in range(4)], "ya")
                hbr = Ring([sb("hb%d" % i, [128, 512], BF16) for i in range(2)], "hb")
                hr = Ring([sb("hr%d" % i, [128, 512]) for i in range(4)], "hr")
                wglu = sb("wglu", [128, 4, 512], BF16)
                s.dma("sp", wglu[:], glu_b.ap()[l].rearrange("(k p) n -> p k n", p=128), writes=["wglu"])
                wos = Ring([sb("wos%d" % i, [128, 8, 128], BF16) for i in range(2)], "wos")
                h1T = sb("h1T", [128, 32, 512], BF16)
                wupr = Ring([sb("wup%d" % i, [128, 8, 512], BF16) for i in range(2)], "wup")
                wdnr = Ring([sb("wdn%d" % i, [128, 32, 128], BF16) for i in range(2)], "wdn")
                rlr = Ring([sb("rl%d" % i, [128, 512]) for i in range(2)], "rl")
                ofr = Ring([sb("of%d" % i, [128, 512]) for i in range(2)], "of")
                last = (l == nlayers - 1)
                hmid2 = [sb("hmidb%d" % i, [128, 8, 512]) for i in range(2)]
                xn22 = [sb("xn2b%d" % i, [128, 8, 512], BF16) for i in range(2)]

                def front(t):
                    t0 = t * 512
                    pz = t % 2
                    hmid, xn2 = hmid2[pz], xn22[pz]
                    for oc in range(4):
                        s.gather(gtl[:, oc, :], GAT.ap(), gix[:, 8 * t + oc:8 * t + oc + 1], reads=["gix"], writes=["gtl"])
                    yas = []
                    for h in range(4):
                        ya, yak = yar.next()
                        s.gather(ya[:], GAT.ap(), gix[:, 8 * t + 4 + h:8 * t + 4 + h + 1], reads=["gix"], writes=[yak])
                        yas.append((ya[:], yak))
                    yield
                    for oc in range(4):
                        pt, pk = pmr.next()
                        s.mm(["wglu", "gtl"], [pk], [(pt[:], wglu[:, k, 128 * oc:128 * (oc + 1)], gb[:, k, :], k == 0, k == 3) for k in range(4)])
                        sg, sgk = sgr.next()
                        s.I("act", "activation", [pk, "vecs"], [sgk], out=sg[:], in_=pt[:], func=AF.Sigmoid, bias=V[:, VGB + oc:VGB + oc + 1])
                        s.I("pool", "tensor_tensor", [sgk, "gtl"], [("osm", oc)], out=osm[:, oc, :], in0=gtl[:, oc, :], in1=sg[:], op=ALU.mult)
                        if oc % 2 == 1:
                            yield
                    rmsnorm((pmr, sqr, tmpr, rstdr), [(osm[:, oc, :], ("osm", oc)) for oc in range(4)], 128,
                            [V[:, VGS + oc:VGS + oc + 1] for oc in range(4)], 512.0, [(mixs[:, oc, :], "mixs") for oc in range(4)])
                    yield
                    rmsnorm((pmr, sqr, tmpr, rstdr), yas, 128, [V[:, VGA + h:VGA + h + 1] for h in range(4)], 512.0,
                            [(mixa[:, h, :], "mixa") for h in range(4)])
                    yield
                    for oc in range(8):
                        ws, wsk = wos.next()
                        s.dma("sp", ws[:], w_out_b.ap()[l, :, 128 * oc:128 * (oc + 1)].rearrange("(k p) n -> p k n", p=128), writes=[wsk])
                        ht, hk = hr.next()
                        s.dma("sp", ht[:], hown.ap()[128 * oc:128 * (oc + 1), t0:t0 + 512], writes=[hk])
                        pt, pk = pmr.next()
                        s.mm([wsk, "mixs", "mixa"], [pk],
                             [(pt[:], ws[:, k, :], mixs[:, k, :], k == 0, False) for k in range(4)] +
                             [(pt[:], ws[:, 4 + h, :], mixa[:, h, :], False, h == 3) for h in range(4)])
                        s.I("dve", "tensor_tensor", [pk, hk], [("hmid", pz, oc)], out=hmid[:, oc, :], in0=pt[:], in1=ht[:], op=ALU.add)
                        if oc % 2 == 1:
                            yield
                    rmsnorm((pmr, sqr, tmpr, rstdr), [(hmid[:, c, :], ("hmid", pz, c)) for c in range(8)], 128,
                            [V[:, VL2 + c:VL2 + c + 1] for c in range(8)], float(D), [(xn2[:, c, :], ("xn2", pz)) for c in range(8)])
                    yield

                def ffn(t):
                    t0 = t * 512
                    pz = t % 2
                    hmid, xn2 = hmid2[pz], xn22[pz]
                    for f4 in range(8):
                        wu, wuk = wupr.next()
                        s.dma("sp", wu[:], w_up_b.ap()[l, :, 512 * f4:512 * (f4 + 1)].rearrange("(k p) n -> p k n", p=128), writes=[wuk])
                        for fi in range(4):
                            f = 4 * f4 + fi
                            pt, pk = pmr.next()
                            s.mm([wuk, ("xn2", pz)], [pk], [(pt[:], wu[:, k, 128 * fi:128 * (fi + 1)], xn2[:, k, :], k == 0, k == 7) for k in range(8)])
                            rl, rlk = rlr.next()
                            s.I("act", "activation", [pk], [rlk], out=rl[:], in_=pt[:], func=AF.Relu)
                            s.I(("dve", "pool")[f % 2], "tensor_tensor", [rlk], [("h1T", f)], out=h1T[:, f, :], in0=rl[:], in1=rl[:], op=ALU.mult)
                        yield
                    for oc in range(8):
                        wd, wdk = wdnr.next()
                        s.dma("sp", wd[:], w_dn_b.ap()[l, :, 128 * oc:128 * (oc + 1)].rearrange("(f p) n -> p f n", p=128), writes=[wdk])
                        pt, pk = pmr.next()
                        s.mm([wdk] + [("h1T", f) for f in range(32)], [pk], [(pt[:], wd[:, f, :], h1T[:, f, :], f == 0, f == 31) for f in range(32)])
                        s.I("dve", "tensor_tensor", [pk, ("hmid", pz, oc)], [("hmid", pz, oc)], out=hmid[:, oc, :], in0=pt[:], in1=hmid[:, oc, :], op=ALU.add)
                        if not last:
                            s.dma("pool", h1own.ap()[128 * oc:128 * (oc + 1), t0:t0 + 512], hmid[:, oc, :], reads=[("hmid", pz, oc)])
                            hb, hbk = hbr.next()
                            s.I("pool", "tensor_copy", [("hmid", pz, oc)], [hbk], out=hb[:], in_=hmid[:, oc, :])
                            s.dma("pool", h1ownb.ap()[128 * oc:128 * (oc + 1), t0:t0 + 512], hb[:], reads=[hbk])
                        yield
                    if last:
                        pt, pk = pmr.next()
                        for c in range(8):
                            sq, sk = sqr.next()
                            s.I("act", "activation", [("hmid", pz, c)], [sk], out=sq[:], in_=hmid[:, c, :], func=AF.Square)
                            s.mm([sk, "onesb"], [pk], [(pt[:], onesb[:], sq[:], c == 0, c == 7)])
                        tm, tk = tmpr.next()
                        s.I("act", "activation", [pk, "cst"], [tk], out=tm[:], in_=pt[:], func=AF.Ln, scale=1.0 / D, bias=epsc)
                        rs, rk = rstdr.next()
                        s.I("act", "activation", [tk], [rk], out=rs[:], in_=tm[:], func=AF.Exp, scale=-0.5)
                        for c in range(8):
                            of_, ofk = ofr.next()
                            s.I("dve", "scalar_tensor_tensor", [("hmid", pz, c), rk, "vecs"], [ofk], out=of_[:], in0=hmid[:, c, :], scalar=fg[:, c:c + 1],
                                in1=rs[:], op0=ALU.mult, op1=ALU.mult)
                            s.dma("pool", outT.ap()[128 * c:128 * (c + 1), t0:t0 + 512], of_[:], reads=[ofk])
                        yield

                def alternate(g1, g2):
                    gens = [g for g in (g1, g2) if g is not None]
                    while gens:
                        for g in list(gens):
                            try:
                                next(g)
                            except StopIteration:
                                gens.remove(g)

                alternate(front(0), None)
                for t in range(4):
                    alternate(ffn(t), front(t + 1) if t + 1 < 4 else None)
                s.flush()
            if not last:
                for j in range(2):
                    s.collective(h1ownb.ap()[512 * j:512 * (j + 1), :], H1.ap()[1024 * j:1024 * (j + 1), :])
    return nc


def _consts():
    c = np.zeros((128, NCONST), np.float32)
    c[:, C_ID:C_ID + 128] = np.eye(128, dtype=np.float32)
    r = np.arange(128)
    c[:, C_PM:C_PM + 128] = (r[:, None] // 32 == r[None, :] // 32).astype(np.float32)
    c[:, C_TRI:C_TRI + 128] = (r[None, :] >= r[:, None]).astype(np.float32)
    selk = np.zeros((128, 8, 128), np.float32)
    selq = np.zeros((128, 8, 128), np.float32)
    c[64, C_SE:C_SE + 64] = 1.0
    c[0, C_SO + 64:C_SO + 128] = 1.0
    for h in range(8):
        selk[96, h, 64:67] = 1.0
        selk[h, h, 67] = 1.0
        selk[32 + h, h, 68] = 1.0
        selk[64 + h, h, 69] = 1.0
        selq[h, h, 64] = -1.0
        selq[32 + h, h, 65] = -1.0
        selq[64 + h, h, 66] = -1.0
        selq[96, h, 67:70] = 1.0
    c[:, C_SELK:C_SELK + 1024] = selk.reshape(128, 1024)
    c[:, C_SELQ:C_SELQ + 1024] = selq.reshape(128, 1024)
    c[:64, C_MISC + 0] = 1.0
    c[64:, C_MISC + 1] = 1.0
    c[:, C_MISC + 2] = EPS
    c[:, C_MISC + 3] = 1.0
    return c


def _prep(inp, r):
    f = lambda k: np.asarray(inp[k], dtype=np.float32)
    w_in = f("w_in")
    wcat = np.zeros((DEPTH, D, NCAT), np.float32)
    vecs = np.zeros((DEPTH, 128, NV), np.float32)
    ssm_small = np.zeros((DEPTH, 128, 24), np.float32)
    ssm_big = np.zeros((DEPTH, 128, 4, NPL, 16), np.float32)
    col = lambda v, n: np.ascontiguousarray(v.reshape(n, 128).T)
    for l in range(DEPTH):
        wcat[l, :, OU:OU + 256] = w_in[l, :, 256 * r:256 * (r + 1)]
        for hl in range(NHL):
            h = NHL * r + hl
            wcat[l, :, OQ + 128 * hl:OQ + 128 * hl + 64] = w_in[l, :, 512 + 64 * h:512 + 64 * (h + 1)]
            wcat[l, :, OK_ + 128 * hl:OK_ + 128 * hl + 64] = w_in[l, :, 1024 + 64 * h:1024 + 64 * (h + 1)]
        wcat[l, :, OV:OV + 256] = w_in[l, :, 1536 + 256 * r:1536 + 256 * (r + 1)]
        for base in (0, 32, 64, 96):
            wcat[l, :, OF + base:OF + base + NHL] = w_in[l, :, 2048 + NHL * r:2048 + NHL * (r + 1)]
            vecs[l, base:base + NHL, VFB] = f("fgate_b")[l][NHL * r:NHL * (r + 1)]
        vecs[l, :, VL1:VL1 + 8] = col(f("ln1_g")[l], 8)
        vecs[l, :, VL2:VL2 + 8] = col(f("ln2_g")[l], 8)
        vecs[l, :, VGS:VGS + 4] = col(f("gn_ssm_g")[l], 4)
        vecs[l, :, VGB:VGB + 4] = col(f("glu_b")[l], 4)
        vecs[l, :, VDS:VDS + 2] = col(f("ssm_d")[l][256 * r:256 * (r + 1)], 2)
        vecs[l, :, VGA:VGA + 4] = col(f("gn_attn_g")[l], 4)
        gs = slice(16 * r, 16 * (r + 1))
        ldt = f("ssm_log_dt")[l][gs].reshape(NPL, 2)
        ssm_small[l, :, 0:8] = np.repeat(ldt.T[:, None, :], 64, axis=1).reshape(128, NPL)
        for i, k in enumerate(("ssm_lambda_re", "ssm_lambda_im")):
            a = f(k)[l][gs].reshape(NPL, 2, 64)
            ssm_small[l, :, 8 * (i + 1):8 * (i + 2)] = a.transpose(1, 2, 0).reshape(128, NPL)
        for i, k in enumerate(("ssm_b_re", "ssm_b_im")):
            a = f(k)[l][gs].reshape(NPL, 2, 64, 16)
            ssm_big[l, :, i] = a.transpose(1, 2, 0, 3).reshape(128, NPL, 16)
        for i, k in enumerate(("ssm_c_re", "ssm_c_im")):
            a = f(k)[l][gs].reshape(NPL, 2, 16, 64)
            ssm_big[l, :, 2 + i] = a.transpose(1, 3, 0, 2).reshape(128, NPL, 16)
    gidx = np.zeros((128, 32), np.uint32)
    p = np.arange(128)
    for tl in range(4):
        for kind in range(2):
            for rk in range(2):
                for sub in range(2):
                    base = r * 4096 + rk * 2048 + tl * 512
                    gidx[:, 8 * tl + 4 * kind + 2 * rk + sub] = base + 256 * kind + 128 * sub + p
    return dict(wcat=wcat, vecs=vecs, ssm_small=ssm_small, ssm_big=ssm_big.reshape(DEPTH, 128, 512), gidx=gidx)


_NC_CACHE = {}


def kernel(**inputs):
    x = np.asarray(inputs["x"], dtype=np.float32)
    f = lambda k: np.asarray(inputs[k], dtype=np.float32)
    col = lambda v, n: np.ascontiguousarray(v.reshape(n, 128).T)
    shared = dict(w_out=f("w_out"), glu_w=f("glu_w"), w_up=f("w_up"), w_down=f("w_down"),
                  fing=col(f("final_g"), 8), consts=_consts())
    per_rank = [_prep(inputs, r) for r in range(2)]
    if "nc" not in _NC_CACHE:
        _NC_CACHE["nc"] = build()
    nc = _NC_CACHE["nc"]
    in_maps = []
    for i in range(8):
        b, r = i // 2, i % 2
        m = dict(shared)
        m.update(per_rank[r])
        xt = np.ascontiguousarray(x[b].T)
        m["xT"] = xt
        m["xTo"] = np.ascontiguousarray(xt[:, LH * r:LH * (r + 1)])
        in_maps.append(m)
    res = run_bass_kernel_spmd(nc, in_maps, core_ids=list(range(8)))
    out = np.zeros((4, L, D), np.float32)
    for i in range(8):
        b, r = i // 2, i % 2
        out[b, LH * r:LH * (r + 1), :] = res.results[i]["outT"].T
    return out
```

```python
import contextlib
import os
import numpy as np
import concourse.bass as bass
import concourse.mybir as mybir
from concourse.bass_utils import run_bass_kernel_spmd

F32 = mybir.dt.float32
BF16 = mybir.dt.bfloat16
AF = mybir.ActivationFunctionType
ALU = mybir.AluOpType

ENGS = ("pe", "act", "dve", "pool", "sp")
L = 4096
D = 1024
NG = 8
DEPTH = 2
EPS = 1e-6
NHL, NBL, NPL = 4, 2, 8
LH = L // 2
NCAT = 256 + 512 + 512 + 256 + 104
OU, OQ, OK_, OV, OF = 0, 256, 768, 1280, 1536
U32 = mybir.dt.uint32
VL1, VL2, VGS, VGB, VFB, VDS, VGA = 0, 8, 16, 20, 24, 25, 27
NV = 31
C_ID, C_PM, C_TRI, C_SE, C_SO, C_SELK, C_SELQ, C_MISC = 0, 128, 256, 384, 512, 640, 640 + 1024, 640 + 2048
NCONST = C_MISC + 4


class Sched:
    def __init__(self, nc, st, n_dma_sems=8):
        self.nc = nc
        self.nd = n_dma_sems
        self.csem = {e: st.enter_context(nc.semaphore("c_" + e)) for e in ENGS}
        self.dsem = {(q, k): st.enter_context(nc.semaphore("d_%s%d" % (q, k)))
                     for q in ("sp", "act", "pool") for k in range(n_dma_sems)}
        self.base = {e: 0 for e in ENGS}
        self.ops = {e: [] for e in ENGS}
        self.last_w = {}
        self.readers = {}
        self.seen = {e: {} for e in ENGS}
        self.dma_uses = {}
        self.dma_rr = {e: 0 for e in ENGS}
        self.ninst = 0
        self.ccsem = st.enter_context(nc.semaphore("ccsem"))
        self.ncc = 0

    def collective(self, in_ap, out_ap):
        self.flush()
        self.ncc += 1
        n = self.ncc
        self.ops["pool"].append(dict(kind="cc", waits=[], inc=False, fn=lambda e: e.collective_compute(
            "AllGather", ALU.bypass, replica_groups=[[0, 1], [2, 3], [4, 5], [6, 7]], ins=[in_ap], outs=[out_ap])))
        for e in ENGS:
            self.ops[e].append(dict(kind="wcc", waits=[], inc=False, n=n))
        self.flush()

    def _need(self, eng, ref, waits):
        if ref is None:
            return
        if ref[0] == "c":
            _, src, idx = ref
            if src == "pe" and eng == "pe":
                return
            cur = self.seen[eng].get(("c", src), -1)
            if idx > cur:
                self.seen[eng][("c", src)] = idx
                self.ops[src][idx]["inc"] = True
                waits.append(ref)
        else:
            _, q, k, val = ref
            cur = self.seen[eng].get(("d", q, k), 0)
            if val > cur:
                self.seen[eng][("d", q, k)] = val
                waits.append(ref)

    def _deps(self, eng, reads, writes):
        waits = []
        for k in reads:
            self._need(eng, self.last_w.get(k), waits)
        for k in writes:
            self._need(eng, self.last_w.get(k), waits)
            for r in self.readers.get(k, ()):
                self._need(eng, r, waits)
        return waits

    def _commit(self, ref, reads, writes):
        for k in reads:
            self.readers.setdefault(k, []).append(ref)
        for k in writes:
            self.last_w[k] = ref
            self.readers[k] = []

    def op(self, eng, fn, reads=(), writes=()):
        waits = self._deps(eng, reads, writes)
        idx = len(self.ops[eng])
        self.ops[eng].append(dict(kind="c", fn=fn, waits=waits, inc=False))
        self._commit(("c", eng, idx), reads, writes)

    def I(self, eng, meth, reads, writes, *a, **kw):
        self.op(eng, lambda e: getattr(e, meth)(*a, **kw), reads, writes)

    def mm(self, reads, writes, items):
        items = list(items)

        def fn(e):
            ins = None
            for (o, l, r, st_, sp_) in items:
                ins = e.matmul(o, l, r, start=st_, stop=sp_)
            return ins
        self.op("pe", fn, reads, writes)

    def dma(self, q, out, in_, reads=(), writes=()):
        k = self.dma_rr[q] % self.nd
        self.dma_rr[q] += 1
        uses = self.dma_uses.get((q, k), 0)
        waits = []
        if uses > 0:
            self._need(q, ("d", q, k, 16 * uses), waits)
        waits += self._deps(q, reads, writes)
        self.dma_uses[(q, k)] = uses + 1
        ref = ("d", q, k, 16 * (uses + 1))
        self.ops[q].append(dict(kind="d", fn=lambda e: e.dma_start(out=out, in_=in_), waits=waits,
                                inc=False, dsem=(q, k)))
        self._commit(ref, reads, writes)

    def gather(self, out, in_, idx, reads=(), writes=()):
        q = "pool"
        k = self.dma_rr[q] % self.nd
        self.dma_rr[q] += 1
        uses = self.dma_uses.get((q, k), 0)
        waits = []
        if uses > 0:
            self._need(q, ("d", q, k, 16 * uses), waits)
        waits += self._deps(q, reads, writes)
        self.dma_uses[(q, k)] = uses + 1
        ref = ("d", q, k, 16 * (uses + 1))
        self.ops[q].append(dict(kind="d", waits=waits, inc=False, dsem=(q, k), fn=lambda e: e.indirect_dma_start(
            out=out, out_offset=None, in_=in_, in_offset=bass.IndirectOffsetOnAxis(ap=idx, axis=0))))
        self._commit(ref, reads, writes)

    def barrier(self):
        lastc = {}
        for e in ENGS:
            for i in range(len(self.ops[e]) - 1, -1, -1):
                if self.ops[e][i]["kind"] == "c":
                    lastc[e] = i
                    break
        for e in ENGS:
            waits = []
            for src, i in lastc.items():
                self._need(e, ("c", src, i), waits)
            for (q, k), uses in self.dma_uses.items():
                self._need(e, ("d", q, k, 16 * uses), waits)
            if waits:
                self.ops[e].append(dict(kind="w", fn=None, waits=waits, inc=False))
        self.last_w.clear()
        self.readers.clear()

    def flush(self):
        self.barrier()
        nc = self.nc
        val = {}
        for e in ENGS:
            c = self.base[e]
            vals = []
            for rec in self.ops[e]:
                if rec["kind"] == "c" and rec["inc"]:
                    c += 1
                    vals.append(c)
                else:
                    vals.append(None)
            val[e] = vals
            self.base[e] = c
        ops = self.ops

        def run(e, eng):
            for rec in ops[e]:
                for w in rec["waits"]:
                    if w[0] == "c":
                        eng.wait_ge(self.csem[w[1]], val[w[1]][w[2]])
                    else:
                        eng.wait_ge(self.dsem[(w[1], w[2])], w[3])
                if rec["kind"] == "c":
                    ins = rec["fn"](eng)
                    if rec["inc"]:
                        ins.then_inc(self.csem[e], 1)
                elif rec["kind"] == "d":
                    rec["fn"](eng).then_inc(self.dsem[rec["dsem"]], 16)
                elif rec["kind"] == "cc":
                    rec["fn"](eng).then_inc(self.ccsem, 1)
                elif rec["kind"] == "wcc":
                    eng.wait_ge(self.ccsem, rec["n"])
                self.ninst += 1

        with nc.Block() as block:
            @block.tensor
            def _(eng):
                run("pe", eng)

            @block.scalar
            def _(eng):
                run("act", eng)

            @block.vector
            def _(eng):
                run("dve", eng)

            @block.gpsimd
            def _(eng):
                run("pool", eng)

            @block.sync
            def _(eng):
                run("sp", eng)
        self.ops = {e: [] for e in ENGS}
        for e in ENGS:
            for k in list(self.seen[e].keys()):
                if k[0] == "c":
                    del self.seen[e][k]


class Ring:
    def __init__(self, tiles, name):
        self.tiles = tiles
        self.name = name
        self.i = 0

    def next(self):
        k = self.i % len(self.tiles)
        self.i += 1
        return self.tiles[k], (self.name, k)


def build(debug=False, nlayers=DEPTH):
    nc = bass.Bass("TRN2", target_bir_lowering=False)
    din = lambda n, shp, dt=F32: nc.dram_tensor(n, shp, dt, kind="ExternalInput")
    xT = din("xT", [D, L])
    xTo = din("xTo", [D, LH])
    gidx_d = din("gidx", [128, 32], U32)
    wcat = din("wcat", [DEPTH, D, NCAT])
    w_out = din("w_out", [DEPTH, D, D])
    glu_w = din("glu_w", [DEPTH, 512, 512])
    w_up = din("w_up", [DEPTH, D, 4096])
    w_down = din("w_down", [DEPTH, 4096, D])
    vecs = din("vecs", [DEPTH, 128, NV])
    fing = din("fing", [128, 8])
    ssm_small = din("ssm_small", [DEPTH, 128, 24])
    ssm_big = din("ssm_big", [DEPTH, 128, 512])
    consts = din("consts", [128, NCONST])
    outT = nc.dram_tensor("outT", [D, LH], F32, kind="ExternalOutput")
    h1own = nc.dram_tensor("h1own", [D, LH], F32, kind="Internal")
    h1ownb = nc.dram_tensor("h1ownb", [D, LH], BF16, kind="Internal")
    H1 = nc.dram_tensor("H1", [2 * D, LH], BF16, kind="Internal")
    UT = nc.dram_tensor("UT", [NBL, 128, 8, 512], BF16, kind="Internal")
    SND = nc.dram_tensor("SND", [2, 4, 512, 512], BF16, kind="Internal")
    GAT = nc.dram_tensor("GAT", [2 * 2 * 4 * 512, 512], BF16, kind="Internal")
    wcat_b = nc.dram_tensor("wcat_b", [DEPTH, D, NCAT], BF16, kind="Internal")
    w_out_b = nc.dram_tensor("w_out_b", [DEPTH, D, D], BF16, kind="Internal")
    glu_b = nc.dram_tensor("glu_wb", [DEPTH, 512, 512], BF16, kind="Internal")
    w_up_b = nc.dram_tensor("w_up_b", [DEPTH, D, 4096], BF16, kind="Internal")
    w_dn_b = nc.dram_tensor("w_dn_b", [DEPTH, 4096, D], BF16, kind="Internal")

    top = contextlib.ExitStack()
    with top:
        s = Sched(nc, top)

        uid = [0]

        def mk(st):
            uid[0] += 1
            pre = "u%d_" % uid[0]

            def sb(name, shape, dt=F32):
                return st.enter_context(nc.sbuf_tensor(pre + name, shape, dt))

            def ps(name, shape, dt=F32):
                return st.enter_context(nc.psum_tensor(pre + name, shape, dt))
            return sb, ps

        gsb, _ = mk(top)
        cst = gsb("cst", [128, NCONST])
        s.dma("sp", cst[:], consts.ap(), writes=["cst"])
        identb = gsb("identb", [128, 128], BF16)
        trib = gsb("trib", [128, 128], BF16)
        selk = gsb("selk", [128, 8, 128], BF16)
        selq = gsb("selq", [128, 8, 128], BF16)
        s.I("dve", "tensor_copy", ["cst"], ["identb"], out=identb[:], in_=cst[:, C_ID:C_ID + 128])
        s.I("dve", "tensor_copy", ["cst"], ["trib"], out=trib[:], in_=cst[:, C_TRI:C_TRI + 128])
        s.I("dve", "tensor_copy", ["cst"], ["selk"], out=selk[:].rearrange("p h c -> p (h c)"), in_=cst[:, C_SELK:C_SELK + 1024])
        s.I("dve", "tensor_copy", ["cst"], ["selq"], out=selq[:].rearrange("p h c -> p (h c)"), in_=cst[:, C_SELQ:C_SELQ + 1024])
        identf = cst[:, C_ID:C_ID + 128]
        selEO = [cst[:, C_SE:C_SE + 128], cst[:, C_SO:C_SO + 128]]
        pmf = cst[:, C_PM:C_PM + 128]
        mkc = [cst[:, C_MISC + 0:C_MISC + 1], cst[:, C_MISC + 1:C_MISC + 2]]
        epsc = cst[:, C_MISC + 2:C_MISC + 3]
        onec = cst[:, C_MISC + 3:C_MISC + 4]
        onesf = gsb("onesf", [128, 128])
        s.I("dve", "memset", [], ["onesf"], onesf[:], 1.0)
        onesb = gsb("onesb", [128, 128], BF16)
        s.I("dve", "memset", [], ["onesb"], onesb[:], 1.0)
        vec = [gsb("vec%d" % l, [128, NV]) for l in range(DEPTH)]
        for l in range(DEPTH):
            s.dma("sp", vec[l][:], vecs.ap()[l], writes=["vecs"])
        fg = gsb("fg", [128, 8])
        s.dma("sp", fg[:], fing.ap(), writes=["vecs"])
        gix = gsb("gix", [128, 32], U32)
        s.dma("sp", gix[:], gidx_d.ap(), writes=["gix"])

        with contextlib.ExitStack() as st:
            sb, ps = mk(st)
            stg = Ring([sb("stg%d" % i, [128, 4096]) for i in range(3)], "stg")
            stb = Ring([sb("stb%d" % i, [128, 4096], BF16) for i in range(3)], "stb")
            cnt = [0]

            def cvt(src, dst, shape):
                t32, k32 = stg.next()
                tb, kb = stb.next()
                n = int(np.prod(shape[1:]))
                if len(shape) == 3:
                    v32 = t32[:, 0:n].rearrange("p (a b) -> p a b", a=shape[1])
                    vb = tb[:, 0:n].rearrange("p (a b) -> p a b", a=shape[1])
                else:
                    v32 = t32[:, 0:n]
                    vb = tb[:, 0:n]
                s.dma("sp", v32, src, writes=[k32])
                eng = ("dve", "pool", "act")[cnt[0] % 3]
                cnt[0] += 1
                if eng == "act":
                    s.I("act", "activation", [k32], [kb], out=tb[:, 0:n], in_=t32[:, 0:n], func=AF.Copy)
                else:
                    s.I(eng, "tensor_copy", [k32], [kb], out=tb[:, 0:n], in_=t32[:, 0:n])
                s.dma("pool", dst, vb, reads=[kb])

            def v3(t, l, r0, nr):
                return t.ap()[l, r0 * 128:(r0 + nr) * 128, :].rearrange("(a p) n -> p a n", p=128)

            for rc in range(8):
                cvt(wcat.ap()[0, rc * 128:(rc + 1) * 128, :], wcat_b.ap()[0, rc * 128:(rc + 1) * 128, :], [128, NCAT])
            s.flush()

        def bg_pieces():
            for l in range(nlayers):
                if l > 0:
                    for rc in range(8):
                        yield (wcat.ap()[l, rc * 128:(rc + 1) * 128, :], wcat_b.ap()[l, rc * 128:(rc + 1) * 128, :], [128, NCAT])
                for rc in range(0, 8, 4):
                    yield (v3(w_out, l, rc, 4), v3(w_out_b, l, rc, 4), [128, 4, 1024])
                yield (v3(glu_w, l, 0, 4), v3(glu_b, l, 0, 4), [128, 4, 512])
                for rc in range(8):
                    yield (w_up.ap()[l, rc * 128:(rc + 1) * 128, :], w_up_b.ap()[l, rc * 128:(rc + 1) * 128, :], [128, 4096])
                for rc in range(0, 32, 4):
                    yield (v3(w_down, l, rc, 4), v3(w_dn_b, l, rc, 4), [128, 4, 1024])

        def v3(t, l, r0, nr):
            return t.ap()[l, r0 * 128:(r0 + nr) * 128, :].rearrange("(a p) n -> p a n", p=128)

        def rmsnorm(rings, srcs, P, gcols, Dn, outs):
            pmr, sqr, tmpr, rstdr = rings
            pt, pk = pmr.next()
            n = len(srcs)
            for c, (a, k) in enumerate(srcs):
                sq, sk = sqr.next()
                s.I("act", "activation", [k], [sk], out=sq[0:P, :], in_=a, func=AF.Square)
                s.mm([sk, "onesb"], [pk], [(pt[0:P, :], onesb[0:P, 0:P], sq[0:P, :], c == 0, c == n - 1)])
            tm, tk = tmpr.next()
            s.I("act", "activation", [pk, "cst"], [tk], out=tm[0:P, :], in_=pt[0:P, :], func=AF.Ln,
                scale=1.0 / Dn, bias=epsc[0:P, :])
            rs, rk = rstdr.next()
            s.I("act", "activation", [tk], [rk], out=rs[0:P, :], in_=tm[0:P, :], func=AF.Exp, scale=-0.5)
            for c, ((a, k), (o, ok)) in enumerate(zip(srcs, outs)):
                s.I("dve", "scalar_tensor_tensor", [k, rk, "vecs"], [ok], out=o, in0=a, scalar=gcols[c],
                    in1=rs[0:P, :], op0=ALU.mult, op1=ALU.mult)

        evac_cnt = [0]
        evac_mode = ["alt"]

        def evac(out, in_, reads, writes):
            e = ("act", "dve")[evac_cnt[0] % 2]
            if evac_mode[0] == "dve":
                e = "dve"
            evac_cnt[0] += 1
            if e == "act":
                s.I("act", "activation", reads, writes, out=out, in_=in_, func=AF.Copy)
            else:
                s.I("dve", "tensor_copy", reads, writes, out=out, in_=in_)

        for l in range(nlayers):
            def hsrc_ap(c, t):
                if l == 0:
                    return xT.ap()[c * 128:(c + 1) * 128, t * 512:(t + 1) * 512]
                r0 = (c // 4) * 1024 + (t // 4) * 512 + (c % 4) * 128
                return H1.ap()[r0:r0 + 128, (t % 4) * 512:(t % 4 + 1) * 512]
            hown = xTo if l == 0 else h1own
            V = vec[l]
            lst = contextlib.ExitStack()
            SS = {}
            pmr_bg = [None]

            def make_setup():
                lsb, _lps = mk(lst)
                ssm_s = lsb("ssm_s", [128, 24])
                ssm_b = lsb("ssm_b", [128, 4, NPL, 16])
                sc = lsb("sc", [128, 64, NPL])
                big = lsb("big", [128, 12, NPL, 16])
                Ebd = lsb("Ebd", [128, 2, 8, NPL, 2, 16], BF16)
                Fbd = lsb("Fbd", [128, 2, 8, NPL, 2, 16], BF16)
                Cbd = lsb("Cbd", [128, 2, NPL, 2, 16], BF16)
                Klag = lsb("Klag", [128, NBL, 8, 128], BF16)
                A1t = lsb("A1t", [128, 2, NPL])
                A2t = lsb("A2t", [128, 2, NPL])
                ktmp = lsb("ktmp", [128, 128])
                SS.update(ssm_s=ssm_s, ssm_b=ssm_b, sc=sc, big=big, Ebd=Ebd, Fbd=Fbd, Cbd=Cbd, Klag=Klag, A1t=A1t, A2t=A2t, ktmp=ktmp)

                def ssm_setup():
                        s.dma("sp", ssm_s[:], ssm_small.ap()[l], writes=["setup"])
                        s.dma("sp", ssm_b[:].rearrange("p a b c -> p (a b c)"), ssm_big.ap()[l], writes=["setup"])
                        SK = ["setup"]

                        def tt(o, a, b, op):
                            s.I("dve", "tensor_tensor", SK, SK, out=o, in0=a, in1=b, op=op)

                        def ts(o, a, s1, op0, s2=None, op1=None):
                            if op1 is None:
                                s.I("dve", "tensor_scalar", SK + ["cst"], SK, out=o, in0=a, scalar1=s1, scalar2=None, op0=op0)
                            else:
                                s.I("dve", "tensor_scalar", SK + ["cst"], SK, out=o, in0=a, scalar1=s1, scalar2=s2, op0=op0, op1=op1)

                        def af(o, a, func, scale=1.0):
                            s.I("act", "activation", SK, SK, out=o, in_=a, func=func, scale=scale)

                        ldt, lre, lim = ssm_s[:, 0:8], ssm_s[:, 8:16], ssm_s[:, 16:24]
                        S_ = lambda i: sc[:, i, :]
                        dt_, x1, mag, th, s16, sn, cs, t1_, t2_, t3_ = [S_(i) for i in range(10)]
                        are, aim, den, rden, nr, sre, sim = [S_(i) for i in range(10, 17)]
                        af(dt_, ldt, AF.Exp)
                        tt(x1, lre, dt_, ALU.mult)
                        af(mag, x1, AF.Exp)
                        tt(th, lim, dt_, ALU.mult)
                        af(s16, th, AF.Sin, 1.0 / 16)
                        af(sn, th, AF.Sin, 1.0 / 8)
                        yield
                        tt(t1_, s16, s16, ALU.mult)
                        ts(cs, t1_, -2.0, ALU.mult, 1.0, ALU.add)
                        for _ in range(3):
                            tt(t1_, cs, cs, ALU.mult)
                            tt(t2_, sn, sn, ALU.mult)
                            tt(t3_, cs, sn, ALU.mult)
                            tt(cs, t1_, t2_, ALU.subtract)
                            ts(sn, t3_, 2.0, ALU.mult)
                        tt(are, mag, cs, ALU.mult)
                        tt(aim, mag, sn, ALU.mult)
                        tt(t1_, lre, lre, ALU.mult)
                        tt(t2_, lim, lim, ALU.mult)
                        yield
                        tt(den, t1_, t2_, ALU.add)
                        s.I("dve", "reciprocal", SK, SK, out=rden, in_=den)
                        ts(nr, are, -1.0, ALU.add)
                        tt(t1_, nr, lre, ALU.mult)
                        tt(t2_, aim, lim, ALU.mult)
                        tt(t1_, t1_, t2_, ALU.add)
                        yield
                        tt(sre, t1_, rden, ALU.mult)
                        tt(t1_, aim, lre, ALU.mult)
                        tt(t2_, nr, lim, ALU.mult)
                        tt(t1_, t1_, t2_, ALU.subtract)
                        tt(sim, t1_, rden, ALU.mult)
                        bc = lambda a: a.unsqueeze(2).to_broadcast([128, NPL, 16])
                        Bre, Bim, CreT, CimT = [ssm_b[:, i, :, :] for i in range(4)]
                        Bbre, Bbim, m1, m2, Ere, Eim = [big[:, i, :, :] for i in range(6)]

                        def cmul(ore, oim, are_, aim_, pre, pim):
                            tt(m1, are_, bc(pre), ALU.mult)
                            tt(m2, aim_, bc(pim), ALU.mult)
                            tt(ore, m1, m2, ALU.subtract)
                            tt(m1, aim_, bc(pre), ALU.mult)
                            tt(m2, are_, bc(pim), ALU.mult)
                            tt(oim, m1, m2, ALU.add)

                        cmul(Bbre, Bbim, Bre, Bim, sre, sim)
                        yield
                        Pre = [S_(20 + 2 * j) for j in range(9)]
                        Pim = [S_(21 + 2 * j) for j in range(9)]
                        s.I("dve", "memset", SK, SK, Pre[0], 1.0)
                        s.I("dve", "memset", SK, SK, Pim[0], 0.0)
                        for j in range(1, 9):
                            tt(t1_, Pre[j - 1], are, ALU.mult)
                            tt(t2_, Pim[j - 1], aim, ALU.mult)
                            tt(Pre[j], t1_, t2_, ALU.subtract)
                            tt(t1_, Pre[j - 1], aim, ALU.mult)
                            tt(t2_, Pim[j - 1], are, ALU.mult)
                            tt(Pim[j], t1_, t2_, ALU.add)
                        s.I("dve", "tensor_copy", SK, SK, out=A1t[:, 0, :], in_=Pre[8])
                        s.I("dve", "tensor_copy", SK, SK, out=A1t[:, 1, :], in_=Pre[8])
                        ts(A2t[:, 0, :], Pim[8], -1.0, ALU.mult)
                        s.I("dve", "tensor_copy", SK, SK, out=A2t[:, 1, :], in_=Pim[8])
                        yield
                        for j in range(8):
                            cmul(Ere, Eim, Bbre, Bbim, Pre[7 - j], Pim[7 - j])
                            yield
                            for g2 in range(2):
                                ts(Ebd[:, 0, j, :, g2, :], Ere, mkc[g2], ALU.mult)
                                ts(Ebd[:, 1, j, :, g2, :], Eim, mkc[g2], ALU.mult)
                        for j in range(8):
                            cmul(Ere, Eim, CreT, CimT, Pre[j + 1], Pim[j + 1])
                            yield
                            for g2 in range(2):
                                ts(Fbd[:, 0, j, :, g2, :], Ere, mkc[g2], ALU.mult)
                                ts(Fbd[:, 1, j, :, g2, :], Eim, mkc[g2], ALU.mult, -1.0, ALU.mult)
                        for g2 in range(2):
                            ts(Cbd[:, 0, :, g2, :], CreT, mkc[g2], ALU.mult)
                            ts(Cbd[:, 1, :, g2, :], CimT, mkc[g2], ALU.mult, -1.0, ALU.mult)
                        for b in range(NBL):
                            for d in range(8):
                                pt, pk = pmr_bg[0].next()
                                er = Ebd[:, 0, 7 - d, 4 * b:4 * b + 4, :, :].rearrange("p a g q -> p (a g q)")
                                ei = Ebd[:, 1, 7 - d, 4 * b:4 * b + 4, :, :].rearrange("p a g q -> p (a g q)")
                                cr = Cbd[:, 0, 4 * b:4 * b + 4, :, :].rearrange("p a g q -> p (a g q)")
                                ci = Cbd[:, 1, 4 * b:4 * b + 4, :, :].rearrange("p a g q -> p (a g q)")
                                s.mm(SK, [pk], [(pt[:, 0:128], er, cr, True, False), (pt[:, 0:128], ei, ci, False, True)])
                                if d == 0:
                                    s.I("dve", "tensor_tensor", [pk, "cst"], ["ktmp"], out=ktmp[:], in0=pt[:, 0:128], in1=pmf, op=ALU.mult)
                                    s.I("dve", "scalar_tensor_tensor", ["ktmp", "cst", "vecs"], ["Klag"], out=Klag[:, b, d, :], in0=identf,
                                        scalar=V[:, VDS + b:VDS + b + 1], in1=ktmp[:], op0=ALU.mult, op1=ALU.add)
                                else:
                                    s.I("dve", "tensor_tensor", [pk, "cst"], ["Klag"], out=Klag[:, b, d, :], in0=pt[:, 0:128], in1=pmf, op=ALU.mult)
                            yield

                        SS.update(Pre=Pre, Pim=Pim, S_=S_)
                        yield
                return ssm_setup()

            sgen = make_setup() if l > 0 else None
            evac_mode[0] = "dve"
            with contextlib.ExitStack() as st:
                sb, ps = mk(st)
                KT = [sb("KT%d" % h, [128, L], BF16) for h in range(NHL)]
                Vr = sb("Vr", [128, 32, 384], BF16)
                s.I("pool", "memset", [], ["Vr"], Vr[:].rearrange("p k (a c) -> p k a c", c=192)[:, :, :, 64:128], 1.0)
                dsb = [sb("dsb%d" % i, [128, 512]) for i in range(2)]
                for i in range(2):
                    s.I("pool", "memset", [], [("dsb", i)], dsb[i][:], 0.0)
                wr = Ring([sb("wr%d" % i, [128, 8, 128], BF16) for i in range(3)], "wr")
                wv = sb("wv", [128, 8, 256], BF16)
                s.dma("sp", wv[:], wcat_b.ap()[l, :, OV:OV + 256].rearrange("(k p) n -> p k n", p=128), writes=["wv"])
                hr = Ring([sb("hr%d" % i, [128, 512], F32 if l == 0 else BF16) for i in range(8)], "hr")
                sqr = Ring([sb("sq%d" % i, [128, 512], BF16) for i in range(3)], "sq")
                tmpr = Ring([sb("tm%d" % i, [128, 512]) for i in range(1)], "tm")
                rstdr = Ring([sb("rs%d" % i, [128, 512]) for i in range(1)], "rs")
                xn = sb("xn", [128, 8, 512], BF16)
                QT2 = [[sb("QT%d_%d" % (z, h), [128, 512], BF16) for h in range(NHL)] for z in range(2)]
                if l == 0:
                    bstg = Ring([sb("bstg%d" % i, [128, 4096]) for i in range(2)], "bstg")
                    bstb = Ring([sb("bstb%d" % i, [128, 4096], BF16) for i in range(2)], "bstb")
                ust = Ring([sb("ust%d" % i, [128, 8, 64], BF16) for i in range(2)], "ust")
                caug = sb("caug", [128, 512], BF16)
                s.I("dve", "memset", [], ["caug"], caug[:], 0.0)
                s.I("dve", "memset", ["caug"], ["caug"], caug[96:104, :], 1.0)
                onesr = sb("onesr", [128, 512], BF16)
                s.I("dve", "memset", [], ["onesr"], onesr[:], 1.0)
                fz = sb("fz", [128, 512])
                fX = [sb("fX%d" % i, [128, 512]) for i in range(2)]
                fr1 = fz
                fb1 = sb("fb1", [128, 512], BF16)
                fb2 = sb("fb2", [128, 512], BF16)
                ptr = Ring([sb("pt%d" % i, [128, 512], BF16) for i in range(3)], "pt")
                ydr = Ring([sb("yd%d" % i, [128, 512]) for i in range(1)], "yd")
                yor = Ring([sb("yo%d" % i, [128, 512], BF16) for i in range(2)], "yo")
                pmr = Ring([ps("pm%d" % i, [128, 512]) for i in range(3)], "pm")
                psr = Ring([ps("psS%d" % i, [128, 512]) for i in range(3)], "psS")
                pyr = Ring([ps("py%d" % i, [128, 512]) for i in range(2)], "py")
                pdr = pmr
                FP = 104

                def wload(c0, ncol):
                    wt, wk = wr.next()
                    s.dma("sp", wt[:, :, 0:ncol], wcat_b.ap()[l, :, c0:c0 + ncol].rearrange("(k p) n -> p k n", p=128), writes=[wk])
                    return wt, wk

                def prep(t):
                    t0 = t * 512
                    QT = QT2[t % 2]
                    hs = []
                    for c in range(8):
                        ht, hk = hr.next()
                        s.dma("sp", ht[:], hsrc_ap(c, t), writes=[hk])
                        hs.append((ht[:], hk))
                    rmsnorm((pmr, sqr, tmpr, rstdr), hs, 128, [V[:, VL1 + c:VL1 + c + 1] for c in range(8)], float(D),
                            [(xn[:, c, :], "xn") for c in range(8)])
                    yield
                    for m in range(NBL):
                        wt, wk = wload(OU + 128 * m, 128)
                        pt, pk = pmr.next()
                        s.mm([wk, "xn"], [pk], [(pt[:], wt[:, k, :], xn[:, k, :], k == 0, k == 7) for k in range(8)])
                        ut, uk = ust.next()
                        evac(ut[:].rearrange("p j c -> p c j"), pt[:].rearrange("p (c j) -> p c j", j=8), [pk], [uk])
                        s.dma("pool", UT.ap()[m, :, :, 64 * t:64 * (t + 1)], ut[:], reads=[uk])
                    yield
                    wt, wk = wload(OF, FP)
                    pt, pk = pmr.next()
                    s.mm([wk, "xn"], [pk], [(pt[0:FP, :], wt[:, k, 0:FP], xn[:, k, :], k == 0, k == 7) for k in range(8)])
                    s.I("dve", "tensor_scalar", [pk, "vecs"], ["fz"], out=fz[0:FP, :], in0=pt[0:FP, :], scalar1=V[0:FP, VFB:VFB + 1],
                        scalar2=-1.0, op0=ALU.add, op1=ALU.mult)
                    s.I("act", "activation", ["fz"], ["fz"], out=fz[0:FP, :], in_=fz[0:FP, :], func=AF.Exp)
                    s.I("act", "activation", ["fz", "cst"], ["fz"], out=fz[0:FP, :], in_=fz[0:FP, :], func=AF.Ln, bias=onec[0:FP, :])
                    Xc, Xp = fX[t % 2], fX[(t + 1) % 2]
                    init = 0.0 if t == 0 else Xp[0:FP, 511:512]
                    s.I("dve", "tensor_tensor_scan", ["fz", "onesr", ("fX", (t + 1) % 2)], [("fX", t % 2)], out=Xc[0:FP, :],
                        data0=onesr[0:FP, :], data1=fz[0:FP, :], initial=init, op0=ALU.mult, op1=ALU.add)
                    xk = ("fX", t % 2)
                    s.I("dve", "tensor_scalar", [xk], ["fb1"], out=fb1[0:FP, :], in0=Xc[0:FP, :], scalar1=8.0, scalar2=None, op0=ALU.mult)
                    s.I("dve", "scalar_tensor_tensor", [xk, "fb1"], ["fz"], out=fr1[0:FP, :], in0=Xc[0:FP, :], scalar=8.0, in1=fb1[0:FP, :],
                        op0=ALU.mult, op1=ALU.subtract)
                    s.I("dve", "tensor_copy", ["fb1", "caug"], ["caug"], out=caug[0:8, :], in_=fb1[0:8, :])
                    s.I("dve", "tensor_copy", ["fz"], ["fb2"], out=fb2[0:FP, :], in_=fr1[0:FP, :])
                    s.I("dve", "tensor_copy", ["fb2", "caug"], ["caug"], out=caug[32:40, :], in_=fb2[32:40, :])
                    s.I("dve", "tensor_tensor", ["fz", "fb2"], ["fz"], out=fr1[0:FP, :], in0=fr1[0:FP, :], in1=fb2[0:FP, :], op=ALU.subtract)
                    s.I("dve", "tensor_copy", ["fz", "caug"], ["caug"], out=caug[64:72, :], in_=fr1[64:72, :])
                    yield
                    for h in range(NHL):
                        wt, wk = wload(OK_ + 128 * h, 128)
                        pt, pk = pmr.next()
                        s.mm([wk, "xn", "caug", "selk"], [pk],
                             [(pt[:], wt[:, k, :], xn[:, k, :], k == 0, False) for k in range(8)] +
                             [(pt[:], selk[:, h, :], caug[:], False, True)])
                        evac(KT[h][:, t0:t0 + 512], pt[:], [pk], [("KT", h, t)])
                        wt, wk = wload(OQ + 128 * h, 128)
                        pt, pk = pmr.next()
                        s.mm([wk, "xn", "caug", "selq"], [pk],
                             [(pt[:], wt[:, k, :], xn[:, k, :], k == 0, False) for k in range(8)] +
                             [(pt[:], selq[:, h, :], caug[:], False, True)])
                        evac(QT[h][:], pt[:], [pk], [("QT", t % 2, h)])
                        yield
                    for i in range(4):
                        pt, pk = pmr.next()
                        s.mm(["wv", "xn"], [pk], [(pt[:, 0:256], xn[:, k, 128 * i:128 * (i + 1)], wv[:, k, :], k == 0, k == 7) for k in range(8)])
                        vv = Vr[:, 4 * t + i, :].rearrange("p (a c) -> p a c", c=192)
                        pv = pt[:, 0:256].rearrange("p (a e d) -> p a e d", e=2, d=64)
                        evac(vv[:, :, 0:64], pv[:, :, 0, :], [pk], [("Vr", t)])
                        evac(vv[:, :, 128:192], pv[:, :, 1, :], [pk], [("Vr", t)])
                    yield

                def att(t):
                    t0 = t * 512
                    QT = QT2[t % 2]
                    nkb = 4 * (t + 1)
                    blocks = [(h, kb) for h in range(NHL) for kb in range(nkb)]
                    LA = 2
                    Sinfo = {}
                    hstate = {}

                    def emit_S(i):
                        h, kb = blocks[i]
                        j = kb - 4 * t
                        q0 = 128 * j if j > 0 else 0
                        pt, pk = psr.next()
                        s.mm([("KT", h, kb // 4), ("QT", t % 2, h)], [pk], [(pt[:, q0:512], KT[h][:, 128 * kb:128 * (kb + 1)], QT[h][:, q0:512], True, True)])
                        Sinfo[i] = (pt, pk, q0, j)

                    def emit_rest(i):
                        h, kb = blocks[i]
                        pt, pk, q0, j = Sinfo.pop(i)
                        par = h % 2
                        vb0 = 192 * (h // 2) + 64 * par
                        if kb == 0:
                            hstate[h] = pyr.next()
                        py, pyk = hstate[h]
                        pb_, pbk = ptr.next()
                        s.I("act", "activation", [pk], [pbk], out=pb_[:, q0:512], in_=pt[:, q0:512], func=AF.Exp, scale=0.125)
                        if j >= 0:
                            s.I("dve", "tensor_tensor", [pbk, "trib"], [pbk], out=pb_[:, q0:q0 + 128], in0=pb_[:, q0:q0 + 128],
                                in1=trib[:], op=ALU.mult)
                        s.mm([pbk, ("Vr", kb // 4), "Vr"], [pyk], [(py[:, q0:512], Vr[:, kb, vb0:vb0 + 128], pb_[:, q0:512], kb == 0, kb == nkb - 1)])
                        if kb != nkb - 1:
                            return
                        ysl = slice(0, 64) if par == 0 else slice(64, 128)
                        dsl = slice(64, 128) if par == 0 else slice(0, 64)
                        s.I("dve", "tensor_copy", [pyk], [("dsb", par)], out=dsb[par][dsl, :], in_=py[dsl, :])
                        pd, pdk = pdr.next()
                        s.mm([("dsb", par), "cst"], [pdk], [(pd[:], selEO[par], dsb[par][:], True, True)])
                        yd, ydk = ydr.next()
                        s.I("dve", "reciprocal", [pdk], [ydk], out=yd[ysl, :], in_=pd[ysl, :])
                        if par == 0:
                            hstate["yo"] = yor.next()
                        yo, yok = hstate["yo"]
                        s.I("dve", "tensor_tensor", [pyk, ydk], [yok], out=yo[ysl, :], in0=py[ysl, :], in1=yd[ysl, :], op=ALU.mult)
                        if par == 1:
                            s.dma("pool", SND.ap()[t // 4, t % 4, 256 + 64 * (h - 1):256 + 64 * (h + 1), :], yo[:], reads=[yok])

                    for i in range(len(blocks) + LA):
                        if i < len(blocks):
                            emit_S(i)
                        if i - LA >= 0:
                            emit_rest(i - LA)
                        if i % 2 == 1:
                            yield

                bgc = [0]

                def bg_gen():
                    for (src_, dst_, shape) in bg_pieces():
                        t32, k32 = bstg.next()
                        tb, kb_ = bstb.next()
                        n = int(np.prod(shape[1:]))
                        if len(shape) == 3:
                            v32 = t32[:, 0:n].rearrange("p (a b) -> p a b", a=shape[1])
                            vb = tb[:, 0:n].rearrange("p (a b) -> p a b", a=shape[1])
                        else:
                            v32 = t32[:, 0:n]
                            vb = tb[:, 0:n]
                        s.dma("sp", v32, src_, writes=[k32])
                        bgc[0] += 1
                        if bgc[0] % 2 == 0:
                            s.I("dve", "tensor_copy", [k32], [kb_], out=tb[:, 0:n], in_=t32[:, 0:n])
                        else:
                            s.I("act", "activation", [k32], [kb_], out=tb[:, 0:n], in_=t32[:, 0:n], func=AF.Copy)
                        s.dma("pool", dst_, vb, reads=[kb_])
                        yield

                def alternate(*gs):
                    gens = [g for g in gs if g is not None]
                    while gens:
                        for g in list(gens):
                            try:
                                next(g)
                            except StopIteration:
                                gens.remove(g)

                import itertools
                bg = bg_gen() if (l == 0 and not os.environ.get("KNOBG")) else None
                pmr_bg[0] = pmr
                alternate(prep(0))
                for t in range(NG):
                    nb = 7 if t < NG - 1 else 1000
                    if os.environ.get("KSEQ"):
                        alternate(att(t))
                        alternate(prep(t + 1) if t + 1 < NG else None)
                    else:
                        alternate(att(t), prep(t + 1) if t + 1 < NG else None, itertools.islice(bg, nb) if bg is not None else None,
                                  itertools.islice(sgen, 14 if t < NG - 1 else 100000) if sgen is not None else None)
                s.flush()
            evac_mode[0] = "alt"
            if os.environ.get("KSTOP") == "1":
                return nc

            with contextlib.ExitStack() as st:
                sb, ps = mk(st)
                pmr = Ring([ps("pm%d" % i, [128, 512]) for i in range(4)], "pm")
                ptb = Ring([ps("ptb%d" % i, [128, 4, 128], BF16) for i in range(2)], "ptb")
                if l == 0:
                    pmr_bg[0] = pmr
                    for _ in make_setup():
                        pass
                Pre, Pim, S_ = SS["Pre"], SS["Pim"], SS["S_"]
                Ebd, Fbd, Klag, A1t, A2t = SS["Ebd"], SS["Fbd"], SS["Klag"], SS["A1t"], SS["A2t"]
                SK = ["setup"]
                DS = sb("DS", [128, 2, NPL, 513])
                s.I("dve", "memset", [], ["DS"], DS[:, :, :, 0:1], 0.0)
                ubr = Ring([sb("ub%d" % i, [128, 8, 512], BF16) for i in range(2)], "ub")
                with contextlib.ExitStack() as st2:
                    sb2, _ = mk(st2)
                    Epad = sb2("Epad", [128, 4, 2, 8, 128], BF16)
                    Wbp = sb2("Wbp", [128, 4, 2, 8, 128], BF16)
                    s.I("pool", "memset", [], ["Epad"], Epad[:], 0.0)
                    for b in range(NBL):
                        ub, ubk = ubr.next()
                        s.dma("sp", ub[:], UT.ap()[b], writes=[ubk])
                        for pb in range(4):
                            s.I("pool", "tensor_copy", SK + ["Epad"], ["Epad"], out=Epad[:, pb, :, :, 32 * pb:32 * pb + 32],
                                in_=Ebd[:, :, :, 4 * b + pb, :, :].rearrange("p r j g q -> p r j (g q)"))
                        for pb in range(4):
                            for ri in range(2):
                                for jj in range(0, 8, 4):
                                    tp, tpk = ptb.next()

                                    def fn(e, tp=tp, pb=pb, ri=ri, jj=jj):
                                        ins = None
                                        for x in range(4):
                                            ins = e.transpose(tp[:, x, :], Epad[:, pb, ri, jj + x, :], identb[:])
                                        return ins
                                    s.op("pe", fn, ["Epad", "identb"], [tpk])
                                    evac(Wbp[:, pb, ri, jj:jj + 4, :], tp[:], [tpk], ["Wbp"])
                        for pb in range(4):
                            for ri in range(2):
                                pt, pk = pmr.next()
                                s.mm(["Wbp", ubk], [pk], [(pt[:], Wbp[:, pb, ri, j, :], ub[:, j, :], j == 0, j == 7) for j in range(8)])
                                evac(DS[:, ri, 4 * b + pb, 1:513], pt[:], [pk], ["DS"])
                s.barrier()
                with contextlib.ExitStack() as st2:
                    sb2, _ = mk(st2)
                    ct1 = sb2("ct1", [128, 2, NPL, 32])
                    ct2 = sb2("ct2", [128, 2, NPL, 32])
                    tabr = sb2("tabr", [128, NPL, 32])
                    tabi = sb2("tabi", [128, NPL, 32])
                    C1 = sb2("C1", [128, 2, NPL, 32])
                    C2 = sb2("C2", [128, 2, NPL, 32])
                    B1 = sb2("B1", [128, 2, NPL])
                    B2 = sb2("B2", [128, 2, NPL])
                    DSv = DS[:, :, :, 1:513].rearrange("p r a (s i) -> p r a s i", i=32)
                    CK = ["DS", "setup", "ct1"]

                    def dv(meth, **kw):
                        s.I("dve", meth, CK, CK, **kw)

                    def cstep(cur, prev, c1, c2, n):
                        s.I("dve", "tensor_tensor", ["DS", "setup", "ct1"], ["ct1"], out=ct1[:, :, :, 0:n], in0=c1, in1=prev, op=ALU.mult)
                        s.I("dve", "tensor_tensor", ["DS", "setup", "ct2"], ["ct2"], out=ct2[:, 0, :, 0:n], in0=c2[:, 0], in1=prev[:, 1], op=ALU.mult)
                        s.I("dve", "tensor_tensor", ["DS", "setup", "ct2"], ["ct2"], out=ct2[:, 1, :, 0:n], in0=c2[:, 1], in1=prev[:, 0], op=ALU.mult)
                        s.I("dve", "tensor_tensor", ["DS", "ct1"], ["DS"], out=cur, in0=cur, in1=ct1[:, :, :, 0:n], op=ALU.add)
                        s.I("dve", "tensor_tensor", ["DS", "ct2"], ["DS"], out=cur, in0=cur, in1=ct2[:, :, :, 0:n], op=ALU.add)

                    A1b = A1t[:].unsqueeze(3).to_broadcast([128, 2, NPL, 16])
                    A2b = A2t[:].unsqueeze(3).to_broadcast([128, 2, NPL, 16])
                    for i in range(1, 32):
                        cstep(DSv[:, :, :, :, i], DSv[:, :, :, :, i - 1], A1b, A2b, 16)
                    pwr, pwi, q1, q2, q3 = [S_(40 + i) for i in range(5)]
                    dv("tensor_copy", out=pwr, in_=Pre[8])
                    dv("tensor_copy", out=pwi, in_=Pim[8])
                    dv("tensor_copy", out=tabr[:, :, 0], in_=Pre[8])
                    dv("tensor_copy", out=tabi[:, :, 0], in_=Pim[8])
                    w = 1
                    m1t = ct1[:, 0, :, :]
                    m2t = ct1[:, 1, :, :]
                    while w < 32:
                        pb_r = pwr.unsqueeze(2).to_broadcast([128, NPL, w])
                        pb_i = pwi.unsqueeze(2).to_broadcast([128, NPL, w])
                        dv("tensor_tensor", out=m1t[:, :, 0:w], in0=tabr[:, :, 0:w], in1=pb_r, op=ALU.mult)
                        dv("tensor_tensor", out=m2t[:, :, 0:w], in0=tabi[:, :, 0:w], in1=pb_i, op=ALU.mult)
                        dv("tensor_tensor", out=tabr[:, :, w:2 * w], in0=m1t[:, :, 0:w], in1=m2t[:, :, 0:w], op=ALU.subtract)
                        dv("tensor_tensor", out=m1t[:, :, 0:w], in0=tabr[:, :, 0:w], in1=pb_i, op=ALU.mult)
                        dv("tensor_tensor", out=m2t[:, :, 0:w], in0=tabi[:, :, 0:w], in1=pb_r, op=ALU.mult)
                        dv("tensor_tensor", out=tabi[:, :, w:2 * w], in0=m1t[:, :, 0:w], in1=m2t[:, :, 0:w], op=ALU.add)
                        dv("tensor_tensor", out=q1, in0=pwr, in1=pwr, op=ALU.mult)
                        dv("tensor_tensor", out=q2, in0=pwi, in1=pwi, op=ALU.mult)
                        dv("tensor_tensor", out=q3, in0=pwr, in1=pwi, op=ALU.mult)
                        dv("tensor_tensor", out=pwr, in0=q1, in1=q2, op=ALU.subtract)
                        dv("tensor_scalar", out=pwi, in0=q3, scalar1=2.0, scalar2=None, op0=ALU.mult)
                        w *= 2
                    dv("tensor_copy", out=B1[:, 0, :], in_=pwr)
                    dv("tensor_copy", out=B1[:, 1, :], in_=pwr)
                    dv("tensor_scalar", out=B2[:, 0, :], in0=pwi, scalar1=-1.0, scalar2=None, op0=ALU.mult)
                    dv("tensor_copy", out=B2[:, 1, :], in_=pwi)
                    dv("tensor_copy", out=C1[:, 0], in_=tabr[:])
                    dv("tensor_copy", out=C1[:, 1], in_=tabr[:])
                    dv("tensor_scalar", out=C2[:, 0], in0=tabi[:], scalar1=-1.0, scalar2=None, op0=ALU.mult)
                    dv("tensor_copy", out=C2[:, 1], in_=tabi[:])
                    B1b = B1[:].unsqueeze(3).to_broadcast([128, 2, NPL, 1])
                    B2b = B2[:].unsqueeze(3).to_broadcast([128, 2, NPL, 1])
                    for sg in range(1, 16):
                        cstep(DSv[:, :, :, sg, 31:32], DSv[:, :, :, sg - 1, 31:32], B1b, B2b, 1)
                    for sg in range(1, 16):
                        tp_ = DSv[:, :, :, sg - 1, 31:32].to_broadcast([128, 2, NPL, 31])
                        cstep(DSv[:, :, :, sg, 0:31], tp_, C1[:, :, :, 0:31], C2[:, :, :, 0:31], 31)
                s.barrier()
                with contextlib.ExitStack() as st2:
                    sb2, _ = mk(st2)
                    Fpad = sb2("Fpad", [128, 4, 2, 8, 128], BF16)
                    Sbf = sb2("Sbf", [128, 2, 4, 512], BF16)
                    gnat = Ring([sb2("gnat%d" % i, [128, 4096], BF16) for i in range(2)], "gnat")
                    s.I("dve", "memset", [], ["Fpad"], Fpad[:], 0.0)
                    for b in range(NBL):
                        ub, ubk = ubr.next()
                        s.dma("sp", ub[:], UT.ap()[b], writes=[ubk])
                        for pb in range(4):
                            s.I("dve", "tensor_copy", SK + ["Fpad"], ["Fpad"], out=Fpad[:, pb, :, :, 32 * pb:32 * pb + 32],
                                in_=Fbd[:, :, :, 4 * b + pb, :, :].rearrange("p r j g q -> p r j (g q)"))
                        for ri in range(2):
                            s.I("dve", "tensor_copy", ["DS", "Sbf"], ["Sbf"], out=Sbf[:, ri, :, :], in_=DS[:, ri, 4 * b:4 * b + 4, 0:512])
                        gn, gnk = gnat.next()
                        gv = gn[:].rearrange("p (c j) -> p j c", j=8)
                        for j in range(8):
                            pt, pk = pmr.next()
                            items = [(pt[:], Klag[:, b, d, :], ub[:, j - d, :], d == 0, False) for d in range(j + 1)]
                            items += [(pt[:], Fpad[:, pb, ri, j, :], Sbf[:, ri, pb, :], False, (pb == 3 and ri == 1))
                                      for pb in range(4) for ri in range(2)]
                            s.mm(["Klag", ubk, "Fpad", "Sbf"], [pk], items)
                            s.I("act", "activation", [pk, gnk], [gnk], out=gv[:, j, :], in_=pt[:], func=AF.Gelu_apprx_tanh)
                        for g8 in range(8):
                            s.dma("pool", SND.ap()[g8 // 4, g8 % 4, 128 * b:128 * (b + 1), :], gn[:, 512 * g8:512 * (g8 + 1)], reads=[gnk])
                if os.environ.get("KSTOP") == "2":
                    s.flush()
                    return nc
                for hf in range(2):
                    s.collective(SND.ap()[hf].rearrange("b r c -> (b r) c"), GAT.ap()[4096 * hf:4096 * (hf + 1), :])
                if os.environ.get("KSTOP") == "3":
                    return nc

            lst.close()
            with contextlib.ExitStack() as st:
                sb, ps = mk(st)
                pmr = Ring([ps("pm%d" % i, [128, 512]) for i in range(8)], "pm")
                gtl = sb("gtl", [128, 4, 512], BF16)
                gb = gtl
                sgr = Ring([sb("sg%d" % i, [128, 512]) for i in range(2)], "sg")
                osm = sb("osm", [128, 4, 512])
                sqr = Ring([sb("sq%d" % i, [128, 512], BF16) for i in range(3)], "sq")
                tmpr = Ring([sb("tm%d" % i, [128, 512]) for i in range(1)], "tm")
                rstdr = Ring([sb("rs%d" % i, [128, 512]) for i in range(2)], "rs")
                mixs = sb("mixs", [128, 4, 512], BF16)
                mixa = sb("mixa", [128, 4, 512], BF16)
                yar = Ring([sb("ya%d" % i, [128, 512], BF16) for i in range(4)], "ya")
                hbr = Ring([sb("hb%d" % i, [128, 512], BF16) for i in range(2)], "hb")
                hr = Ring([sb("hr%d" % i, [128, 512]) for i in range(4)], "hr")
                wglu = sb("wglu", [128, 4, 512], BF16)
                s.dma("sp", wglu[:], glu_b.ap()[l].rearrange("(k p) n -> p k n", p=128), writes=["wglu"])
                wos = Ring([sb("wos%d" % i, [128, 8, 128], BF16) for i in range(2)], "wos")
                h1T = sb("h1T", [128, 32, 512], BF16)
                wupr = Ring([sb("wup%d" % i, [128, 8, 512], BF16) for i in range(2)], "wup")
                wdnr = Ring([sb("wdn%d" % i, [128, 32, 128], BF16) for i in range(2)], "wdn")
                rlr = Ring([sb("rl%d" % i, [128, 512]) for i in range(2)], "rl")
                ofr = Ring([sb("of%d" % i, [128, 512]) for i in range(2)], "of")
                last = (l == nlayers - 1)
                hmid2 = [sb("hmidb%d" % i, [128, 8, 512]) for i in range(2)]
                xn22 = [sb("xn2b%d" % i, [128, 8, 512], BF16) for i in range(2)]

                def front(t):
                    t0 = t * 512
                    pz = t % 2
                    hmid, xn2 = hmid2[pz], xn22[pz]
                    for oc in range(4):
                        s.gather(gtl[:, oc, :], GAT.ap(), gix[:, 8 * t + oc:8 * t + oc + 1], reads=["gix"], writes=["gtl"])
                    yas = []
                    for h in range(4):
                        ya, yak = yar.next()
                        s.gather(ya[:], GAT.ap(), gix[:, 8 * t + 4 + h:8 * t + 4 + h + 1], reads=["gix"], writes=[yak])
                        yas.append((ya[:], yak))
                    yield
                    for oc in range(4):
                        pt, pk = pmr.next()
                        s.mm(["wglu", "gtl"], [pk], [(pt[:], wglu[:, k, 128 * oc:128 * (oc + 1)], gb[:, k, :], k == 0, k == 3) for k in range(4)])
                        sg, sgk = sgr.next()
                        s.I("act", "activation", [pk, "vecs"], [sgk], out=sg[:], in_=pt[:], func=AF.Sigmoid, bias=V[:, VGB + oc:VGB + oc + 1])
                        s.I("pool", "tensor_tensor", [sgk, "gtl"], [("osm", oc)], out=osm[:, oc, :], in0=gtl[:, oc, :], in1=sg[:], op=ALU.mult)
                        if oc % 2 == 1:
                            yield
                    rmsnorm((pmr, sqr, tmpr, rstdr), [(osm[:, oc, :], ("osm", oc)) for oc in range(4)], 128,
                            [V[:, VGS + oc:VGS + oc + 1] for oc in range(4)], 512.0, [(mixs[:, oc, :], "mixs") for oc in range(4)])
                    yield
                    rmsnorm((pmr, sqr, tmpr, rstdr), yas, 128, [V[:, VGA + h:VGA + h + 1] for h in range(4)], 512.0,
                            [(mixa[:, h, :], "mixa") for h in range(4)])
                    yield
                    for oc in range(8):
                        ws, wsk = wos.next()
                        s.dma("sp", ws[:], w_out_b.ap()[l, :, 128 * oc:128 * (oc + 1)].rearrange("(k p) n -> p k n", p=128), writes=[wsk])
                        ht, hk = hr.next()
                        s.dma("sp", ht[:], hown.ap()[128 * oc:128 * (oc + 1), t0:t0 + 512], writes=[hk])
                        pt, pk = pmr.next()
                        s.mm([wsk, "mixs", "mixa"], [pk],
                             [(pt[:], ws[:, k, :], mixs[:, k, :], k == 0, False) for k in range(4)] +
                             [(pt[:], ws[:, 4 + h, :], mixa[:, h, :], False, h == 3) for h in range(4)])
                        s.I("dve", "tensor_tensor", [pk, hk], [("hmid", pz, oc)], out=hmid[:, oc, :], in0=pt[:], in1=ht[:], op=ALU.add)
                        if oc % 2 == 1:
                            yield
                    rmsnorm((pmr, sqr, tmpr, rstdr), [(hmid[:, c, :], ("hmid", pz, c)) for c in range(8)], 128,
                            [V[:, VL2 + c:VL2 + c + 1] for c in range(8)], float(D), [(xn2[:, c, :], ("xn2", pz)) for c in range(8)])
                    yield

                def ffn(t):
                    t0 = t * 512
                    pz = t % 2
                    hmid, xn2 = hmid2[pz], xn22[pz]
                    for f4 in range(8):
                        wu, wuk = wupr.next()
                        s.dma("sp", wu[:], w_up_b.ap()[l, :, 512 * f4:512 * (f4 + 1)].rearrange("(k p) n -> p k n", p=128), writes=[wuk])
                        for fi in range(4):
                            f = 4 * f4 + fi
                            pt, pk = pmr.next()
                            s.mm([wuk, ("xn2", pz)], [pk], [(pt[:], wu[:, k, 128 * fi:128 * (fi + 1)], xn2[:, k, :], k == 0, k == 7) for k in range(8)])
                            rl, rlk = rlr.next()
                            s.I("act", "activation", [pk], [rlk], out=rl[:], in_=pt[:], func=AF.Relu)
                            s.I(("dve", "pool")[f % 2], "tensor_tensor", [rlk], [("h1T", f)], out=h1T[:, f, :], in0=rl[:], in1=rl[:], op=ALU.mult)
                        yield
                    for oc in range(8):
                        wd, wdk = wdnr.next()
                        s.dma("sp", wd[:], w_dn_b.ap()[l, :, 128 * oc:128 * (oc + 1)].rearrange("(f p) n -> p f n", p=128), writes=[wdk])
                        pt, pk = pmr.next()
                        s.mm([wdk] + [("h1T", f) for f in range(32)], [pk], [(pt[:], wd[:, f, :], h1T[:, f, :], f == 0, f == 31) for f in range(32)])
                        s.I("dve", "tensor_tensor", [pk, ("hmid", pz, oc)], [("hmid", pz, oc)], out=hmid[:, oc, :], in0=pt[:], in1=hmid[:, oc, :], op=ALU.add)
                        if not last:
                            s.dma("pool", h1own.ap()[128 * oc:128 * (oc + 1), t0:t0 + 512], hmid[:, oc, :], reads=[("hmid", pz, oc)])
                            hb, hbk = hbr.next()
                            s.I("pool", "tensor_copy", [("hmid", pz, oc)], [hbk], out=hb[:], in_=hmid[:, oc, :])
                            s.dma("pool", h1ownb.ap()[128 * oc:128 * (oc + 1), t0:t0 + 512], hb[:], reads=[hbk])
                        yield
                    if last:
                        pt, pk = pmr.next()
                        for c in range(8):
                            sq, sk = sqr.next()
                            s.I("act", "activation", [("hmid", pz, c)], [sk], out=sq[:], in_=hmid[:, c, :], func=AF.Square)
                            s.mm([sk, "onesb"], [pk], [(pt[:], onesb[:], sq[:], c == 0, c == 7)])
                        tm, tk = tmpr.next()
                        s.I("act", "activation", [pk, "cst"], [tk], out=tm[:], in_=pt[:], func=AF.Ln, scale=1.0 / D, bias=epsc)
                        rs, rk = rstdr.next()
                        s.I("act", "activation", [tk], [rk], out=rs[:], in_=tm[:], func=AF.Exp, scale=-0.5)
                        for c in range(8):
                            of_, ofk = ofr.next()
                            s.I("dve", "scalar_tensor_tensor", [("hmid", pz, c), rk, "vecs"], [ofk], out=of_[:], in0=hmid[:, c, :], scalar=fg[:, c:c + 1],
                                in1=rs[:], op0=ALU.mult, op1=ALU.mult)
                            s.dma("pool", outT.ap()[128 * c:128 * (c + 1), t0:t0 + 512], of_[:], reads=[ofk])
                        yield

                def alternate(g1, g2):
                    gens = [g for g in (g1, g2) if g is not None]
                    while gens:
                        for g in list(gens):
                            try:
                                next(g)
                            except StopIteration:
                                gens.remove(g)

                alternate(front(0), None)
                for t in range(4):
                    alternate(ffn(t), front(t + 1) if t + 1 < 4 else None)
                s.flush()
            if not last:
                for j in range(2):
                    s.collective(h1ownb.ap()[512 * j:512 * (j + 1), :], H1.ap()[1024 * j:1024 * (j + 1), :])
    return nc


def _consts():
    c = np.zeros((128, NCONST), np.float32)
    c[:, C_ID:C_ID + 128] = np.eye(128, dtype=np.float32)
    r = np.arange(128)
    c[:, C_PM:C_PM + 128] = (r[:, None] // 32 == r[None, :] // 32).astype(np.float32)
    c[:, C_TRI:C_TRI + 128] = (r[None, :] >= r[:, None]).astype(np.float32)
    selk = np.zeros((128, 8, 128), np.float32)
    selq = np.zeros((128, 8, 128), np.float32)
    c[64, C_SE:C_SE + 64] = 1.0
    c[0, C_SO + 64:C_SO + 128] = 1.0
    for h in range(8):
        selk[96, h, 64:67] = 1.0
        selk[h, h, 67] = 1.0
        selk[32 + h, h, 68] = 1.0
        selk[64 + h, h, 69] = 1.0
        selq[h, h, 64] = -1.0
        selq[32 + h, h, 65] = -1.0
        selq[64 + h, h, 66] = -1.0
        selq[96, h, 67:70] = 1.0
    c[:, C_SELK:C_SELK + 1024] = selk.reshape(128, 1024)
    c[:, C_SELQ:C_SELQ + 1024] = selq.reshape(128, 1024)
    c[:64, C_MISC + 0] = 1.0
    c[64:, C_MISC + 1] = 1.0
    c[:, C_MISC + 2] = EPS
    c[:, C_MISC + 3] = 1.0
    return c


def _prep(inp, r):
    f = lambda k: np.asarray(inp[k], dtype=np.float32)
    w_in = f("w_in")
    wcat = np.zeros((DEPTH, D, NCAT), np.float32)
    vecs = np.zeros((DEPTH, 128, NV), np.float32)
    ssm_small = np.zeros((DEPTH, 128, 24), np.float32)
    ssm_big = np.zeros((DEPTH, 128, 4, NPL, 16), np.float32)
    col = lambda v, n: np.ascontiguousarray(v.reshape(n, 128).T)
    for l in range(DEPTH):
        wcat[l, :, OU:OU + 256] = w_in[l, :, 256 * r:256 * (r + 1)]
        for hl in range(NHL):
            h = NHL * r + hl
            wcat[l, :, OQ + 128 * hl:OQ + 128 * hl + 64] = w_in[l, :, 512 + 64 * h:512 + 64 * (h + 1)]
            wcat[l, :, OK_ + 128 * hl:OK_ + 128 * hl + 64] = w_in[l, :, 1024 + 64 * h:1024 + 64 * (h + 1)]
        wcat[l, :, OV:OV + 256] = w_in[l, :, 1536 + 256 * r:1536 + 256 * (r + 1)]
        for base in (0, 32, 64, 96):
            wcat[l, :, OF + base:OF + base + NHL] = w_in[l, :, 2048 + NHL * r:2048 + NHL * (r + 1)]
            vecs[l, base:base + NHL, VFB] = f("fgate_b")[l][NHL * r:NHL * (r + 1)]
        vecs[l, :, VL1:VL1 + 8] = col(f("ln1_g")[l], 8)
        vecs[l, :, VL2:VL2 + 8] = col(f("ln2_g")[l], 8)
        vecs[l, :, VGS:VGS + 4] = col(f("gn_ssm_g")[l], 4)
        vecs[l, :, VGB:VGB + 4] = col(f("glu_b")[l], 4)
        vecs[l, :, VDS:VDS + 2] = col(f("ssm_d")[l][256 * r:256 * (r + 1)], 2)
        vecs[l, :, VGA:VGA + 4] = col(f("gn_attn_g")[l], 4)
        gs = slice(16 * r, 16 * (r + 1))
        ldt = f("ssm_log_dt")[l][gs].reshape(NPL, 2)
        ssm_small[l, :, 0:8] = np.repeat(ldt.T[:, None, :], 64, axis=1).reshape(128, NPL)
        for i, k in enumerate(("ssm_lambda_re", "ssm_lambda_im")):
            a = f(k)[l][gs].reshape(NPL, 2, 64)
            ssm_small[l, :, 8 * (i + 1):8 * (i + 2)] = a.transpose(1, 2, 0).reshape(128, NPL)
        for i, k in enumerate(("ssm_b_re", "ssm_b_im")):
            a = f(k)[l][gs].reshape(NPL, 2, 64, 16)
            ssm_big[l, :, i] = a.transpose(1, 2, 0, 3).reshape(128, NPL, 16)
        for i, k in enumerate(("ssm_c_re", "ssm_c_im")):
            a = f(k)[l][gs].reshape(NPL, 2, 16, 64)
            ssm_big[l, :, 2 + i] = a.transpose(1, 3, 0, 2).reshape(128, NPL, 16)
    gidx = np.zeros((128, 32), np.uint32)
    p = np.arange(128)
    for tl in range(4):
        for kind in range(2):
            for rk in range(2):
                for sub in range(2):
                    base = r * 4096 + rk * 2048 + tl * 512
                    gidx[:, 8 * tl + 4 * kind + 2 * rk + sub] = base + 256 * kind + 128 * sub + p
    return dict(wcat=wcat, vecs=vecs, ssm_small=ssm_small, ssm_big=ssm_big.reshape(DEPTH, 128, 512), gidx=gidx)


_NC_CACHE = {}


def kernel(**inputs):
    x = np.asarray(inputs["x"], dtype=np.float32)
    f = lambda k: np.asarray(inputs[k], dtype=np.float32)
    col = lambda v, n: np.ascontiguousarray(v.reshape(n, 128).T)
    shared = dict(w_out=f("w_out"), glu_w=f("glu_w"), w_up=f("w_up"), w_down=f("w_down"),
                  fing=col(f("final_g"), 8), consts=_consts())
    per_rank = [_prep(inputs, r) for r in range(2)]
    if "nc" not in _NC_CACHE:
        _NC_CACHE["nc"] = build()
    nc = _NC_CACHE["nc"]
    in_maps = []
    for i in range(8):
        b, r = i // 2, i % 2
        m = dict(shared)
        m.update(per_rank[r])
        xt = np.ascontiguousarray(x[b].T)
        m["xT"] = xt
        m["xTo"] = np.ascontiguousarray(xt[:, LH * r:LH * (r + 1)])
        in_maps.append(m)
    res = run_bass_kernel_spmd(nc, in_maps, core_ids=list(range(8)))
    out = np.zeros((4, L, D), np.float32)
    for i in range(8):
        b, r = i // 2, i % 2
        out[b, LH * r:LH * (r + 1), :] = res.results[i]["outT"].T
    return out
```

```python
import contextlib
import os
import numpy as np
import concourse.bass as bass
import concourse.mybir as mybir
from concourse.bass_utils import run_bass_kernel_spmd

F32 = mybir.dt.float32
BF16 = mybir.dt.bfloat16
AF = mybir.ActivationFunctionType
ALU = mybir.AluOpType

ENGS = ("pe", "act", "dve", "pool", "sp")
L = 4096
D = 1024
NG = 8
DEPTH = 2
EPS = 1e-6
NHL, NBL, NPL = 4, 2, 8
LH = L // 2
NCAT = 256 + 512 + 512 + 256 + 104
OU, OQ, OK_, OV, OF = 0, 256, 768, 1280, 1536
U32 = mybir.dt.uint32
VL1, VL2, VGS, VGB, VFB, VDS, VGA = 0, 8, 16, 20, 24, 25, 27
NV = 31
C_ID, C_PM, C_TRI, C_SE, C_SO, C_SELK, C_SELQ, C_MISC = 0, 128, 256, 384, 512, 640, 640 + 1024, 640 + 2048
NCONST = C_MISC + 4


class Sched:
    def __init__(self, nc, st, n_dma_sems=8):
        self.nc = nc
        self.nd = n_dma_sems
        self.csem = {e: st.enter_context(nc.semaphore("c_" + e)) for e in ENGS}
        self.dsem = {(q, k): st.enter_context(nc.semaphore("d_%s%d" % (q, k)))
                     for q in ("sp", "act", "pool") for k in range(n_dma_sems)}
        self.base = {e: 0 for e in ENGS}
        self.ops = {e: [] for e in ENGS}
        self.last_w = {}
        self.readers = {}
        self.seen = {e: {} for e in ENGS}
        self.dma_uses = {}
        self.dma_rr = {e: 0 for e in ENGS}
        self.ninst = 0
        self.ccsem = st.enter_context(nc.semaphore("ccsem"))
        self.ncc = 0

    def collective(self, in_ap, out_ap):
        self.flush()
        self.ncc += 1
        n = self.ncc
        self.ops["pool"].append(dict(kind="cc", waits=[], inc=False, fn=lambda e: e.collective_compute(
            "AllGather", ALU.bypass, replica_groups=[[0, 1], [2, 3], [4, 5], [6, 7]], ins=[in_ap], outs=[out_ap])))
        for e in ENGS:
            self.ops[e].append(dict(kind="wcc", waits=[], inc=False, n=n))
        self.flush()

    def _need(self, eng, ref, waits):
        if ref is None:
            return
        if ref[0] == "c":
            _, src, idx = ref
            if src == "pe" and eng == "pe":
                return
            cur = self.seen[eng].get(("c", src), -1)
            if idx > cur:
                self.seen[eng][("c", src)] = idx
                self.ops[src][idx]["inc"] = True
                waits.append(ref)
        else:
            _, q, k, val = ref
            cur = self.seen[eng].get(("d", q, k), 0)
            if val > cur:
                self.seen[eng][("d", q, k)] = val
                waits.append(ref)

    def _deps(self, eng, reads, writes):
        waits = []
        for k in reads:
            self._need(eng, self.last_w.get(k), waits)
        for k in writes:
            self._need(eng, self.last_w.get(k), waits)
            for r in self.readers.get(k, ()):
                self._need(eng, r, waits)
        return waits

    def _commit(self, ref, reads, writes):
        for k in reads:
            self.readers.setdefault(k, []).append(ref)
        for k in writes:
            self.last_w[k] = ref
            self.readers[k] = []

    def op(self, eng, fn, reads=(), writes=()):
        waits = self._deps(eng, reads, writes)
        idx = len(self.ops[eng])
        self.ops[eng].append(dict(kind="c", fn=fn, waits=waits, inc=False))
        self._commit(("c", eng, idx), reads, writes)

    def I(self, eng, meth, reads, writes, *a, **kw):
        self.op(eng, lambda e: getattr(e, meth)(*a, **kw), reads, writes)

    def mm(self, reads, writes, items):
        items = list(items)

        def fn(e):
            ins = None
            for (o, l, r, st_, sp_) in items:
                ins = e.matmul(o, l, r, start=st_, stop=sp_)
            return ins
        self.op("pe", fn, reads, writes)

    def dma(self, q, out, in_, reads=(), writes=()):
        k = self.dma_rr[q] % self.nd
        self.dma_rr[q] += 1
        uses = self.dma_uses.get((q, k), 0)
        waits = []
        if uses > 0:
            self._need(q, ("d", q, k, 16 * uses), waits)
        waits += self._deps(q, reads, writes)
        self.dma_uses[(q, k)] = uses + 1
        ref = ("d", q, k, 16 * (uses + 1))
        self.ops[q].append(dict(kind="d", fn=lambda e: e.dma_start(out=out, in_=in_), waits=waits,
                                inc=False, dsem=(q, k)))
        self._commit(ref, reads, writes)

    def gather(self, out, in_, idx, reads=(), writes=()):
        q = "pool"
        k = self.dma_rr[q] % self.nd
        self.dma_rr[q] += 1
        uses = self.dma_uses.get((q, k), 0)
        waits = []
        if uses > 0:
            self._need(q, ("d", q, k, 16 * uses), waits)
        waits += self._deps(q, reads, writes)
        self.dma_uses[(q, k)] = uses + 1
        ref = ("d", q, k, 16 * (uses + 1))
        self.ops[q].append(dict(kind="d", waits=waits, inc=False, dsem=(q, k), fn=lambda e: e.indirect_dma_start(
            out=out, out_offset=None, in_=in_, in_offset=bass.IndirectOffsetOnAxis(ap=idx, axis=0))))
        self._commit(ref, reads, writes)

    def barrier(self):
        lastc = {}
        for e in ENGS:
            for i in range(len(self.ops[e]) - 1, -1, -1):
                if self.ops[e][i]["kind"] == "c":
                    lastc[e] = i
                    break
        for e in ENGS:
            waits = []
            for src, i in lastc.items():
                self._need(e, ("c", src, i), waits)
            for (q, k), uses in self.dma_uses.items():
                self._need(e, ("d", q, k, 16 * uses), waits)
            if waits:
                self.ops[e].append(dict(kind="w", fn=None, waits=waits, inc=False))
        self.last_w.clear()
        self.readers.clear()

    def flush(self):
        self.barrier()
        nc = self.nc
        val = {}
        for e in ENGS:
            c = self.base[e]
            vals = []
            for rec in self.ops[e]:
                if rec["kind"] == "c" and rec["inc"]:
                    c += 1
                    vals.append(c)
                else:
                    vals.append(None)
            val[e] = vals
            self.base[e] = c
        ops = self.ops

        def run(e, eng):
            for rec in ops[e]:
                for w in rec["waits"]:
                    if w[0] == "c":
                        eng.wait_ge(self.csem[w[1]], val[w[1]][w[2]])
                    else:
                        eng.wait_ge(self.dsem[(w[1], w[2])], w[3])
                if rec["kind"] == "c":
                    ins = rec["fn"](eng)
                    if rec["inc"]:
                        ins.then_inc(self.csem[e], 1)
                elif rec["kind"] == "d":
                    rec["fn"](eng).then_inc(self.dsem[rec["dsem"]], 16)
                elif rec["kind"] == "cc":
                    rec["fn"](eng).then_inc(self.ccsem, 1)
                elif rec["kind"] == "wcc":
                    eng.wait_ge(self.ccsem, rec["n"])
                self.ninst += 1

        with nc.Block() as block:
            @block.tensor
            def _(eng):
                run("pe", eng)

            @block.scalar
            def _(eng):
                run("act", eng)

            @block.vector
            def _(eng):
                run("dve", eng)

            @block.gpsimd
            def _(eng):
                run("pool", eng)

            @block.sync
            def _(eng):
                run("sp", eng)
        self.ops = {e: [] for e in ENGS}
        for e in ENGS:
            for k in list(self.seen[e].keys()):
                if k[0] == "c":
                    del self.seen[e][k]


class Ring:
    def __init__(self, tiles, name):
        self.tiles = tiles
        self.name = name
        self.i = 0

    def next(self):
        k = self.i % len(self.tiles)
        self.i += 1
        return self.tiles[k], (self.name, k)


def build(debug=False, nlayers=DEPTH):
    nc = bass.Bass("TRN2", target_bir_lowering=False)
    din = lambda n, shp, dt=F32: nc.dram_tensor(n, shp, dt, kind="ExternalInput")
    xT = din("xT", [D, L])
    xTo = din("xTo", [D, LH])
    gidx_d = din("gidx", [128, 32], U32)
    wcat = din("wcat", [DEPTH, D, NCAT])
    w_out = din("w_out", [DEPTH, D, D])
    glu_w = din("glu_w", [DEPTH, 512, 512])
    w_up = din("w_up", [DEPTH, D, 4096])
    w_down = din("w_down", [DEPTH, 4096, D])
    vecs = din("vecs", [DEPTH, 128, NV])
    fing = din("fing", [128, 8])
    ssm_small = din("ssm_small", [DEPTH, 128, 24])
    ssm_big = din("ssm_big", [DEPTH, 128, 512])
    consts = din("consts", [128, NCONST])
    outT = nc.dram_tensor("outT", [D, LH], F32, kind="ExternalOutput")
    h1own = nc.dram_tensor("h1own", [D, LH], F32, kind="Internal")
    h1ownb = nc.dram_tensor("h1ownb", [D, LH], BF16, kind="Internal")
    H1 = nc.dram_tensor("H1", [2 * D, LH], BF16, kind="Internal")
    UT = nc.dram_tensor("UT", [NBL, 128, 8, 512], BF16, kind="Internal")
    SND = nc.dram_tensor("SND", [2, 4, 512, 512], BF16, kind="Internal")
    GAT = nc.dram_tensor("GAT", [2 * 2 * 4 * 512, 512], BF16, kind="Internal")
    wcat_b = nc.dram_tensor("wcat_b", [DEPTH, D, NCAT], BF16, kind="Internal")
    w_out_b = nc.dram_tensor("w_out_b", [DEPTH, D, D], BF16, kind="Internal")
    glu_b = nc.dram_tensor("glu_wb", [DEPTH, 512, 512], BF16, kind="Internal")
    w_up_b = nc.dram_tensor("w_up_b", [DEPTH, D, 4096], BF16, kind="Internal")
    w_dn_b = nc.dram_tensor("w_dn_b", [DEPTH, 4096, D], BF16, kind="Internal")

    top = contextlib.ExitStack()
    with top:
        s = Sched(nc, top)

        uid = [0]

        def mk(st):
            uid[0] += 1
            pre = "u%d_" % uid[0]

            def sb(name, shape, dt=F32):
                return st.enter_context(nc.sbuf_tensor(pre + name, shape, dt))

            def ps(name, shape, dt=F32):
                return st.enter_context(nc.psum_tensor(pre + name, shape, dt))
            return sb, ps

        gsb, _ = mk(top)
        cst = gsb("cst", [128, NCONST])
        s.dma("sp", cst[:], consts.ap(), writes=["cst"])
        identb = gsb("identb", [128, 128], BF16)
        trib = gsb("trib", [128, 128], BF16)
        selk = gsb("selk", [128, 8, 128], BF16)
        selq = gsb("selq", [128, 8, 128], BF16)
        s.I("dve", "tensor_copy", ["cst"], ["identb"], out=identb[:], in_=cst[:, C_ID:C_ID + 128])
        s.I("dve", "tensor_copy", ["cst"], ["trib"], out=trib[:], in_=cst[:, C_TRI:C_TRI + 128])
        s.I("dve", "tensor_copy", ["cst"], ["selk"], out=selk[:].rearrange("p h c -> p (h c)"), in_=cst[:, C_SELK:C_SELK + 1024])
        s.I("dve", "tensor_copy", ["cst"], ["selq"], out=selq[:].rearrange("p h c -> p (h c)"), in_=cst[:, C_SELQ:C_SELQ + 1024])
        identf = cst[:, C_ID:C_ID + 128]
        selEO = [cst[:, C_SE:C_SE + 128], cst[:, C_SO:C_SO + 128]]
        pmf = cst[:, C_PM:C_PM + 128]
        mkc = [cst[:, C_MISC + 0:C_MISC + 1], cst[:, C_MISC + 1:C_MISC + 2]]
        epsc = cst[:, C_MISC + 2:C_MISC + 3]
        onec = cst[:, C_MISC + 3:C_MISC + 4]
        onesf = gsb("onesf", [128, 128])
        s.I("dve", "memset", [], ["onesf"], onesf[:], 1.0)
        onesb = gsb("onesb", [128, 128], BF16)
        s.I("dve", "memset", [], ["onesb"], onesb[:], 1.0)
        vec = [gsb("vec%d" % l, [128, NV]) for l in range(DEPTH)]
        for l in range(DEPTH):
            s.dma("sp", vec[l][:], vecs.ap()[l], writes=["vecs"])
        fg = gsb("fg", [128, 8])
        s.dma("sp", fg[:], fing.ap(), writes=["vecs"])
        gix = gsb("gix", [128, 32], U32)
        s.dma("sp", gix[:], gidx_d.ap(), writes=["gix"])

        with contextlib.ExitStack() as st:
            sb, ps = mk(st)
            stg = Ring([sb("stg%d" % i, [128, 4096]) for i in range(3)], "stg")
            stb = Ring([sb("stb%d" % i, [128, 4096], BF16) for i in range(3)], "stb")
            cnt = [0]

            def cvt(src, dst, shape):
                t32, k32 = stg.next()
                tb, kb = stb.next()
                n = int(np.prod(shape[1:]))
                if len(shape) == 3:
                    v32 = t32[:, 0:n].rearrange("p (a b) -> p a b", a=shape[1])
                    vb = tb[:, 0:n].rearrange("p (a b) -> p a b", a=shape[1])
                else:
                    v32 = t32[:, 0:n]
                    vb = tb[:, 0:n]
                s.dma("sp", v32, src, writes=[k32])
                eng = ("dve", "pool", "act")[cnt[0] % 3]
                cnt[0] += 1
                if eng == "act":
                    s.I("act", "activation", [k32], [kb], out=tb[:, 0:n], in_=t32[:, 0:n], func=AF.Copy)
                else:
                    s.I(eng, "tensor_copy", [k32], [kb], out=tb[:, 0:n], in_=t32[:, 0:n])
                s.dma("pool", dst, vb, reads=[kb])

            def v3(t, l, r0, nr):
                return t.ap()[l, r0 * 128:(r0 + nr) * 128, :].rearrange("(a p) n -> p a n", p=128)

            for rc in range(8):
                cvt(wcat.ap()[0, rc * 128:(rc + 1) * 128, :], wcat_b.ap()[0, rc * 128:(rc + 1) * 128, :], [128, NCAT])
            s.flush()

        def bg_pieces():
            for l in range(nlayers):
                if l > 0:
                    for rc in range(8):
                        yield (wcat.ap()[l, rc * 128:(rc + 1) * 128, :], wcat_b.ap()[l, rc * 128:(rc + 1) * 128, :], [128, NCAT])
                for rc in range(0, 8, 4):
                    yield (v3(w_out, l, rc, 4), v3(w_out_b, l, rc, 4), [128, 4, 1024])
                yield (v3(glu_w, l, 0, 4), v3(glu_b, l, 0, 4), [128, 4, 512])
                for rc in range(8):
                    yield (w_up.ap()[l, rc * 128:(rc + 1) * 128, :], w_up_b.ap()[l, rc * 128:(rc + 1) * 128, :], [128, 4096])
                for rc in range(0, 32, 4):
                    yield (v3(w_down, l, rc, 4), v3(w_dn_b, l, rc, 4), [128, 4, 1024])

        def v3(t, l, r0, nr):
            return t.ap()[l, r0 * 128:(r0 + nr) * 128, :].rearrange("(a p) n -> p a n", p=128)

        def rmsnorm(rings, srcs, P, gcols, Dn, outs):
            pmr, sqr, tmpr, rstdr = rings
            pt, pk = pmr.next()
            n = len(srcs)
            for c, (a, k) in enumerate(srcs):
                sq, sk = sqr.next()
                s.I("act", "activation", [k], [sk], out=sq[0:P, :], in_=a, func=AF.Square)
                s.mm([sk, "onesb"], [pk], [(pt[0:P, :], onesb[0:P, 0:P], sq[0:P, :], c == 0, c == n - 1)])
            tm, tk = tmpr.next()
            s.I("act", "activation", [pk, "cst"], [tk], out=tm[0:P, :], in_=pt[0:P, :], func=AF.Ln,
                scale=1.0 / Dn, bias=epsc[0:P, :])
            rs, rk = rstdr.next()
            s.I("act", "activation", [tk], [rk], out=rs[0:P, :], in_=tm[0:P, :], func=AF.Exp, scale=-0.5)
            for c, ((a, k), (o, ok)) in enumerate(zip(srcs, outs)):
                s.I("dve", "scalar_tensor_tensor", [k, rk, "vecs"], [ok], out=o, in0=a, scalar=gcols[c],
                    in1=rs[0:P, :], op0=ALU.mult, op1=ALU.mult)

        evac_cnt = [0]
        evac_mode = ["alt"]

        def evac(out, in_, reads, writes):
            e = ("act", "dve")[evac_cnt[0] % 2]
            if evac_mode[0] == "dve":
                e = "dve"
            evac_cnt[0] += 1
            if e == "act":
                s.I("act", "activation", reads, writes, out=out, in_=in_, func=AF.Copy)
            else:
                s.I("dve", "tensor_copy", reads, writes, out=out, in_=in_)

        for l in range(nlayers):
            def hsrc_ap(c, t):
                if l == 0:
                    return xT.ap()[c * 128:(c + 1) * 128, t * 512:(t + 1) * 512]
                r0 = (c // 4) * 1024 + (t // 4) * 512 + (c % 4) * 128
                return H1.ap()[r0:r0 + 128, (t % 4) * 512:(t % 4 + 1) * 512]
            hown = xTo if l == 0 else h1own
            V = vec[l]
            lst = contextlib.ExitStack()
            SS = {}
            pmr_bg = [None]

            def make_setup():
                lsb, _lps = mk(lst)
                ssm_s = lsb("ssm_s", [128, 24])
                ssm_b = lsb("ssm_b", [128, 4, NPL, 16])
                sc = lsb("sc", [128, 64, NPL])
                big = lsb("big", [128, 12, NPL, 16])
                Ebd = lsb("Ebd", [128, 2, 8, NPL, 2, 16], BF16)
                Fbd = lsb("Fbd", [128, 2, 8, NPL, 2, 16], BF16)
                Cbd = lsb("Cbd", [128, 2, NPL, 2, 16], BF16)
                Klag = lsb("Klag", [128, NBL, 8, 128], BF16)
                A1t = lsb("A1t", [128, 2, NPL])
                A2t = lsb("A2t", [128, 2, NPL])
                ktmp = lsb("ktmp", [128, 128])
                SS.update(ssm_s=ssm_s, ssm_b=ssm_b, sc=sc, big=big, Ebd=Ebd, Fbd=Fbd, Cbd=Cbd, Klag=Klag, A1t=A1t, A2t=A2t, ktmp=ktmp)

                def ssm_setup():
                        s.dma("sp", ssm_s[:], ssm_small.ap()[l], writes=["setup"])
                        s.dma("sp", ssm_b[:].rearrange("p a b c -> p (a b c)"), ssm_big.ap()[l], writes=["setup"])
                        SK = ["setup"]

                        def tt(o, a, b, op):
                            s.I("dve", "tensor_tensor", SK, SK, out=o, in0=a, in1=b, op=op)

                        def ts(o, a, s1, op0, s2=None, op1=None):
                            if op1 is None:
                                s.I("dve", "tensor_scalar", SK + ["cst"], SK, out=o, in0=a, scalar1=s1, scalar2=None, op0=op0)
                            else:
                                s.I("dve", "tensor_scalar", SK + ["cst"], SK, out=o, in0=a, scalar1=s1, scalar2=s2, op0=op0, op1=op1)

                        def af(o, a, func, scale=1.0):
                            s.I("act", "activation", SK, SK, out=o, in_=a, func=func, scale=scale)

                        ldt, lre, lim = ssm_s[:, 0:8], ssm_s[:, 8:16], ssm_s[:, 16:24]
                        S_ = lambda i: sc[:, i, :]
                        dt_, x1, mag, th, s16, sn, cs, t1_, t2_, t3_ = [S_(i) for i in range(10)]
                        are, aim, den, rden, nr, sre, sim = [S_(i) for i in range(10, 17)]
                        af(dt_, ldt, AF.Exp)
                        tt(x1, lre, dt_, ALU.mult)
                        af(mag, x1, AF.Exp)
                        tt(th, lim, dt_, ALU.mult)
                        af(s16, th, AF.Sin, 1.0 / 16)
                        af(sn, th, AF.Sin, 1.0 / 8)
                        yield
                        tt(t1_, s16, s16, ALU.mult)
                        ts(cs, t1_, -2.0, ALU.mult, 1.0, ALU.add)
                        for _ in range(3):
                            tt(t1_, cs, cs, ALU.mult)
                            tt(t2_, sn, sn, ALU.mult)
                            tt(t3_, cs, sn, ALU.mult)
                            tt(cs, t1_, t2_, ALU.subtract)
                            ts(sn, t3_, 2.0, ALU.mult)
                        tt(are, mag, cs, ALU.mult)
                        tt(aim, mag, sn, ALU.mult)
                        tt(t1_, lre, lre, ALU.mult)
                        tt(t2_, lim, lim, ALU.mult)
                        yield
                        tt(den, t1_, t2_, ALU.add)
                        s.I("dve", "reciprocal", SK, SK, out=rden, in_=den)
                        ts(nr, are, -1.0, ALU.add)
                        tt(t1_, nr, lre, ALU.mult)
                        tt(t2_, aim, lim, ALU.mult)
                        tt(t1_, t1_, t2_, ALU.add)
                        yield
                        tt(sre, t1_, rden, ALU.mult)
                        tt(t1_, aim, lre, ALU.mult)
                        tt(t2_, nr, lim, ALU.mult)
                        tt(t1_, t1_, t2_, ALU.subtract)
                        tt(sim, t1_, rden, ALU.mult)
                        bc = lambda a: a.unsqueeze(2).to_broadcast([128, NPL, 16])
                        Bre, Bim, CreT, CimT = [ssm_b[:, i, :, :] for i in range(4)]
                        Bbre, Bbim, m1, m2, Ere, Eim = [big[:, i, :, :] for i in range(6)]

                        def cmul(ore, oim, are_, aim_, pre, pim):
                            tt(m1, are_, bc(pre), ALU.mult)
                            tt(m2, aim_, bc(pim), ALU.mult)
                            tt(ore, m1, m2, ALU.subtract)
                            tt(m1, aim_, bc(pre), ALU.mult)
                            tt(m2, are_, bc(pim), ALU.mult)
                            tt(oim, m1, m2, ALU.add)

                        cmul(Bbre, Bbim, Bre, Bim, sre, sim)
                        yield
                        Pre = [S_(20 + 2 * j) for j in range(9)]
                        Pim = [S_(21 + 2 * j) for j in range(9)]
                        s.I("dve", "memset", SK, SK, Pre[0], 1.0)
                        s.I("dve", "memset", SK, SK, Pim[0], 0.0)
                        for j in range(1, 9):
                            tt(t1_, Pre[j - 1], are, ALU.mult)
                            tt(t2_, Pim[j - 1], aim, ALU.mult)
                            tt(Pre[j], t1_, t2_, ALU.subtract)
                            tt(t1_, Pre[j - 1], aim, ALU.mult)
                            tt(t2_, Pim[j - 1], are, ALU.mult)
                            tt(Pim[j], t1_, t2_, ALU.add)
                        s.I("dve", "tensor_copy", SK, SK, out=A1t[:, 0, :], in_=Pre[8])
                        s.I("dve", "tensor_copy", SK, SK, out=A1t[:, 1, :], in_=Pre[8])
                        ts(A2t[:, 0, :], Pim[8], -1.0, ALU.mult)
                        s.I("dve", "tensor_copy", SK, SK, out=A2t[:, 1, :], in_=Pim[8])
                        yield
                        for j in range(8):
                            cmul(Ere, Eim, Bbre, Bbim, Pre[7 - j], Pim[7 - j])
                            yield
                            for g2 in range(2):
                                ts(Ebd[:, 0, j, :, g2, :], Ere, mkc[g2], ALU.mult)
                                ts(Ebd[:, 1, j, :, g2, :], Eim, mkc[g2], ALU.mult)
                        for j in range(8):
                            cmul(Ere, Eim, CreT, CimT, Pre[j + 1], Pim[j + 1])
                            yield
                            for g2 in range(2):
                                ts(Fbd[:, 0, j, :, g2, :], Ere, mkc[g2], ALU.mult)
                                ts(Fbd[:, 1, j, :, g2, :], Eim, mkc[g2], ALU.mult, -1.0, ALU.mult)
                        for g2 in range(2):
                            ts(Cbd[:, 0, :, g2, :], CreT, mkc[g2], ALU.mult)
                            ts(Cbd[:, 1, :, g2, :], CimT, mkc[g2], ALU.mult, -1.0, ALU.mult)
                        for b in range(NBL):
                            for d in range(8):
                                pt, pk = pmr_bg[0].next()
                                er = Ebd[:, 0, 7 - d, 4 * b:4 * b + 4, :, :].rearrange("p a g q -> p (a g q)")
                                ei = Ebd[:, 1, 7 - d, 4 * b:4 * b + 4, :, :].rearrange("p a g q -> p (a g q)")
                                cr = Cbd[:, 0, 4 * b:4 * b + 4, :, :].rearrange("p a g q -> p (a g q)")
                                ci = Cbd[:, 1, 4 * b:4 * b + 4, :, :].rearrange("p a g q -> p (a g q)")
                                s.mm(SK, [pk], [(pt[:, 0:128], er, cr, True, False), (pt[:, 0:128], ei, ci, False, True)])
                                if d == 0:
                                    s.I("dve", "tensor_tensor", [pk, "cst"], ["ktmp"], out=ktmp[:], in0=pt[:, 0:128], in1=pmf, op=ALU.mult)
                                    s.I("dve", "scalar_tensor_tensor", ["ktmp", "cst", "vecs"], ["Klag"], out=Klag[:, b, d, :], in0=identf,
                                        scalar=V[:, VDS + b:VDS + b + 1], in1=ktmp[:], op0=ALU.mult, op1=ALU.add)
                                else:
                                    s.I("dve", "tensor_tensor", [pk, "cst"], ["Klag"], out=Klag[:, b, d, :], in0=pt[:, 0:128], in1=pmf, op=ALU.mult)
                            yield

                        SS.update(Pre=Pre, Pim=Pim, S_=S_)
                        yield
                return ssm_setup()

            sgen = make_setup() if l > 0 else None
            evac_mode[0] = "dve"
            with contextlib.ExitStack() as st:
                sb, ps = mk(st)
                KT = [sb("KT%d" % h, [128, L], BF16) for h in range(NHL)]
                Vr = sb("Vr", [128, 32, 384], BF16)
                s.I("pool", "memset", [], ["Vr"], Vr[:].rearrange("p k (a c) -> p k a c", c=192)[:, :, :, 64:128], 1.0)
                dsb = [sb("dsb%d" % i, [128, 512]) for i in range(2)]
                for i in range(2):
                    s.I("pool", "memset", [], [("dsb", i)], dsb[i][:], 0.0)
                wr = Ring([sb("wr%d" % i, [128, 8, 128], BF16) for i in range(3)], "wr")
                wv = sb("wv", [128, 8, 256], BF16)
                s.dma("sp", wv[:], wcat_b.ap()[l, :, OV:OV + 256].rearrange("(k p) n -> p k n", p=128), writes=["wv"])
                hr = Ring([sb("hr%d" % i, [128, 512], F32 if l == 0 else BF16) for i in range(8)], "hr")
                sqr = Ring([sb("sq%d" % i, [128, 512], BF16) for i in range(3)], "sq")
                tmpr = Ring([sb("tm%d" % i, [128, 512]) for i in range(1)], "tm")
                rstdr = Ring([sb("rs%d" % i, [128, 512]) for i in range(1)], "rs")
                xn = sb("xn", [128, 8, 512], BF16)
                QT2 = [[sb("QT%d_%d" % (z, h), [128, 512], BF16) for h in range(NHL)] for z in range(2)]
                if l == 0:
                    bstg = Ring([sb("bstg%d" % i, [128, 4096]) for i in range(2)], "bstg")
                    bstb = Ring([sb("bstb%d" % i, [128, 4096], BF16) for i in range(2)], "bstb")
                ust = Ring([sb("ust%d" % i, [128, 8, 64], BF16) for i in range(2)], "ust")
                caug = sb("caug", [128, 512], BF16)
                s.I("dve", "memset", [], ["caug"], caug[:], 0.0)
                s.I("dve", "memset", ["caug"], ["caug"], caug[96:104, :], 1.0)
                onesr = sb("onesr", [128, 512], BF16)
                s.I("dve", "memset", [], ["onesr"], onesr[:], 1.0)
                fz = sb("fz", [128, 512])
                fX = [sb("fX%d" % i, [128, 512]) for i in range(2)]
                fr1 = fz
                fb1 = sb("fb1", [128, 512], BF16)
                fb2 = sb("fb2", [128, 512], BF16)
                ptr = Ring([sb("pt%d" % i, [128, 512], BF16) for i in range(3)], "pt")
                ydr = Ring([sb("yd%d" % i, [128, 512]) for i in range(1)], "yd")
                yor = Ring([sb("yo%d" % i, [128, 512], BF16) for i in range(2)], "yo")
                pmr = Ring([ps("pm%d" % i, [128, 512]) for i in range(3)], "pm")
                psr = Ring([ps("psS%d" % i, [128, 512]) for i in range(3)], "psS")
                pyr = Ring([ps("py%d" % i, [128, 512]) for i in range(2)], "py")
                pdr = pmr
                FP = 104

                def wload(c0, ncol):
                    wt, wk = wr.next()
                    s.dma("sp", wt[:, :, 0:ncol], wcat_b.ap()[l, :, c0:c0 + ncol].rearrange("(k p) n -> p k n", p=128), writes=[wk])
                    return wt, wk

                def prep(t):
                    t0 = t * 512
                    QT = QT2[t % 2]
                    hs = []
                    for c in range(8):
                        ht, hk = hr.next()
                        s.dma("sp", ht[:], hsrc_ap(c, t), writes=[hk])
                        hs.append((ht[:], hk))
                    rmsnorm((pmr, sqr, tmpr, rstdr), hs, 128, [V[:, VL1 + c:VL1 + c + 1] for c in range(8)], float(D),
                            [(xn[:, c, :], "xn") for c in range(8)])
                    yield
                    for m in range(NBL):
                        wt, wk = wload(OU + 128 * m, 128)
                        pt, pk = pmr.next()
                        s.mm([wk, "xn"], [pk], [(pt[:], wt[:, k, :], xn[:, k, :], k == 0, k == 7) for k in range(8)])
                        ut, uk = ust.next()
                        evac(ut[:].rearrange("p j c -> p c j"), pt[:].rearrange("p (c j) -> p c j", j=8), [pk], [uk])
                        s.dma("pool", UT.ap()[m, :, :, 64 * t:64 * (t + 1)], ut[:], reads=[uk])
                    yield
                    wt, wk = wload(OF, FP)
                    pt, pk = pmr.next()
                    s.mm([wk, "xn"], [pk], [(pt[0:FP, :], wt[:, k, 0:FP], xn[:, k, :], k == 0, k == 7) for k in range(8)])
                    s.I("dve", "tensor_scalar", [pk, "vecs"], ["fz"], out=fz[0:FP, :], in0=pt[0:FP, :], scalar1=V[0:FP, VFB:VFB + 1],
                        scalar2=-1.0, op0=ALU.add, op1=ALU.mult)
                    s.I("act", "activation", ["fz"], ["fz"], out=fz[0:FP, :], in_=fz[0:FP, :], func=AF.Exp)
                    s.I("act", "activation", ["fz", "cst"], ["fz"], out=fz[0:FP, :], in_=fz[0:FP, :], func=AF.Ln, bias=onec[0:FP, :])
                    Xc, Xp = fX[t % 2], fX[(t + 1) % 2]
                    init = 0.0 if t == 0 else Xp[0:FP, 511:512]
                    s.I("dve", "tensor_tensor_scan", ["fz", "onesr", ("fX", (t + 1) % 2)], [("fX", t % 2)], out=Xc[0:FP, :],
                        data0=onesr[0:FP, :], data1=fz[0:FP, :], initial=init, op0=ALU.mult, op1=ALU.add)
                    xk = ("fX", t % 2)
                    s.I("dve", "tensor_scalar", [xk], ["fb1"], out=fb1[0:FP, :], in0=Xc[0:FP, :], scalar1=8.0, scalar2=None, op0=ALU.mult)
                    s.I("dve", "scalar_tensor_tensor", [xk, "fb1"], ["fz"], out=fr1[0:FP, :], in0=Xc[0:FP, :], scalar=8.0, in1=fb1[0:FP, :],
                        op0=ALU.mult, op1=ALU.subtract)
                    s.I("dve", "tensor_copy", ["fb1", "caug"], ["caug"], out=caug[0:8, :], in_=fb1[0:8, :])
                    s.I("dve", "tensor_copy", ["fz"], ["fb2"], out=fb2[0:FP, :], in_=fr1[0:FP, :])
                    s.I("dve", "tensor_copy", ["fb2", "caug"], ["caug"], out=caug[32:40, :], in_=fb2[32:40, :])
                    s.I("dve", "tensor_tensor", ["fz", "fb2"], ["fz"], out=fr1[0:FP, :], in0=fr1[0:FP, :], in1=fb2[0:FP, :], op=ALU.subtract)
                    s.I("dve", "tensor_copy", ["fz", "caug"], ["caug"], out=caug[64:72, :], in_=fr1[64:72, :])
                    yield
                    for h in range(NHL):
                        wt, wk = wload(OK_ + 128 * h, 128)
                        pt, pk = pmr.next()
                        s.mm([wk, "xn", "caug", "selk"], [pk],
                             [(pt[:], wt[:, k, :], xn[:, k, :], k == 0, False) for k in range(8)] +
                             [(pt[:], selk[:, h, :], caug[:], False, True)])
                        evac(KT[h][:, t0:t0 + 512], pt[:], [pk], [("KT", h, t)])
                        wt, wk = wload(OQ + 128 * h, 128)
                        pt, pk = pmr.next()
                        s.mm([wk, "xn", "caug", "selq"], [pk],
                             [(pt[:], wt[:, k, :], xn[:, k, :], k == 0, False) for k in range(8)] +
                             [(pt[:], selq[:, h, :], caug[:], False, True)])
                        evac(QT[h][:], pt[:], [pk], [("QT", t % 2, h)])
                        yield
                    for i in range(4):
                        pt, pk = pmr.next()
                        s.mm(["wv", "xn"], [pk], [(pt[:, 0:256], xn[:, k, 128 * i:128 * (i + 1)], wv[:, k, :], k == 0, k == 7) for k in range(8)])
                        vv = Vr[:, 4 * t + i, :].rearrange("p (a c) -> p a c", c=192)
                        pv = pt[:, 0:256].rearrange("p (a e d) -> p a e d", e=2, d=64)
                        evac(vv[:, :, 0:64], pv[:, :, 0, :], [pk], [("Vr", t)])
                        evac(vv[:, :, 128:192], pv[:, :, 1, :], [pk], [("Vr", t)])
                    yield

                def att(t):
                    t0 = t * 512
                    QT = QT2[t % 2]
                    nkb = 4 * (t + 1)
                    blocks = [(h, kb) for h in range(NHL) for kb in range(nkb)]
                    LA = 2
                    Sinfo = {}
                    hstate = {}

                    def emit_S(i):
                        h, kb = blocks[i]
                        j = kb - 4 * t
                        q0 = 128 * j if j > 0 else 0
                        pt, pk = psr.next()
                        s.mm([("KT", h, kb // 4), ("QT", t % 2, h)], [pk], [(pt[:, q0:512], KT[h][:, 128 * kb:128 * (kb + 1)], QT[h][:, q0:512], True, True)])
                        Sinfo[i] = (pt, pk, q0, j)

                    def emit_rest(i):
                        h, kb = blocks[i]
                        pt, pk, q0, j = Sinfo.pop(i)
                        par = h % 2
                        vb0 = 192 * (h // 2) + 64 * par
                        if kb == 0:
                            hstate[h] = pyr.next()
                        py, pyk = hstate[h]
                        pb_, pbk = ptr.next()
                        s.I("act", "activation", [pk], [pbk], out=pb_[:, q0:512], in_=pt[:, q0:512], func=AF.Exp, scale=0.125)
                        if j >= 0:
                            s.I("dve", "tensor_tensor", [pbk, "trib"], [pbk], out=pb_[:, q0:q0 + 128], in0=pb_[:, q0:q0 + 128],
                                in1=trib[:], op=ALU.mult)
                        s.mm([pbk, ("Vr", kb // 4), "Vr"], [pyk], [(py[:, q0:512], Vr[:, kb, vb0:vb0 + 128], pb_[:, q0:512], kb == 0, kb == nkb - 1)])
                        if kb != nkb - 1:
                            return
                        ysl = slice(0, 64) if par == 0 else slice(64, 128)
                        dsl = slice(64, 128) if par == 0 else slice(0, 64)
                        s.I("dve", "tensor_copy", [pyk], [("dsb", par)], out=dsb[par][dsl, :], in_=py[dsl, :])
                        pd, pdk = pdr.next()
                        s.mm([("dsb", par), "cst"], [pdk], [(pd[:], selEO[par], dsb[par][:], True, True)])
                        yd, ydk = ydr.next()
                        s.I("dve", "reciprocal", [pdk], [ydk], out=yd[ysl, :], in_=pd[ysl, :])
                        if par == 0:
                            hstate["yo"] = yor.next()
                        yo, yok = hstate["yo"]
                        s.I("dve", "tensor_tensor", [pyk, ydk], [yok], out=yo[ysl, :], in0=py[ysl, :], in1=yd[ysl, :], op=ALU.mult)
                        if par == 1:
                            s.dma("pool", SND.ap()[t // 4, t % 4, 256 + 64 * (h - 1):256 + 64 * (h + 1), :], yo[:], reads=[yok])

                    for i in range(len(blocks) + LA):
                        if i < len(blocks):
                            emit_S(i)
                        if i - LA >= 0:
                            emit_rest(i - LA)
                        if i % 8 == 7:
                            yield

                bgc = [0]

                def bg_gen():
                    for (src_, dst_, shape) in bg_pieces():
                        t32, k32 = bstg.next()
                        tb, kb_ = bstb.next()
                        n = int(np.prod(shape[1:]))
                        if len(shape) == 3:
                            v32 = t32[:, 0:n].rearrange("p (a b) -> p a b", a=shape[1])
                            vb = tb[:, 0:n].rearrange("p (a b) -> p a b", a=shape[1])
                        else:
                            v32 = t32[:, 0:n]
                            vb = tb[:, 0:n]
                        s.dma("sp", v32, src_, writes=[k32])
                        bgc[0] += 1
                        if bgc[0] % 2 == 0:
                            s.I("dve", "tensor_copy", [k32], [kb_], out=tb[:, 0:n], in_=t32[:, 0:n])
                        else:
                            s.I("act", "activation", [k32], [kb_], out=tb[:, 0:n], in_=t32[:, 0:n], func=AF.Copy)
                        s.dma("pool", dst_, vb, reads=[kb_])
                        yield

                def alternate(*gs):
                    gens = [g for g in gs if g is not None]
                    while gens:
                        for g in list(gens):
                            try:
                                next(g)
                            except StopIteration:
                                gens.remove(g)

                import itertools
                bg = bg_gen() if (l == 0 and not os.environ.get("KNOBG")) else None
                pmr_bg[0] = pmr
                alternate(prep(0))
                for t in range(NG):
                    nb = 7 if t < NG - 1 else 1000
                    if os.environ.get("KSEQ"):
                        alternate(att(t))
                        alternate(prep(t + 1) if t + 1 < NG else None)
                    else:
                        alternate(att(t), prep(t + 1) if t + 1 < NG else None, itertools.islice(bg, nb) if bg is not None else None,
                                  itertools.islice(sgen, 14 if t < NG - 1 else 100000) if sgen is not None else None)
                s.flush()
            evac_mode[0] = "alt"
            if os.environ.get("KSTOP") == "1":
                return nc

            with contextlib.ExitStack() as st:
                sb, ps = mk(st)
                pmr = Ring([ps("pm%d" % i, [128, 512]) for i in range(4)], "pm")
                ptb = Ring([ps("ptb%d" % i, [128, 4, 128], BF16) for i in range(2)], "ptb")
                if l == 0:
                    pmr_bg[0] = pmr
                    for _ in make_setup():
                        pass
                Pre, Pim, S_ = SS["Pre"], SS["Pim"], SS["S_"]
                Ebd, Fbd, Klag, A1t, A2t = SS["Ebd"], SS["Fbd"], SS["Klag"], SS["A1t"], SS["A2t"]
                SK = ["setup"]
                DS = sb("DS", [128, 2, NPL, 513])
                s.I("dve", "memset", [], ["DS"], DS[:, :, :, 0:1], 0.0)
                ubr = Ring([sb("ub%d" % i, [128, 8, 512], BF16) for i in range(2)], "ub")
                with contextlib.ExitStack() as st2:
                    sb2, _ = mk(st2)
                    Epad = sb2("Epad", [128, 4, 2, 8, 128], BF16)
                    Wbp = sb2("Wbp", [128, 4, 2, 8, 128], BF16)
                    s.I("pool", "memset", [], ["Epad"], Epad[:], 0.0)
                    for b in range(NBL):
                        ub, ubk = ubr.next()
                        s.dma("sp", ub[:], UT.ap()[b], writes=[ubk])
                        for pb in range(4):
                            s.I("pool", "tensor_copy", SK + ["Epad"], ["Epad"], out=Epad[:, pb, :, :, 32 * pb:32 * pb + 32],
                                in_=Ebd[:, :, :, 4 * b + pb, :, :].rearrange("p r j g q -> p r j (g q)"))
                        for pb in range(4):
                            for ri in range(2):
                                for jj in range(0, 8, 4):
                                    tp, tpk = ptb.next()

                                    def fn(e, tp=tp, pb=pb, ri=ri, jj=jj):
                                        ins = None
                                        for x in range(4):
                                            ins = e.transpose(tp[:, x, :], Epad[:, pb, ri, jj + x, :], identb[:])
                                        return ins
                                    s.op("pe", fn, ["Epad", "identb"], [tpk])
                                    evac(Wbp[:, pb, ri, jj:jj + 4, :], tp[:], [tpk], ["Wbp"])
                        for pb in range(4):
                            for ri in range(2):
                                pt, pk = pmr.next()
                                s.mm(["Wbp", ubk], [pk], [(pt[:], Wbp[:, pb, ri, j, :], ub[:, j, :], j == 0, j == 7) for j in range(8)])
                                evac(DS[:, ri, 4 * b + pb, 1:513], pt[:], [pk], ["DS"])
                s.barrier()
                with contextlib.ExitStack() as st2:
                    sb2, _ = mk(st2)
                    ct1 = sb2("ct1", [128, 2, NPL, 32])
                    ct2 = sb2("ct2", [128, 2, NPL, 32])
                    tabr = sb2("tabr", [128, NPL, 32])
                    tabi = sb2("tabi", [128, NPL, 32])
                    C1 = sb2("C1", [128, 2, NPL, 32])
                    C2 = sb2("C2", [128, 2, NPL, 32])
                    B1 = sb2("B1", [128, 2, NPL])
                    B2 = sb2("B2", [128, 2, NPL])
                    DSv = DS[:, :, :, 1:513].rearrange("p r a (s i) -> p r a s i", i=32)
                    CK = ["DS", "setup", "ct1"]

                    def dv(meth, **kw):
                        s.I("dve", meth, CK, CK, **kw)

                    def cstep(cur, prev, c1, c2, n):
                        s.I("dve", "tensor_tensor", ["DS", "setup", "ct1"], ["ct1"], out=ct1[:, :, :, 0:n], in0=c1, in1=prev, op=ALU.mult)
                        s.I("dve", "tensor_tensor", ["DS", "setup", "ct2"], ["ct2"], out=ct2[:, 0, :, 0:n], in0=c2[:, 0], in1=prev[:, 1], op=ALU.mult)
                        s.I("dve", "tensor_tensor", ["DS", "setup", "ct2"], ["ct2"], out=ct2[:, 1, :, 0:n], in0=c2[:, 1], in1=prev[:, 0], op=ALU.mult)
                        s.I("dve", "tensor_tensor", ["DS", "ct1"], ["DS"], out=cur, in0=cur, in1=ct1[:, :, :, 0:n], op=ALU.add)
                        s.I("dve", "tensor_tensor", ["DS", "ct2"], ["DS"], out=cur, in0=cur, in1=ct2[:, :, :, 0:n], op=ALU.add)

                    A1b = A1t[:].unsqueeze(3).to_broadcast([128, 2, NPL, 16])
                    A2b = A2t[:].unsqueeze(3).to_broadcast([128, 2, NPL, 16])
                    for i in range(1, 32):
                        cstep(DSv[:, :, :, :, i], DSv[:, :, :, :, i - 1], A1b, A2b, 16)
                    pwr, pwi, q1, q2, q3 = [S_(40 + i) for i in range(5)]
                    dv("tensor_copy", out=pwr, in_=Pre[8])
                    dv("tensor_copy", out=pwi, in_=Pim[8])
                    dv("tensor_copy", out=tabr[:, :, 0], in_=Pre[8])
                    dv("tensor_copy", out=tabi[:, :, 0], in_=Pim[8])
                    w = 1
                    m1t = ct1[:, 0, :, :]
                    m2t = ct1[:, 1, :, :]
                    while w < 32:
                        pb_r = pwr.unsqueeze(2).to_broadcast([128, NPL, w])
                        pb_i = pwi.unsqueeze(2).to_broadcast([128, NPL, w])
                        dv("tensor_tensor", out=m1t[:, :, 0:w], in0=tabr[:, :, 0:w], in1=pb_r, op=ALU.mult)
                        dv("tensor_tensor", out=m2t[:, :, 0:w], in0=tabi[:, :, 0:w], in1=pb_i, op=ALU.mult)
                        dv("tensor_tensor", out=tabr[:, :, w:2 * w], in0=m1t[:, :, 0:w], in1=m2t[:, :, 0:w], op=ALU.subtract)
                        dv("tensor_tensor", out=m1t[:, :, 0:w], in0=tabr[:, :, 0:w], in1=pb_i, op=ALU.mult)
                        dv("tensor_tensor", out=m2t[:, :, 0:w], in0=tabi[:, :, 0:w], in1=pb_r, op=ALU.mult)
                        dv("tensor_tensor", out=tabi[:, :, w:2 * w], in0=m1t[:, :, 0:w], in1=m2t[:, :, 0:w], op=ALU.add)
                        dv("tensor_tensor", out=q1, in0=pwr, in1=pwr, op=ALU.mult)
                        dv("tensor_tensor", out=q2, in0=pwi, in1=pwi, op=ALU.mult)
                        dv("tensor_tensor", out=q3, in0=pwr, in1=pwi, op=ALU.mult)
                        dv("tensor_tensor", out=pwr, in0=q1, in1=q2, op=ALU.subtract)
                        dv("tensor_scalar", out=pwi, in0=q3, scalar1=2.0, scalar2=None, op0=ALU.mult)
                        w *= 2
                    dv("tensor_copy", out=B1[:, 0, :], in_=pwr)
                    dv("tensor_copy", out=B1[:, 1, :], in_=pwr)
                    dv("tensor_scalar", out=B2[:, 0, :], in0=pwi, scalar1=-1.0, scalar2=None, op0=ALU.mult)
                    dv("tensor_copy", out=B2[:, 1, :], in_=pwi)
                    dv("tensor_copy", out=C1[:, 0], in_=tabr[:])
                    dv("tensor_copy", out=C1[:, 1], in_=tabr[:])
                    dv("tensor_scalar", out=C2[:, 0], in0=tabi[:], scalar1=-1.0, scalar2=None, op0=ALU.mult)
                    dv("tensor_copy", out=C2[:, 1], in_=tabi[:])
                    B1b = B1[:].unsqueeze(3).to_broadcast([128, 2, NPL, 1])
                    B2b = B2[:].unsqueeze(3).to_broadcast([128, 2, NPL, 1])
                    for sg in range(1, 16):
                        cstep(DSv[:, :, :, sg, 31:32], DSv[:, :, :, sg - 1, 31:32], B1b, B2b, 1)
                    for sg in range(1, 16):
                        tp_ = DSv[:, :, :, sg - 1, 31:32].to_broadcast([128, 2, NPL, 31])
                        cstep(DSv[:, :, :, sg, 0:31], tp_, C1[:, :, :, 0:31], C2[:, :, :, 0:31], 31)
                s.barrier()
                with contextlib.ExitStack() as st2:
                    sb2, _ = mk(st2)
                    Fpad = sb2("Fpad", [128, 4, 2, 8, 128], BF16)
                    Sbf = sb2("Sbf", [128, 2, 4, 512], BF16)
                    gnat = Ring([sb2("gnat%d" % i, [128, 4096], BF16) for i in range(2)], "gnat")
                    s.I("dve", "memset", [], ["Fpad"], Fpad[:], 0.0)
                    for b in range(NBL):
                        ub, ubk = ubr.next()
                        s.dma("sp", ub[:], UT.ap()[b], writes=[ubk])
                        for pb in range(4):
                            s.I("dve", "tensor_copy", SK + ["Fpad"], ["Fpad"], out=Fpad[:, pb, :, :, 32 * pb:32 * pb + 32],
                                in_=Fbd[:, :, :, 4 * b + pb, :, :].rearrange("p r j g q -> p r j (g q)"))
                        for ri in range(2):
                            s.I("dve", "tensor_copy", ["DS", "Sbf"], ["Sbf"], out=Sbf[:, ri, :, :], in_=DS[:, ri, 4 * b:4 * b + 4, 0:512])
                        gn, gnk = gnat.next()
                        gv = gn[:].rearrange("p (c j) -> p j c", j=8)
                        for j in range(8):
                            pt, pk = pmr.next()
                            items = [(pt[:], Klag[:, b, d, :], ub[:, j - d, :], d == 0, False) for d in range(j + 1)]
                            items += [(pt[:], Fpad[:, pb, ri, j, :], Sbf[:, ri, pb, :], False, (pb == 3 and ri == 1))
                                      for pb in range(4) for ri in range(2)]
                            s.mm(["Klag", ubk, "Fpad", "Sbf"], [pk], items)
                            s.I("act", "activation", [pk, gnk], [gnk], out=gv[:, j, :], in_=pt[:], func=AF.Gelu_apprx_tanh)
                        for g8 in range(8):
                            s.dma("pool", SND.ap()[g8 // 4, g8 % 4, 128 * b:128 * (b + 1), :], gn[:, 512 * g8:512 * (g8 + 1)], reads=[gnk])
                if os.environ.get("KSTOP") == "2":
                    s.flush()
                    return nc
                for hf in range(2):
                    s.collective(SND.ap()[hf].rearrange("b r c -> (b r) c"), GAT.ap()[4096 * hf:4096 * (hf + 1), :])
                if os.environ.get("KSTOP") == "3":
                    return nc

            lst.close()
            with contextlib.ExitStack() as st:
                sb, ps = mk(st)
                pmr = Ring([ps("pm%d" % i, [128, 512]) for i in range(8)], "pm")
                gtl = sb("gtl", [128, 4, 512], BF16)
                gb = gtl
                sgr = Ring([sb("sg%d" % i, [128, 512]) for i in range(2)], "sg")
                osm = sb("osm", [128, 4, 512])
                sqr = Ring([sb("sq%d" % i, [128, 512], BF16) for i in range(3)], "sq")
                tmpr = Ring([sb("tm%d" % i, [128, 512]) for i in range(1)], "tm")
                rstdr = Ring([sb("rs%d" % i, [128, 512]) for i in range(2)], "rs")
                mixs = sb("mixs", [128, 4, 512], BF16)
                mixa = sb("mixa", [128, 4, 512], BF16)
                yar = Ring([sb("ya%d" % i, [128, 512], BF16) for i in range(4)], "ya")
                hbr = Ring([sb("hb%d" % i, [128, 512], BF16) for i in range(2)], "hb")
                hr = Ring([sb("hr%d" % i, [128, 512]) for i in range(4)], "hr")
                wglu = sb("wglu", [128, 4, 512], BF16)
                s.dma("sp", wglu[:], glu_b.ap()[l].rearrange("(k p) n -> p k n", p=128), writes=["wglu"])
                wos = Ring([sb("wos%d" % i, [128, 8, 128], BF16) for i in range(2)], "wos")
                h1T = sb("h1T", [128, 32, 512], BF16)
                wupr = Ring([sb("wup%d" % i, [128, 8, 512], BF16) for i in range(2)], "wup")
                wdnr = Ring([sb("wdn%d" % i, [128, 32, 128], BF16) for i in range(2)], "wdn")
                rlr = Ring([sb("rl%d" % i, [128, 512]) for i in range(2)], "rl")
                ofr = Ring([sb("of%d" % i, [128, 512]) for i in range(2)], "of")
                last = (l == nlayers - 1)
                hmid2 = [sb("hmidb%d" % i, [128, 8, 512]) for i in range(2)]
                xn22 = [sb("xn2b%d" % i, [128, 8, 512], BF16) for i in range(2)]

                def front(t):
                    t0 = t * 512
                    pz = t % 2
                    hmid, xn2 = hmid2[pz], xn22[pz]
                    for oc in range(4):
                        s.gather(gtl[:, oc, :], GAT.ap(), gix[:, 8 * t + oc:8 * t + oc + 1], reads=["gix"], writes=["gtl"])
                    yas = []
                    for h in range(4):
                        ya, yak = yar.next()
                        s.gather(ya[:], GAT.ap(), gix[:, 8 * t + 4 + h:8 * t + 4 + h + 1], reads=["gix"], writes=[yak])
                        yas.append((ya[:], yak))
                    yield
                    for oc in range(4):
                        pt, pk = pmr.next()
                        s.mm(["wglu", "gtl"], [pk], [(pt[:], wglu[:, k, 128 * oc:128 * (oc + 1)], gb[:, k, :], k == 0, k == 3) for k in range(4)])
                        sg, sgk = sgr.next()
                        s.I("act", "activation", [pk, "vecs"], [sgk], out=sg[:], in_=pt[:], func=AF.Sigmoid, bias=V[:, VGB + oc:VGB + oc + 1])
                        s.I("pool", "tensor_tensor", [sgk, "gtl"], [("osm", oc)], out=osm[:, oc, :], in0=gtl[:, oc, :], in1=sg[:], op=ALU.mult)
                        if oc % 2 == 1:
                            yield
                    rmsnorm((pmr, sqr, tmpr, rstdr), [(osm[:, oc, :], ("osm", oc)) for oc in range(4)], 128,
                            [V[:, VGS + oc:VGS + oc + 1] for oc in range(4)], 512.0, [(mixs[:, oc, :], "mixs") for oc in range(4)])
                    yield
                    rmsnorm((pmr, sqr, tmpr, rstdr), yas, 128, [V[:, VGA + h:VGA + h + 1] for h in range(4)], 512.0,
                            [(mixa[:, h, :], "mixa") for h in range(4)])
                    yield
                    for oc in range(8):
                        ws, wsk = wos.next()
                        s.dma("sp", ws[:], w_out_b.ap()[l, :, 128 * oc:128 * (oc + 1)].rearrange("(k p) n -> p k n", p=128), writes=[wsk])
                        ht, hk = hr.next()
                        s.dma("sp", ht[:], hown.ap()[128 * oc:128 * (oc + 1), t0:t0 + 512], writes=[hk])
                        pt, pk = pmr.next()
                        s.mm([wsk, "mixs", "mixa"], [pk],
                             [(pt[:], ws[:, k, :], mixs[:, k, :], k == 0, False) for k in range(4)] +
                             [(pt[:], ws[:, 4 + h, :], mixa[:, h, :], False, h == 3) for h in range(4)])
                        s.I("dve", "tensor_tensor", [pk, hk], [("hmid", pz, oc)], out=hmid[:, oc, :], in0=pt[:], in1=ht[:], op=ALU.add)
                        if oc % 2 == 1:
                            yield
                    rmsnorm((pmr, sqr, tmpr, rstdr), [(hmid[:, c, :], ("hmid", pz, c)) for c in range(8)], 128,
                            [V[:, VL2 + c:VL2 + c + 1] for c in range(8)], float(D), [(xn2[:, c, :], ("xn2", pz)) for c in range(8)])
                    yield

                def ffn(t):
                    t0 = t * 512
                    pz = t % 2
                    hmid, xn2 = hmid2[pz], xn22[pz]
                    for f4 in range(8):
                        wu, wuk = wupr.next()
                        s.dma("sp", wu[:], w_up_b.ap()[l, :, 512 * f4:512 * (f4 + 1)].rearrange("(k p) n -> p k n", p=128), writes=[wuk])
                        for fi in range(4):
                            f = 4 * f4 + fi
                            pt, pk = pmr.next()
                            s.mm([wuk, ("xn2", pz)], [pk], [(pt[:], wu[:, k, 128 * fi:128 * (fi + 1)], xn2[:, k, :], k == 0, k == 7) for k in range(8)])
                            rl, rlk = rlr.next()
                            s.I("act", "activation", [pk], [rlk], out=rl[:], in_=pt[:], func=AF.Relu)
                            s.I(("dve", "pool")[f % 2], "tensor_tensor", [rlk], [("h1T", f)], out=h1T[:, f, :], in0=rl[:], in1=rl[:], op=ALU.mult)
                        yield
                    for oc in range(8):
                        wd, wdk = wdnr.next()
                        s.dma("sp", wd[:], w_dn_b.ap()[l, :, 128 * oc:128 * (oc + 1)].rearrange("(f p) n -> p f n", p=128), writes=[wdk])
                        pt, pk = pmr.next()
                        s.mm([wdk] + [("h1T", f) for f in range(32)], [pk], [(pt[:], wd[:, f, :], h1T[:, f, :], f == 0, f == 31) for f in range(32)])
                        s.I("dve", "tensor_tensor", [pk, ("hmid", pz, oc)], [("hmid", pz, oc)], out=hmid[:, oc, :], in0=pt[:], in1=hmid[:, oc, :], op=ALU.add)
                        if not last:
                            s.dma("pool", h1own.ap()[128 * oc:128 * (oc + 1), t0:t0 + 512], hmid[:, oc, :], reads=[("hmid", pz, oc)])
                            hb, hbk = hbr.next()
                            s.I("pool", "tensor_copy", [("hmid", pz, oc)], [hbk], out=hb[:], in_=hmid[:, oc, :])
                            s.dma("pool", h1ownb.ap()[128 * oc:128 * (oc + 1), t0:t0 + 512], hb[:], reads=[hbk])
                        yield
                    if last:
                        pt, pk = pmr.next()
                        for c in range(8):
                            sq, sk = sqr.next()
                            s.I("act", "activation", [("hmid", pz, c)], [sk], out=sq[:], in_=hmid[:, c, :], func=AF.Square)
                            s.mm([sk, "onesb"], [pk], [(pt[:], onesb[:], sq[:], c == 0, c == 7)])
                        tm, tk = tmpr.next()
                        s.I("act", "activation", [pk, "cst"], [tk], out=tm[:], in_=pt[:], func=AF.Ln, scale=1.0 / D, bias=epsc)
                        rs, rk = rstdr.next()
                        s.I("act", "activation", [tk], [rk], out=rs[:], in_=tm[:], func=AF.Exp, scale=-0.5)
                        for c in range(8):
                            of_, ofk = ofr.next()
                            s.I("dve", "scalar_tensor_tensor", [("hmid", pz, c), rk, "vecs"], [ofk], out=of_[:], in0=hmid[:, c, :], scalar=fg[:, c:c + 1],
                                in1=rs[:], op0=ALU.mult, op1=ALU.mult)
                            s.dma("pool", outT.ap()[128 * c:128 * (c + 1), t0:t0 + 512], of_[:], reads=[ofk])
                        yield

                def alternate(g1, g2):
                    gens = [g for g in (g1, g2) if g is not None]
                    while gens:
                        for g in list(gens):
                            try:
                                next(g)
                            except StopIteration:
                                gens.remove(g)

                alternate(front(0), None)
                for t in range(4):
                    alternate(ffn(t), front(t + 1) if t + 1 < 4 else None)
                s.flush()
            if not last:
                for j in range(2):
                    s.collective(h1ownb.ap()[512 * j:512 * (j + 1), :], H1.ap()[1024 * j:1024 * (j + 1), :])
    return nc


def _consts():
    c = np.zeros((128, NCONST), np.float32)
    c[:, C_ID:C_ID + 128] = np.eye(128, dtype=np.float32)
    r = np.arange(128)
    c[:, C_PM:C_PM + 128] = (r[:, None] // 32 == r[None, :] // 32).astype(np.float32)
    c[:, C_TRI:C_TRI + 128] = (r[None, :] >= r[:, None]).astype(np.float32)
    selk = np.zeros((128, 8, 128), np.float32)
    selq = np.zeros((128, 8, 128), np.float32)
    c[64, C_SE:C_SE + 64] = 1.0
    c[0, C_SO + 64:C_SO + 128] = 1.0
    for h in range(8):
        selk[96, h, 64:67] = 1.0
        selk[h, h, 67] = 1.0
        selk[32 + h, h, 68] = 1.0
        selk[64 + h, h, 69] = 1.0
        selq[h, h, 64] = -1.0
        selq[32 + h, h, 65] = -1.0
        selq[64 + h, h, 66] = -1.0
        selq[96, h, 67:70] = 1.0
    c[:, C_SELK:C_SELK + 1024] = selk.reshape(128, 1024)
    c[:, C_SELQ:C_SELQ + 1024] = selq.reshape(128, 1024)
    c[:64, C_MISC + 0] = 1.0
    c[64:, C_MISC + 1] = 1.0
    c[:, C_MISC + 2] = EPS
    c[:, C_MISC + 3] = 1.0
    return c


def _prep(inp, r):
    f = lambda k: np.asarray(inp[k], dtype=np.float32)
    w_in = f("w_in")
    wcat = np.zeros((DEPTH, D, NCAT), np.float32)
    vecs = np.zeros((DEPTH, 128, NV), np.float32)
    ssm_small = np.zeros((DEPTH, 128, 24), np.float32)
    ssm_big = np.zeros((DEPTH, 128, 4, NPL, 16), np.float32)
    col = lambda v, n: np.ascontiguousarray(v.reshape(n, 128).T)
    for l in range(DEPTH):
        wcat[l, :, OU:OU + 256] = w_in[l, :, 256 * r:256 * (r + 1)]
        for hl in range(NHL):
            h = NHL * r + hl
            wcat[l, :, OQ + 128 * hl:OQ + 128 * hl + 64] = w_in[l, :, 512 + 64 * h:512 + 64 * (h + 1)]
            wcat[l, :, OK_ + 128 * hl:OK_ + 128 * hl + 64] = w_in[l, :, 1024 + 64 * h:1024 + 64 * (h + 1)]
        wcat[l, :, OV:OV + 256] = w_in[l, :, 1536 + 256 * r:1536 + 256 * (r + 1)]
        for base in (0, 32, 64, 96):
            wcat[l, :, OF + base:OF + base + NHL] = w_in[l, :, 2048 + NHL * r:2048 + NHL * (r + 1)]
            vecs[l, base:base + NHL, VFB] = f("fgate_b")[l][NHL * r:NHL * (r + 1)]
        vecs[l, :, VL1:VL1 + 8] = col(f("ln1_g")[l], 8)
        vecs[l, :, VL2:VL2 + 8] = col(f("ln2_g")[l], 8)
        vecs[l, :, VGS:VGS + 4] = col(f("gn_ssm_g")[l], 4)
        vecs[l, :, VGB:VGB + 4] = col(f("glu_b")[l], 4)
        vecs[l, :, VDS:VDS + 2] = col(f("ssm_d")[l][256 * r:256 * (r + 1)], 2)
        vecs[l, :, VGA:VGA + 4] = col(f("gn_attn_g")[l], 4)
        gs = slice(16 * r, 16 * (r + 1))
        ldt = f("ssm_log_dt")[l][gs].reshape(NPL, 2)
        ssm_small[l, :, 0:8] = np.repeat(ldt.T[:, None, :], 64, axis=1).reshape(128, NPL)
        for i, k in enumerate(("ssm_lambda_re", "ssm_lambda_im")):
            a = f(k)[l][gs].reshape(NPL, 2, 64)
            ssm_small[l, :, 8 * (i + 1):8 * (i + 2)] = a.transpose(1, 2, 0).reshape(128, NPL)
        for i, k in enumerate(("ssm_b_re", "ssm_b_im")):
            a = f(k)[l][gs].reshape(NPL, 2, 64, 16)
            ssm_big[l, :, i] = a.transpose(1, 2, 0, 3).reshape(128, NPL, 16)
        for i, k in enumerate(("ssm_c_re", "ssm_c_im")):
            a = f(k)[l][gs].reshape(NPL, 2, 16, 64)
            ssm_big[l, :, 2 + i] = a.transpose(1, 3, 0, 2).reshape(128, NPL, 16)
    gidx = np.zeros((128, 32), np.uint32)
    p = np.arange(128)
    for tl in range(4):
        for kind in range(2):
            for rk in range(2):
                for sub in range(2):
                    base = r * 4096 + rk * 2048 + tl * 512
                    gidx[:, 8 * tl + 4 * kind + 2 * rk + sub] = base + 256 * kind + 128 * sub + p
    return dict(wcat=wcat, vecs=vecs, ssm_small=ssm_small, ssm_big=ssm_big.reshape(DEPTH, 128, 512), gidx=gidx)


_NC_CACHE = {}


def kernel(**inputs):
    x = np.asarray(inputs["x"], dtype=np.float32)
    f = lambda k: np.asarray(inputs[k], dtype=np.float32)
    col = lambda v, n: np.ascontiguousarray(v.reshape(n, 128).T)
    shared = dict(w_out=f("w_out"), glu_w=f("glu_w"), w_up=f("w_up"), w_down=f("w_down"),
                  fing=col(f("final_g"), 8), consts=_consts())
    per_rank = [_prep(inputs, r) for r in range(2)]
    if "nc" not in _NC_CACHE:
        _NC_CACHE["nc"] = build()
    nc = _NC_CACHE["nc"]
    in_maps = []
    for i in range(8):
        b, r = i // 2, i % 2
        m = dict(shared)
        m.update(per_rank[r])
        xt = np.ascontiguousarray(x[b].T)
        m["xT"] = xt
        m["xTo"] = np.ascontiguousarray(xt[:, LH * r:LH * (r + 1)])
        in_maps.append(m)
    res = run_bass_kernel_spmd(nc, in_maps, core_ids=list(range(8)))
    out = np.zeros((4, L, D), np.float32)
    for i in range(8):
        b, r = i // 2, i % 2
        out[b, LH * r:LH * (r + 1), :] = res.results[i]["outT"].T
    return out
```
